# Optimizing a Trainium2 kernel written in Bass

```python
import math
import jax, jax.numpy as jnp
from jax import lax
import numpy as np


D_MODEL = 1024
BATCH = 8
SEQ = 4096
DEPTH = 1
DEC_BATCH = 4
DEC_SEQ = 8192
PAST_LEN = 128

MLA_HEADS = 8
QK_NOPE = 128
QK_ROPE = 64
QK_HEAD = QK_NOPE + QK_ROPE
V_HEAD = 128
Q_LORA = 384
KV_LORA = 256
MLA_WIDTH = MLA_HEADS * V_HEAD
ROPE_THETA = 10000.0
Q_BLOCK = 128

SSM_EXPAND = 2
SSM_INNER = SSM_EXPAND * D_MODEL
SSM_HEADDIM = 64
SSM_HEADS = SSM_INNER // SSM_HEADDIM
SSM_GROUPS = 4
SSM_STATE = 128
SSM_CONV_DIM = SSM_INNER + 2 * SSM_GROUPS * SSM_STATE
CONV_WIDTH = 5
CHUNK = 128

EPS = 1e-6

IN_SIZES = (Q_LORA, KV_LORA, QK_ROPE, MLA_WIDTH,
            SSM_INNER, SSM_CONV_DIM, SSM_HEADS, SSM_HEADS,
            D_MODEL, D_MODEL)
N_IN = Q_LORA + KV_LORA + QK_ROPE + MLA_WIDTH + SSM_INNER + SSM_CONV_DIM + 2 * SSM_HEADS + 2 * D_MODEL

kernel_name = 'hybrid_mla_ssd_gated_encoder'


def _split_cols(t, sizes):
    idx = []
    acc = 0
    for sz in sizes[:-1]:
        acc += sz
        idx.append(acc)
    return jnp.split(t, idx, axis=-1)


def _rms_norm(t, g):
    tf = t.astype(jnp.float32)
    tf = tf * lax.rsqrt(jnp.mean(tf * tf, axis=-1, keepdims=True) + EPS)
    return (tf * g.astype(jnp.float32)).astype(t.dtype)


def _rope_tables(s):
    half = QK_ROPE // 2
    inv_freq = jnp.exp(-math.log(ROPE_THETA) * jnp.arange(half, dtype=jnp.float32) / half)
    ang = jnp.arange(s, dtype=jnp.float32)[:, None] * inv_freq[None, :]
    return jnp.cos(ang), jnp.sin(ang)


def _apply_rope(t, cos, sin):
    half = QK_ROPE // 2
    tf = t.astype(jnp.float32)
    t1, t2 = tf[..., :half], tf[..., half:]
    c = cos[None, :, None, :]
    s_ = sin[None, :, None, :]
    return jnp.concatenate([t1 * c - t2 * s_, t2 * c + t1 * s_], axis=-1).astype(t.dtype)


def _mla_attention(q, k, v):
    b, s = q.shape[0], q.shape[1]
    nb = s // Q_BLOCK
    qb = q.reshape(b, nb, Q_BLOCK, MLA_HEADS, QK_HEAD).transpose(1, 0, 2, 3, 4)
    scale = QK_HEAD ** -0.5

    def block(qi):
        sc = jnp.einsum('bqhd,bkhd->bhqk', qi, k, preferred_element_type=jnp.float32) * scale
        p = jax.nn.softmax(sc, axis=-1)
        return jnp.einsum('bhqk,bkhd->bqhd', p.astype(v.dtype), v)

    o = lax.map(block, qb)
    return o.transpose(1, 0, 2, 3, 4).reshape(b, s, MLA_WIDTH)


def _dwconv_centred(t, w, bias):
    pad = CONV_WIDTH // 2
    ker = w.astype(t.dtype)[:, None, :]
    y = lax.conv_general_dilated(t, ker, window_strides=(1,), padding=[(pad, pad)],
                                 dimension_numbers=('NWC', 'WIO', 'NWC'),
                                 feature_group_count=t.shape[-1])
    return y + bias.astype(t.dtype)


def _ssd_chunked(x, dt, A, B, C, D):
    b, s, h, p = x.shape
    g, n = B.shape[2], B.shape[3]
    e = h // g
    nc = s // CHUNK
    xf = x.astype(jnp.float32)
    a = (dt * A[None, None, :]).reshape(b, nc, CHUNK, h)
    a_cum = jnp.cumsum(a, axis=2)
    xdt = (xf * dt[..., None]).reshape(b, nc, CHUNK, g, e, p)
    Bc = B.astype(jnp.float32).reshape(b, nc, CHUNK, g, n)
    Cc = C.astype(jnp.float32).reshape(b, nc, CHUNK, g, n)
    acl = a_cum.transpose(0, 1, 3, 2)
    diff = acl[..., :, None] - acl[..., None, :]
    tri = jnp.tril(jnp.ones((CHUNK, CHUNK), dtype=bool))
    Lm = jnp.exp(jnp.where(tri, diff, -jnp.inf)).reshape(b, nc, g, e, CHUNK, CHUNK)
    CB = jnp.einsum('bclgn,bcsgn->bcgls', Cc, Bc)
    y_diag = jnp.einsum('bcgls,bcgels,bcsgep->bclgep', CB, Lm, xdt)
    a_last = a_cum[:, :, -1:, :]
    decay_states = jnp.exp(a_last - a_cum).reshape(b, nc, CHUNK, g, e)
    states = jnp.einsum('bclgn,bclge,bclgep->bcgepn', Bc, decay_states, xdt)
    chunk_decay = jnp.exp(a_last[:, :, 0, :]).reshape(b, nc, g, e)

    def step(hstate, inp):
        dec, st = inp
        return dec[..., None, None] * hstate + st, hstate

    h0 = jnp.zeros((b, g, e, p, n), jnp.float32)
    _, prev = lax.scan(step, h0, (chunk_decay.transpose(1, 0, 2, 3),
                                  states.transpose(1, 0, 2, 3, 4, 5)))
    prev = prev.transpose(1, 0, 2, 3, 4, 5)
    y_off = jnp.einsum('bclgn,bcgepn,bclge->bclgep', Cc, prev,
                       jnp.exp(a_cum).reshape(b, nc, CHUNK, g, e))
    y = (y_diag + y_off).reshape(b, s, h, p) + xf * D.astype(jnp.float32)[None, None, :, None]
    return y


def _bidirectional_ssd(x, dt_f, dt_b, A_f, A_b, B, C, D_f, D_b):
    y_f = _ssd_chunked(x, dt_f, A_f, B, C, D_f)
    fl = lambda t: jnp.flip(t, axis=1)
    y_b = fl(_ssd_chunked(fl(x), fl(dt_b), A_b, fl(B), fl(C), D_b))
    return y_f + y_b


def _encoder_layer(x, c, norm_g, w_ada, b_ada, w_in, q_a_norm, w_q_up, kv_a_norm, w_kv_up,
                   q_norm, k_norm, w_proj_a, conv_w, conv_b, dt_bias_f, dt_bias_b,
                   a_log_f, a_log_b, d_f, d_b, ssm_norm, w_proj_b, w_out):
    b, s, _ = x.shape
    mod = jax.nn.silu(c) @ w_ada + b_ada
    shift, scale, gate = jnp.split(mod, 3, axis=-1)
    h = _rms_norm(x, norm_g) * (1.0 + scale[:, None, :]) + shift[:, None, :]

    proj = h @ w_in
    (q_c, kv_c, k_rope, gate_a, z, xbc, dt_f_raw, dt_b_raw, g_a, g_b) = _split_cols(proj, IN_SIZES)

    q = (_rms_norm(q_c, q_a_norm) @ w_q_up).reshape(b, s, MLA_HEADS, QK_HEAD)
    kv = (_rms_norm(kv_c, kv_a_norm) @ w_kv_up).reshape(b, s, MLA_HEADS, QK_NOPE + V_HEAD)
    k_nope, v = kv[..., :QK_NOPE], kv[..., QK_NOPE:]
    k = jnp.concatenate([k_nope, jnp.broadcast_to(k_rope[:, :, None, :], (b, s, MLA_HEADS, QK_ROPE))], axis=-1)
    q = _rms_norm(q, q_norm)
    k = _rms_norm(k, k_norm)
    cos, sin = _rope_tables(s)
    q = jnp.concatenate([q[..., :QK_NOPE], _apply_rope(q[..., QK_NOPE:], cos, sin)], axis=-1)
    k = jnp.concatenate([k[..., :QK_NOPE], _apply_rope(k[..., QK_NOPE:], cos, sin)], axis=-1)
    attn = _mla_attention(q, k, v)
    branch_a = (attn * jax.nn.silu(gate_a)) @ w_proj_a

    xbc = jax.nn.silu(_dwconv_centred(xbc, conv_w, conv_b))
    xs, Bm, Cm = _split_cols(xbc, (SSM_INNER, SSM_GROUPS * SSM_STATE, SSM_GROUPS * SSM_STATE))
    xs = xs.reshape(b, s, SSM_HEADS, SSM_HEADDIM)
    Bm = Bm.reshape(b, s, SSM_GROUPS, SSM_STATE)
    Cm = Cm.reshape(b, s, SSM_GROUPS, SSM_STATE)
    dt_f = jax.nn.softplus(dt_f_raw.astype(jnp.float32) + dt_bias_f.astype(jnp.float32))
    dt_b = jax.nn.softplus(dt_b_raw.astype(jnp.float32) + dt_bias_b.astype(jnp.float32))
    A_f = -jnp.exp(a_log_f.astype(jnp.float32))
    A_b = -jnp.exp(a_log_b.astype(jnp.float32))
    y = _bidirectional_ssd(xs, dt_f, dt_b, A_f, A_b, Bm, Cm, d_f, d_b)
    y = y.reshape(b, s, SSM_INNER).astype(x.dtype) * jax.nn.silu(z)
    y = _rms_norm(y.reshape(b, s, SSM_GROUPS, SSM_INNER // SSM_GROUPS),
                  ssm_norm.reshape(SSM_GROUPS, SSM_INNER // SSM_GROUPS)).reshape(b, s, SSM_INNER)
    branch_b = y @ w_proj_b

    merged = jax.nn.sigmoid(g_a) * branch_a + jax.nn.sigmoid(g_b) * branch_b
    out = merged @ w_out
    return (x + gate[:, None, :] * out).astype(x.dtype)


def setup_inputs(seed: int = 0) -> dict:
    key = jax.random.key(seed)
    ks = iter(jax.random.split(key, 48))

    def normal(shape, scale):
        return scale * jax.random.normal(next(ks), shape, jnp.float32)

    def gain(n):
        return 1.0 + 0.1 * jax.random.normal(next(ks), (DEPTH, n), jnp.float32)

    def dt_bias():
        u = jax.random.uniform(next(ks), (DEPTH, SSM_HEADS), jnp.float32)
        dt = jnp.exp(u * (math.log(0.1) - math.log(0.001)) + math.log(0.001))
        return dt + jnp.log(-jnp.expm1(-dt))

    def a_log():
        return jnp.log(jax.random.uniform(next(ks), (DEPTH, SSM_HEADS), jnp.float32, minval=1.0, maxval=16.0))

    x_prompt = normal((BATCH, SEQ, D_MODEL), 1.0)
    x_sample = normal((DEC_BATCH, DEC_SEQ, D_MODEL), 1.0)
    c_prompt = normal((BATCH, D_MODEL), 1.0)
    c_sample = normal((DEC_BATCH, D_MODEL), 1.0)
    return {
        'x_prompt': x_prompt,
        'x_sample': x_sample,
        'c_prompt': c_prompt,
        'c_sample': c_sample,
        'norm_g': gain(D_MODEL),
        'w_ada': normal((DEPTH, D_MODEL, 3 * D_MODEL), 0.5 * D_MODEL ** -0.5),
        'b_ada': normal((DEPTH, 3 * D_MODEL), 0.02),
        'w_in': normal((DEPTH, D_MODEL, N_IN), D_MODEL ** -0.5),
        'q_a_norm': gain(Q_LORA),
        'w_q_up': normal((DEPTH, Q_LORA, MLA_HEADS * QK_HEAD), Q_LORA ** -0.5),
        'kv_a_norm': gain(KV_LORA),
        'w_kv_up': normal((DEPTH, KV_LORA, MLA_HEADS * (QK_NOPE + V_HEAD)), KV_LORA ** -0.5),
        'q_norm': gain(QK_HEAD),
        'k_norm': gain(QK_HEAD),
        'w_proj_a': normal((DEPTH, MLA_WIDTH, D_MODEL), MLA_WIDTH ** -0.5),
        'conv_w': normal((DEPTH, CONV_WIDTH, SSM_CONV_DIM), CONV_WIDTH ** -0.5),
        'conv_b': normal((DEPTH, SSM_CONV_DIM), 0.02),
        'dt_bias_f': dt_bias(),
        'dt_bias_b': dt_bias(),
        'a_log_f': a_log(),
        'a_log_b': a_log(),
        'd_f': gain(SSM_HEADS),
        'd_b': gain(SSM_HEADS),
        'ssm_norm': gain(SSM_INNER),
        'w_proj_b': normal((DEPTH, SSM_INNER, D_MODEL), SSM_INNER ** -0.5),
        'w_out': normal((DEPTH, D_MODEL, D_MODEL), D_MODEL ** -0.5),
    }


def reference(x_prompt, x_sample, c_prompt, c_sample, norm_g, w_ada, b_ada, w_in, q_a_norm, w_q_up,
              kv_a_norm, w_kv_up, q_norm, k_norm, w_proj_a, conv_w, conv_b, dt_bias_f, dt_bias_b,
              a_log_f, a_log_b, d_f, d_b, ssm_norm, w_proj_b, w_out):
    layers = [(norm_g[l], w_ada[l], b_ada[l], w_in[l], q_a_norm[l], w_q_up[l], kv_a_norm[l], w_kv_up[l],
               q_norm[l], k_norm[l], w_proj_a[l], conv_w[l], conv_b[l], dt_bias_f[l], dt_bias_b[l],
               a_log_f[l], a_log_b[l], d_f[l], d_b[l], ssm_norm[l], w_proj_b[l], w_out[l])
              for l in range(DEPTH)]
    y_prompt = x_prompt
    y_sample = x_sample
    for l in range(DEPTH):
        y_prompt = _encoder_layer(y_prompt, c_prompt, *layers[l])
        y_sample = _encoder_layer(y_sample, c_sample, *layers[l])
    return (y_prompt, y_sample)
```

```python
import contextlib
import numpy as np
import concourse.bass as bass
import concourse.mybir as mybir
from concourse.bass_utils import run_bass_kernel_spmd

F32 = mybir.dt.float32
BF16 = mybir.dt.bfloat16
AF = mybir.ActivationFunctionType
ALU = mybir.AluOpType

D = 1024
T = 4096
NCH = 32
EPS = 1e-6
N_IN = 8960
C_Q, C_KV, C_KR, C_GA, C_Z, C_X, C_B, C_C, C_DTF, C_DTB, C_GGA, C_GGB = (
    0, 384, 640, 704, 1728, 3776, 5824, 6336, 6848, 6880, 6912, 7936)
SM_SCALE = 192 ** -0.5


class _Op:
    __slots__ = ("eng", "fn", "reads", "writes", "dma", "deps", "sig", "late", "early")

    def __init__(self, eng, fn, reads, writes, dma, late=()):
        self.eng, self.fn, self.reads, self.writes, self.dma = eng, fn, reads, writes, dma
        self.deps = set()
        self.sig = None
        self.late = frozenset(late)
        self.early = set()


class Prog:
    EPOCH = 24000
    NDMA = 16

    def __init__(self, nc, es):
        self.nc, self.es = nc, es
        self.ops = []
        self.h = {"pe": nc.tensor, "act": nc.scalar, "dve": nc.vector, "pool": nc.gpsimd, "sp": nc.sync}
        self.barriers = []

    def op(self, eng, fn, reads=(), writes=(), dma=False, late=()):
        self.ops.append(_Op(eng, fn, tuple(reads), tuple(writes), dma, late))

    def barrier(self):
        self.barriers.append(len(self.ops))

    def emit(self):
        nc, ops = self.nc, self.ops
        lastw, readers = {}, {}
        bset = set(self.barriers)
        last_eng = {}
        dma_since = []
        pend_bar = {}
        dma_hist = {}
        for i, o in enumerate(ops):
            if i in bset:
                deps = set(last_eng.values()) | set(dma_since)
                for e in self.h:
                    pend_bar[e] = pend_bar.get(e, set()) | deps
                dma_since = []
                lastw, readers = {}, {}
            if o.eng in pend_bar:
                o.deps |= pend_bar.pop(o.eng)
                o.early |= o.deps
            raw = set()
            for r in o.reads:
                lw = lastw.get(r)
                if lw is not None:
                    ds = lw if isinstance(lw, set) else {lw}
                    raw |= ds
                    if r not in o.late:
                        o.early |= ds
            o.deps |= raw
            for w in o.writes:
                ds = set(readers.get(w, ()))
                lw = lastw.get(w)
                if lw is not None and not isinstance(lw, set):
                    ds.add(lw)
                o.deps |= ds
                if w not in o.late:
                    o.early |= ds
            for r in o.reads:
                readers.setdefault(r, []).append(i)
            for w in o.writes:
                if w.startswith("@"):
                    lastw.setdefault(w, set()).add(i)
                else:
                    lastw[w] = i
                    readers[w] = []
            if o.dma:
                hist = dma_hist.setdefault(o.eng, [])
                if len(hist) >= self.NDMA:
                    o.deps.add(hist[-self.NDMA])
                    o.early.add(hist[-self.NDMA])
                hist.append(i)
                dma_since.append(i)
            else:
                last_eng[o.eng] = i
            o.deps.discard(i)
            o.deps = {d for d in o.deps if ops[d].dma or ops[d].eng != o.eng
                      or (d in raw and o.eng != "pe")}
        needed = set()
        for o in ops:
            needed |= o.deps
        ccount = {}
        csems = {}
        dcount = {}
        dsems = {}
        for i, o in enumerate(ops):
            if o.dma:
                n = dcount.get(o.eng, 0)
                dcount[o.eng] = n + 1
                pool = dsems.setdefault(o.eng, [])
                if len(pool) < self.NDMA:
                    pool.append(self.es.enter_context(nc.semaphore(f"d_{o.eng}_{len(pool)}")))
                o.sig = (pool[n % self.NDMA], 16 * (n // self.NDMA + 1))
            elif i in needed:
                n = ccount.get(o.eng, 0)
                ccount[o.eng] = n + 1
                lst = csems.setdefault(o.eng, [])
                if n // self.EPOCH >= len(lst):
                    lst.append(self.es.enter_context(nc.semaphore(f"c_{o.eng}_{len(lst)}")))
                o.sig = (lst[n // self.EPOCH], n % self.EPOCH + 1)
        waited = {e: {} for e in self.h}
        for i, o in enumerate(ops):
            h = self.h[o.eng]
            wl = {}
            early_k = set()
            for d in o.deps:
                sem, val = ops[d].sig
                k = id(sem)
                if d in o.early or not o.late:
                    early_k.add(k)
                if waited[o.eng].get(k, 0) < val and wl.get(k, (None, 0))[1] < val:
                    wl[k] = (sem, val)
            attach = None
            for k, (sem, val) in wl.items():
                if k not in early_k and attach is None:
                    attach = (sem, val)
                    continue
                h.wait_ge(sem, val)
                waited[o.eng][k] = val
            if o.fn is None:
                continue
            ins = o.fn()
            if attach is not None:
                ins._wait_ge(attach[0], attach[1])
                waited[o.eng][id(attach[0])] = attach[1]
            if o.sig is not None:
                ins.then_inc(o.sig[0], 16 if o.dma else 1)


def build_program(debug=False, stages=None):
    nc = bass.Bass("TRN2", target_bir_lowering=False)
    es = contextlib.ExitStack()
    P = Prog(nc, es)
    run = (lambda s: True) if stages is None else (lambda s: s in stages)

    def din(name, shape, dt=F32):
        return nc.dram_tensor(name, list(shape), dt, kind="ExternalInput").ap()

    def dscr(name, shape, dt):
        kind = "ExternalOutput" if (debug is True or (debug and name in debug)) else "Internal"
        return nc.dram_tensor(name, list(shape), dt, kind=kind).ap()

    x_in = din("x", [3, T, D])
    cT_in = din("cT", [2, 128, 8])
    w_ada = din("w_ada", [D, 3 * D])
    b_ada = din("b_ada", [1, 3 * D])
    norm_g = din("norm_g", [1, D])
    w_in = din("w_in", [D, N_IN])
    w_krs = din("w_krs", [D, 64])
    w_dt = din("w_dt", [2, D, 64])
    w_qn = din("w_qn", [8, 384, 128])
    w_qr = din("w_qr", [8, 384, 64])
    w_qs = din("w_qs", [8, 384, 64])
    w_kn = din("w_kn", [8, 256, 128])
    w_v = din("w_v", [8, 256, 128])
    gq_a = din("gq_a", [128, 3])
    gkv_a = din("gkv_a", [128, 2])
    qk_g = din("qk_g", [128, 6])
    conv_wc = din("conv_wc", [2, 128, 24 * 5])
    conv_bc = din("conv_bc", [128, 24])
    dtb = din("dtb", [2, 1, 64])
    alog = din("alog", [2, 1, 64])
    dskip = din("dskip", [2, 1, 64])
    ssm_g = din("ssm_g", [1, 2048])
    w_pa = din("w_pa", [D, D])
    w_pb = din("w_pb", [2048, D])
    w_o = din("w_o", [D, D])
    ropeC = din("ropeC", [2, 64, 8192])
    ropeS = din("ropeS", [2, 64, 8192])
    consts = din("consts", [128, 5 * 128])
    y_out = nc.dram_tensor("y", [2, T, D], F32, kind="ExternalOutput").ap()

    S = []
    for j in range(2):
        tk = T * (1 + j)
        S.append(dict(
            qnT=dscr(f"qnT{j}", [3, 128, T], BF16), kvnT=dscr(f"kvnT{j}", [2, 128, tk], BF16),
            KRg=dscr(f"KRg{j}", [64, tk], BF16),
            sga=dscr(f"sga{j}", [8, 128, T], BF16), sig=dscr(f"sig{j}", [16, 128, T], BF16),
            xs_tm=dscr(f"xstm{j}", [tk, 2048], BF16), B_tm=dscr(f"Btm{j}", [tk, 512], BF16),
            BT=dscr(f"BT{j}", [4, 128, T], BF16), CT=dscr(f"CT{j}", [4, 128, T], BF16),
            z_tm=dscr(f"ztm{j}", [T, 2048], BF16),
            dt=dscr(f"dt{j}", [tk, 64], F32), a=dscr(f"a{j}", [tk, 64], F32),
            AO=dscr(f"AO{j}", [8, 128, T], BF16), yb=dscr(f"yb{j}", [T, 2048], F32),
            ynT=dscr(f"ynT{j}", [16, 128, T], BF16),
        ))

    import itertools
    uid = itertools.count()

    def sb(name, shape, dt=F32):
        return es.enter_context(nc.sbuf_tensor(name, list(shape), dt))

    PS = [es.enter_context(nc.psum_tensor(f"ps{i}", [128, 512], F32)) for i in range(8)]
    rot = {"i": 0}

    def bank(lo=0, hi=8):
        n = hi - lo
        b = lo + rot.setdefault((lo, hi), 0) % n
        rot[(lo, hi)] += 1
        return b

    def psk(b):
        return f"ps{b}"

    def E(eng, meth, reads, writes, *a, **kw):
        h = P.h[eng]
        P.op(eng, lambda: getattr(h, meth)(*a, **kw), reads, writes)

    def DMA(eng, out, in_, reads, writes, slow=False):
        h = P.h[eng]
        if slow:
            P.op(eng, lambda: h.dma_start(out=out, in_=in_, allow_slow_non_contiguous=True), reads, writes, dma=True)
        else:
            P.op(eng, lambda: h.dma_start(out=out, in_=in_), reads, writes, dma=True)

    def MM(out, lhsT, rhs, start, stop, reads, writes, late=()):
        P.op("pe", lambda: nc.tensor.matmul(out, lhsT=lhsT, rhs=rhs, start=start, stop=stop), reads, writes, late=late)

    def TR(out, in_, ident, reads, writes):
        P.op("pe", lambda: nc.tensor.transpose(out, in_, ident), reads, writes)

    def ACT(out, in_, func, reads, writes, **kw):
        P.op("act", lambda: nc.scalar.activation(out=out, in_=in_, func=func, **kw), reads, writes)

    cp_flip = {"i": 0}

    def COPY(out, in_, reads, writes, eng=None):
        if eng is None:
            eng = ("act", "dve")[cp_flip["i"] % 2]
            cp_flip["i"] += 1
        if eng == "act":
            ACT(out, in_, AF.Copy, reads, writes)
        else:
            E(eng, "tensor_copy", reads, writes, out=out, in_=in_)

    def rstd_act(out, in_, scale, reads, writes, tmp, tmpk):
        ACT(tmp, in_, AF.Ln, reads, [tmpk], bias=eps_c[0:in_.shape[0], 0:1], scale=scale)
        ACT(out, tmp, AF.Exp, [tmpk], writes, scale=-0.5)

    cst32 = sb("cst32", [128, 5 * 128])
    cstbf = sb("cstbf", [128, 5 * 128], BF16)
    ones32 = sb("ones32", [128, 128])
    onesbf = sb("onesbf", [128, 128], BF16)
    eps_c = sb("eps_c", [128, 1])
    DMA("sp", cst32[:], consts[:, :], [], ["cst32"])
    E("dve", "tensor_copy", ["cst32"], ["cstbf"], out=cstbf[:], in_=cst32[:])
    E("dve", "memset", [], ["ones32"], ones32[:], 1.0)
    E("dve", "memset", [], ["onesbf"], onesbf[:], 1.0)
    E("dve", "memset", [], ["eps_c"], eps_c[:], EPS)
    IDENT, LE, GE, GT, LT = range(5)

    def c32(i):
        return cst32[:, i * 128:(i + 1) * 128]

    def cbf(i):
        return cstbf[:, i * 128:(i + 1) * 128]

    gmod, shiftb, gateb = {}, {}, {}

    def stage_A(j):
        with contextlib.ExitStack() as st:
            def sbt(name, shape, dt=F32):
                return st.enter_context(nc.sbuf_tensor(f"{name}_u{next(uid)}", list(shape), dt))
            ct = sbt("A_ct", [128, 8])
            sg = sbt("A_sg", [128, 8])
            csb = sbt("A_csb", [128, 8, 128])
            wbuf = [sbt(f"A_w{i}", [128, 8, 512]) for i in range(2)]
            bb = sbt("A_bb", [128, 3 * D])
            ngb = sbt("A_ngb", [128, D])
            DMA("sp", ct[:], cT_in[j], [], ["A_ct"])
            DMA("sp", bb[:], b_ada[0:1, :].partition_broadcast(128), [], ["A_bb"])
            DMA("sp", ngb[:], norm_g[0:1, :].partition_broadcast(128), [], ["A_ngb"])
            ACT(sg[:], ct[:], AF.Exp, ["A_ct"], ["A_sg"], scale=-1.0)
            E("dve", "tensor_scalar_add", ["A_sg"], ["A_sg"], out=sg[:], in0=sg[:], scalar1=1.0)
            E("dve", "reciprocal", ["A_sg"], ["A_sg"], out=sg[:], in_=sg[:])
            E("dve", "tensor_mul", ["A_sg", "A_ct"], ["A_sg"], out=sg[:], in0=sg[:], in1=ct[:])
            E("dve", "tensor_copy", ["A_sg"], ["A_csb"], out=csb[:],
              in_=sg[:].unsqueeze(2).to_broadcast([128, 8, 128]))
            wv = w_ada.rearrange("(kc p) n -> p kc n", p=128)
            for nb in range(6):
                wb = wbuf[nb % 2]
                wk = f"A_w{nb % 2}"
                DMA("sp", wb[:], wv[:, :, nb * 512:(nb + 1) * 512], [], [wk])
                b = bank()
                for k in range(8):
                    MM(PS[b][:], csb[:, k, :], wb[:, k, :], k == 0, k == 7, [wk, "A_csb"], [psk(b)])
                sec, off = nb // 2, (nb % 2) * 512
                dst = (shiftb[j], gmod[j], gateb[j])[sec]
                dk = (f"shiftb{j}", f"gmod{j}", f"gateb{j}")[sec]
                E("dve", "tensor_tensor", [psk(b), "A_bb"], [dk], out=dst[:, off:off + 512], in0=PS[b][:],
                  in1=bb[:, nb * 512:(nb + 1) * 512], op=ALU.add)
            E("dve", "scalar_tensor_tensor", [f"gmod{j}", "A_ngb"], [f"gmod{j}"], out=gmod[j][:], in0=gmod[j][:],
              scalar=1.0, in1=ngb[:], op0=ALU.add, op1=ALU.mult)
        P.barrier()

    def stage_B(j, hT, srcs):
        with contextlib.ExitStack() as st:
            def sbt(name, shape, dt=F32):
                return st.enter_context(nc.sbuf_tensor(f"{name}_u{next(uid)}", list(shape), dt))
            xt = [sbt(f"B_xt{i}", [128, D]) for i in range(2)]
            junk = sbt("B_junk", [128, D])
            htmp = sbt("B_htmp", [128, D])
            hb = [sbt(f"B_hb{i}", [128, D], BF16) for i in range(2)]
            ssq = [sbt(f"B_ssq{i}", [128, 1]) for i in range(2)]
            lnv = [sbt(f"B_ln{i}", [128, 1]) for i in range(2)]
            rs = [sbt(f"B_rs{i}", [128, 1]) for i in range(2)]
            for n, (xi, r0, nr, c0) in enumerate(srcs):
                s = n % 2
                DMA("sp", xt[s][0:nr, :], x_in[xi, r0:r0 + nr, :], [], [f"B_xt{s}"])
                ACT(junk[0:nr, :], xt[s][0:nr, :], AF.Square, [f"B_xt{s}"], ["B_junk", f"B_ssq{s}"],
                    accum_out=ssq[s][0:nr, :])
                ACT(lnv[s][0:nr, :], ssq[s][0:nr, :], AF.Ln, [f"B_ssq{s}"], [f"B_ln{s}"], bias=eps_c[0:nr, :],
                    scale=1.0 / D)
                ACT(rs[s][0:nr, :], lnv[s][0:nr, :], AF.Exp, [f"B_ln{s}"], [f"B_rs{s}"], scale=-0.5)
                E("dve", "scalar_tensor_tensor", [f"B_xt{s}", f"B_rs{s}", f"gmod{j}"], ["B_htmp"],
                  out=htmp[0:nr, :], in0=xt[s][0:nr, :], scalar=rs[s][0:nr, 0:1], in1=gmod[j][0:nr, :],
                  op0=ALU.mult, op1=ALU.mult)
                E("dve", "tensor_tensor", ["B_htmp", f"shiftb{j}"], [f"B_hb{s}"], out=hb[s][0:nr, :],
                  in0=htmp[0:nr, :], in1=shiftb[j][0:nr, :], op=ALU.add)
                b = bank()
                pv = PS[b][:].bitcast(BF16).rearrange("p (k t) -> p k t", k=8)
                for k in range(8):
                    TR(pv[:, k, 0:nr], hb[s][0:nr, k * 128:(k + 1) * 128], cbf(IDENT)[0:nr, 0:nr],
                       [f"B_hb{s}", "cstbf"], [psk(b)])
                COPY(hT[:, :, c0:c0 + nr], pv[:, :, 0:nr], [psk(b)], ["@hT"])

    def load_w_chunk(st_w32, st_wbf, key32, keybf, src_ap, ncols):
        DMA("sp", st_w32[:, :, 0:ncols], src_ap.rearrange("(kc p) n -> p kc n", p=128), [], [key32])
        E("pool", "tensor_copy", [key32], [keybf], out=st_wbf[:, :, 0:ncols], in_=st_w32[:, :, 0:ncols])

    def stage_C_fm(j, hT, other):
        sc = S[j]
        tok0 = T if other else 0
        with contextlib.ExitStack() as st:
            def sbt(name, shape, dt=F32):
                return st.enter_context(nc.sbuf_tensor(f"{name}_u{next(uid)}", list(shape), dt))
            w32 = [sbt(f"C_w32{i}", [128, 8, 128]) for i in range(2)]
            wbf = [sbt(f"C_wbf{i}", [128, 8, 128], BF16) for i in range(2)]
            gst = [sbt(f"C_gst{i}", [128, 512], BF16) for i in range(3)]
            pbufs = [sbt(f"C_pbuf{i}", [128, T + 4]) for i in range(2)]
            caccs = [sbt(f"C_cacc{i}", [128, T]) for i in range(2)]
            xc = [sbt(f"C_xc{i}", [128, T], BF16) for i in range(2)]
            tmst = [sbt(f"C_tm{i}", [128, 8, 128], BF16) for i in range(2)]
            cw = sbt("C_cw", [128, 24 * 5])
            cb = sbt("C_cb", [128, 24])
            DMA("sp", cw[:], conv_wc[j], [], ["C_cw"])
            DMA("sp", cb[:], conv_bc[:, :], [], ["C_cb"])
            chunks = []
            if not other:
                chunks += [("sig", i, C_GGA + 128 * i) for i in range(16)]
                chunks += [("sga", i, C_GA + 128 * i) for i in range(8)]
            nx = 20 if other else 24
            chunks += [("xbc", i, C_X + 128 * i) for i in range(nx)]
            load_w_chunk(w32[0], wbf[0], "C_w320", "C_wbf0", w_in[:, chunks[0][2]:chunks[0][2] + 128], 128)
            cnts = {"g": 0, "t": 0}

            def X(n):
                kind, ci, col = chunks[n]
                s = n % 2
                pbuf = pbufs[s]
                pk = f"@C_pbuf{s}"
                cacc = caccs[s]
                ck = f"C_cacc{s}"
                if n + 1 < len(chunks):
                    c2 = chunks[n + 1][2]
                    load_w_chunk(w32[1 - s], wbf[1 - s], f"C_w32{1 - s}", f"C_wbf{1 - s}", w_in[:, c2:c2 + 128], 128)
                for i in range(8):
                    b = bank()
                    for k in range(8):
                        MM(PS[b][:], wbf[s][:, k, :], hT[:, k, 2 + 512 * i:2 + 512 * (i + 1)], k == 0, k == 7,
                           [f"C_wbf{s}", "@hT"], [psk(b)])
                    if kind in ("sig", "sga"):
                        g = cnts["g"] % 3
                        cnts["g"] += 1
                        ACT(gst[g][:], PS[b][:], AF.Sigmoid if kind == "sig" else AF.Silu, [psk(b)], [f"C_gst{g}"])
                        dst = sc[kind][ci, :, 512 * i:512 * (i + 1)]
                        DMA("pool", dst, gst[g][:], [f"C_gst{g}"], [f"@{kind}{j}"])
                    else:
                        COPY(pbuf[:, 2 + 512 * i:2 + 512 * (i + 1)], PS[b][:], [psk(b)], [pk], eng="act")
                if kind != "xbc":
                    return
                b = bank()
                for hh, c0 in enumerate((0, T + 2)):
                    for k in range(8):
                        MM(PS[b][:, 2 * hh:2 * hh + 2], wbf[s][:, k, :], hT[:, k, c0:c0 + 2], k == 0, k == 7,
                           [f"C_wbf{s}", "@hT"], [psk(b)])
                E("dve", "tensor_copy", [psk(b)], [pk], out=pbuf[:, 0:2], in_=PS[b][:, 0:2])
                E("dve", "tensor_copy", [psk(b)], [pk], out=pbuf[:, T + 2:T + 4], in_=PS[b][:, 2:4])
                ACT(cacc[:], pbuf[:, 0:T], AF.Identity, [pk, "C_cw", "C_cb"], [ck],
                    scale=cw[:, ci * 5:ci * 5 + 1], bias=cb[:, ci:ci + 1])
                for w in range(1, 5):
                    E("dve", "scalar_tensor_tensor", [pk, "C_cw", ck], [ck], out=cacc[:],
                      in0=pbuf[:, w:T + w], scalar=cw[:, ci * 5 + w:ci * 5 + w + 1], in1=cacc[:],
                      op0=ALU.mult, op1=ALU.add)

            def Y(n):
                kind, ci, col = chunks[n]
                if kind != "xbc":
                    return
                s = n % 2
                cacc = caccs[s]
                ck = f"C_cacc{s}"
                xs_ = n % 2
                ACT(xc[xs_][:], cacc[:], AF.Silu, [ck], [f"C_xc{xs_}"])
                if ci >= 16 and not other:
                    g = (ci - 16) % 4
                    dst = (sc["BT"] if ci < 20 else sc["CT"])[g]
                    DMA("pool", dst, xc[xs_][:], [f"C_xc{xs_}"], [f"@{'BT' if ci < 20 else 'CT'}{j}"])
                if ci < 20:
                    if ci < 16:
                        dv = sc["xs_tm"].rearrange("(t p) c -> p t c", p=128)
                        dkey, ccol = f"@xs_tm{j}", ci * 128
                    else:
                        dv = sc["B_tm"].rearrange("(t p) c -> p t c", p=128)
                        dkey, ccol = f"@B_tm{j}", (ci - 16) * 128
                    for tg in range(4):
                        b = bank()
                        pv = PS[b][:].bitcast(BF16).rearrange("p (k t) -> p k t", k=8)
                        for t8 in range(8):
                            tt = tg * 8 + t8
                            TR(pv[:, t8, :], xc[xs_][:, tt * 128:(tt + 1) * 128], cbf(IDENT),
                               [f"C_xc{xs_}", "cstbf"], [psk(b)])
                        ts_ = cnts["t"] % 2
                        cnts["t"] += 1
                        COPY(tmst[ts_][:], pv, [psk(b)], [f"C_tm{ts_}"], eng="act")
                        t0 = tok0 // 128 + tg * 8
                        DMA("pool", dv[:, t0:t0 + 8, ccol:ccol + 128], tmst[ts_][:], [f"C_tm{ts_}"], [dkey])

            X(0)
            for n in range(len(chunks)):
                if n + 1 < len(chunks):
                    X(n + 1)
                Y(n)

    def stage_C_z(j, hT):
        sc = S[j]
        with contextlib.ExitStack() as st:
            def sbt(name, shape, dt=F32):
                return st.enter_context(nc.sbuf_tensor(f"{name}_u{next(uid)}", list(shape), dt))
            w32 = [sbt(f"Z_w32{i}", [128, 8, 512]) for i in range(2)]
            wbf = [sbt(f"Z_wbf{i}", [128, 8, 512], BF16) for i in range(2)]
            zst = [sbt(f"Z_st{i}", [128, 512], BF16) for i in range(3)]
            load_w_chunk(w32[0], wbf[0], "Z_w320", "Z_wbf0", w_in[:, C_Z:C_Z + 512], 512)
            for nb in range(4):
                s = nb % 2
                if nb + 1 < 4:
                    load_w_chunk(w32[1 - s], wbf[1 - s], f"Z_w32{1 - s}", f"Z_wbf{1 - s}",
                                 w_in[:, C_Z + 512 * (nb + 1):C_Z + 512 * (nb + 2)], 512)
                for t in range(NCH):
                    b = bank()
                    for k in range(8):
                        MM(PS[b][:], hT[:, k, 2 + 128 * t:2 + 128 * (t + 1)], wbf[s][:, k, :], k == 0, k == 7,
                           [f"Z_wbf{s}", "@hT"], [psk(b)])
                    g = (nb * NCH + t) % 3
                    ACT(zst[g][:], PS[b][:], AF.Silu, [psk(b)], [f"Z_st{g}"])
                    DMA("pool", sc["z_tm"][128 * t:128 * (t + 1), 512 * nb:512 * (nb + 1)], zst[g][:], [f"Z_st{g}"],
                        [f"@z_tm{j}"])

    def stage_C_dt(j, hT, other):
        sc = S[j]
        tok0 = T if other else 0
        with contextlib.ExitStack() as st:
            def sbt(name, shape, dt=F32):
                return st.enter_context(nc.sbuf_tensor(f"{name}_u{next(uid)}", list(shape), dt))
            w32 = sbt("T_w32", [128, 8, 64])
            wbf = sbt("T_wbf", [128, 8, 64], BF16)
            bias = sbt("T_bias", [128, 64])
            alg = sbt("T_alg", [128, 64])
            v = [sbt(f"T_v{i}", [128, 8, 64]) for i in range(2)]
            av = [sbt(f"T_av{i}", [128, 8, 64]) for i in range(2)]
            dtt = [sbt(f"T_dt{i}", [128, 8, 64]) for i in range(2)]
            aa = [sbt(f"T_a{i}", [128, 8, 64]) for i in range(2)]
            load_w_chunk(w32, wbf, "T_w32", "T_wbf", w_dt[j], 64)
            DMA("sp", bias[:], dtb[j, 0:1, :].partition_broadcast(128), [], ["T_bias"])
            DMA("sp", alg[:], alog[j, 0:1, :].partition_broadcast(128), [], ["T_alg"])
            ACT(alg[:], alg[:], AF.Exp, ["T_alg"], ["T_alg"])
            for tb in range(4):
                s = tb % 2
                b = bank()
                pv = PS[b][:].rearrange("p (t c) -> p t c", t=8)
                for t8 in range(8):
                    t = tb * 8 + t8
                    for k in range(8):
                        MM(pv[:, t8, :], hT[:, k, 2 + 128 * t:2 + 128 * (t + 1)], wbf[:, k, :], k == 0, k == 7,
                           ["T_wbf", "@hT"], [psk(b)])
                E("dve", "tensor_tensor", [psk(b), "T_bias"], [f"T_v{s}"], out=v[s][:], in0=pv,
                  in1=bias[:].unsqueeze(1).to_broadcast([128, 8, 64]), op=ALU.add)
                E("dve", "scalar_tensor_tensor", [f"T_v{s}"], [f"T_av{s}"], out=av[s][:], in0=v[s][:], scalar=-1.0,
                  in1=v[s][:], op0=ALU.mult, op1=ALU.max)
                ACT(av[s][:], av[s][:], AF.Exp, [f"T_av{s}"], [f"T_av{s}"], scale=-1.0)
                ACT(av[s][:], av[s][:], AF.Ln, [f"T_av{s}"], [f"T_av{s}"], bias=1.0, scale=1.0)
                E("dve", "scalar_tensor_tensor", [f"T_v{s}", f"T_av{s}"], [f"T_dt{s}"], out=dtt[s][:], in0=v[s][:],
                  scalar=0.0, in1=av[s][:], op0=ALU.max, op1=ALU.add)
                E("dve", "scalar_tensor_tensor", [f"T_dt{s}", "T_alg"], [f"T_a{s}"], out=aa[s][:], in0=dtt[s][:],
                  scalar=-1.0, in1=alg[:].unsqueeze(1).to_broadcast([128, 8, 64]), op0=ALU.mult, op1=ALU.mult)
                r0 = tok0 + tb * 1024
                DMA("pool", sc["dt"][r0:r0 + 1024, :].rearrange("(t p) c -> p t c", p=128), dtt[s][:], [f"T_dt{s}"],
                    [f"@dt{j}"])
                DMA("pool", sc["a"][r0:r0 + 1024, :].rearrange("(t p) c -> p t c", p=128), aa[s][:], [f"T_a{s}"],
                    [f"@a{j}"])

    def stage_C_lat(j, hT, other, ssqr):
        sc = S[j]
        tok0 = T if other else 0
        with contextlib.ExitStack() as st:
            def sbt(name, shape, dt=F32):
                return st.enter_context(nc.sbuf_tensor(f"{name}_u{next(uid)}", list(shape), dt))
            w32 = sbt("L_w32", [128, 8, 768])
            wbf = sbt("L_wbf", [128, 8, 768], BF16)
            ga = sbt("L_ga", [128, 5])
            qkg = sbt("L_qkg", [128, 6])
            sq = [sbt(f"L_sq{i}", [128, 512], BF16) for i in range(3)]
            lnb = sbt("L_ln", [128, 512])
            rsb = sbt("L_rs", [128, 512])
            ost = [sbt(f"L_ost{i}", [128, 3, 512], BF16) for i in range(2)]
            ct = [sbt(f"L_ct{i}", [64, 512]) for i in range(2)]
            stt_ = [sbt(f"L_stt{i}", [64, 512]) for i in range(2)]
            t1 = sbt("L_t1", [64, 512])
            t2 = sbt("L_t2", [64, 512])
            krst = [sbt(f"L_krst{i}", [64, 512], BF16) for i in range(2)]
            DMA("sp", w32[:, :, 0:704], w_in[:, 0:704].rearrange("(kc p) n -> p kc n", p=128), [], ["L_w32"])
            DMA("sp", w32[:, :, 704:768], w_krs[:, :].rearrange("(kc p) n -> p kc n", p=128), [], ["L_w32"])
            E("pool", "tensor_copy", ["L_w32"], ["L_wbf"], out=wbf[:], in_=w32[:])
            DMA("sp", ga[:, 0:3], gq_a[:, :], [], ["L_ga"])
            DMA("sp", ga[:, 3:5], gkv_a[:, :], [], ["L_ga"])
            DMA("sp", qkg[:], qk_g[:, :], [], ["L_qkg"])
            groups = ([] if other else [("qnT", 0, 3, 384.0, 0)]) + [("kvnT", 3, 2, 256.0, 3)]
            for i in range(8):
                cols = slice(2 + 512 * i, 2 + 512 * (i + 1))
                s = i % 2
                for (name, c0, ncn, nf, g0) in groups:
                    bs = []
                    for c in range(ncn):
                        b = bank()
                        bs.append(b)
                        for k in range(8):
                            MM(PS[b][:], wbf[:, k, (c0 + c) * 128:(c0 + c + 1) * 128], hT[:, k, cols], k == 0, k == 7,
                               ["L_wbf", "@hT"], [psk(b)])
                        ACT(sq[c][:], PS[b][:], AF.Square, [psk(b)], [f"L_sq{c}"])
                    bq = bank()
                    for c in range(ncn):
                        MM(PS[bq][:], onesbf[:], sq[c][:], c == 0, c == ncn - 1, ["onesbf", f"L_sq{c}"], [psk(bq)])
                    rstd_act(rsb[:], PS[bq][:], 1.0 / nf, [psk(bq)], ["L_rs"], lnb[:], "L_ln")
                    for c in range(ncn):
                        E("dve", "scalar_tensor_tensor", [psk(bs[c]), "L_ga", "L_rs"], [f"L_ost{s}"],
                          out=ost[s][:, c, :], in0=PS[bs[c]][:], scalar=ga[:, g0 + c:g0 + c + 1], in1=rsb[:],
                          op0=ALU.mult, op1=ALU.mult)
                    tcols = slice(tok0 + 512 * i, tok0 + 512 * (i + 1))
                    DMA("pool", sc[name][:, :, tcols].rearrange("c p t -> p c t"), ost[s][:, 0:ncn, :],
                        [f"L_ost{s}"], [f"@{name}{j}"])
                bk, bks = bank(), bank()
                for k in range(8):
                    MM(PS[bk][0:64, :], wbf[:, k, 640:704], hT[:, k, cols], k == 0, k == 7, ["L_wbf", "@hT"], [psk(bk)])
                for k in range(8):
                    MM(PS[bks][0:64, :], wbf[:, k, 704:768], hT[:, k, cols], k == 0, k == 7, ["L_wbf", "@hT"],
                       [psk(bks)])
                ACT(sq[0][0:64, :], PS[bk][0:64, :], AF.Square, [psk(bk)], ["L_sq0"])
                bq = bank()
                for t4 in range(4):
                    MM(PS[bq][:, t4:t4 + 1], sq[0][0:64, t4 * 128:(t4 + 1) * 128], onesbf[0:64, 0:1], True, True,
                       ["L_sq0", "onesbf"], [psk(bq)])
                kt0 = (tok0 + 512 * i) // 128
                E("dve", "tensor_copy", [psk(bq)], ["ssqr"], out=ssqr[:, kt0:kt0 + 4], in_=PS[bq][:, 0:4])
                pos = slice(tok0 + 512 * i, tok0 + 512 * (i + 1))
                DMA("sp", ct[s][:], ropeC[j, :, pos], [], [f"L_ct{s}"])
                DMA("sp", stt_[s][:], ropeS[j, :, pos], [], [f"L_stt{s}"])
                E("dve", "scalar_tensor_tensor", [psk(bk), "L_qkg", f"L_ct{s}"], ["L_t1"], out=t1[:],
                  in0=PS[bk][0:64, :], scalar=qkg[0:64, 4:5], in1=ct[s][:], op0=ALU.mult, op1=ALU.mult)
                E("dve", "scalar_tensor_tensor", [psk(bks), "L_qkg", f"L_stt{s}"], ["L_t2"], out=t2[:],
                  in0=PS[bks][0:64, :], scalar=qkg[0:64, 5:6], in1=stt_[s][:], op0=ALU.mult, op1=ALU.mult)
                E("dve", "tensor_tensor", ["L_t1", "L_t2"], [f"L_krst{s}"], out=krst[s][:], in0=t1[:], in1=t2[:],
                  op=ALU.add)
                DMA("pool", sc["KRg"][:, pos], krst[s][:], [f"L_krst{s}"], [f"@KRg{j}"])

    def stage_D(j, ssqr):
        from collections import deque
        sc = S[j]
        tk = T * (1 + j)
        nkt = tk // 128
        LOOK = 2
        SB = (0, 6)
        with contextlib.ExitStack() as st:
            def sbt(name, shape, dt=F32):
                return st.enter_context(nc.sbuf_tensor(f"{name}_u{next(uid)}", list(shape), dt))
            qn = sbt("D_qn", [128, 3, T], BF16)
            kvn = sbt("D_kvn", [128, 2, tk], BF16)
            krg = sbt("D_krg", [128, tk], BF16)
            kn = [sbt(f"D_kn{i}", [128, tk], BF16) for i in range(2)]
            vv = [sbt(f"D_v{i}", [128, nkt, 128], BF16) for i in range(2)]
            qkg = sbt("D_qkg", [128, 6])
            w32 = sbt("D_w32", [128, 3 * 256 + 2 * 256])
            wq = [sbt(f"D_wq{i}", [128, 3, 256], BF16) for i in range(2)]
            wk = [sbt(f"D_wk{i}", [128, 2, 256], BF16) for i in range(2)]
            sqk = sbt("D_sqk0", [128, 512], BF16)
            ssqk = sbt("D_ssqk", [128, 64])
            rks = [sbt(f"D_rks{i}", [128, 64]) for i in range(2)]
            qrn = sbt("D_qrn", [128, 512])
            qrr = sbt("D_qrr", [64, 512])
            qrs = sbt("D_qrs", [64, 512])
            sqn = sbt("D_sqn", [128, 512], BF16)
            sqr = sbt("D_sqr", [64, 512], BF16)
            lnb = sbt("D_ln", [128, 512])
            rsb = sbt("D_rs", [128, 512])
            QN = [sbt(f"D_QN{i}", [128, 512], BF16) for i in range(2)]
            QR = [sbt(f"D_QR{i}", [128, 512], BF16) for i in range(2)]
            ct = sbt("D_ct0", [64, 512])
            stt_ = sbt("D_stt0", [64, 512])
            NPT = 4
            PT = [sbt(f"D_PT{i}", [128, 512], BF16) for i in range(NPT)]
            pacc = [sbt(f"D_pacc{i}", [128, 512]) for i in range(2)]
            lnr = sbt("D_lnr", [128, 512])
            rinv = sbt("D_rinv", [128, 512])
            aost = [sbt(f"D_ao{i}", [128, 512], BF16) for i in range(2)]
            for c in range(3):
                DMA("sp", qn[:, c, :], sc["qnT"][c], [f"@qnT{j}"], ["D_qn"])
            for c in range(2):
                DMA("sp", kvn[:, c, :], sc["kvnT"][c], [f"@kvnT{j}"], ["D_kvn"])
            E("dve", "memset", [], ["D_krg"], krg[64:128, :], 0.0)
            DMA("sp", krg[0:64, :], sc["KRg"][:, :], [f"@KRg{j}"], ["D_krg"])
            DMA("sp", qkg[:], qk_g[:, :], [], ["D_qkg"])
            for i in range(2):
                E("dve", "memset", [], [f"D_QR{i}"], QR[i][64:128, :], 0.0)
            ptc = {"i": 0}
            hi, lo = deque(), deque()

            sqk2 = [sqk, sbt("D_sqk1", [128, 512], BF16)]

            def kv_tasks(h):
                s = h % 2
                tl = []

                def t_w():
                    wv32 = w32[:, 0:768].rearrange("p (c n) -> p c n", c=3)
                    DMA("sp", wv32[:, :, 0:128], w_qn[h].rearrange("(c p) n -> p c n", p=128), [], ["D_w32"])
                    DMA("sp", wv32[:, :, 128:192], w_qr[h].rearrange("(c p) n -> p c n", p=128), [], ["D_w32"])
                    DMA("sp", wv32[:, :, 192:256], w_qs[h].rearrange("(c p) n -> p c n", p=128), [], ["D_w32"])
                    wk32 = w32[:, 768:1280].rearrange("p (c n) -> p c n", c=2)
                    DMA("sp", wk32[:, :, 0:128], w_kn[h].rearrange("(c p) n -> p c n", p=128), [], ["D_w32"])
                    DMA("sp", wk32[:, :, 128:256], w_v[h].rearrange("(c p) n -> p c n", p=128), [], ["D_w32"])
                    E("pool", "tensor_copy", ["D_w32"], [f"D_wq{s}"], out=wq[s][:], in_=wv32)
                    E("pool", "tensor_copy", ["D_w32"], [f"D_wk{s}"], out=wk[s][:], in_=wk32)
                tl += [t_w] + [None] * 8
                ng = tk // 512

                def mk_k(i):
                    def t_k():
                        b = bank(*SB)
                        cols = slice(512 * i, 512 * (i + 1))
                        for c in range(2):
                            MM(PS[b][:], wk[s][:, c, 0:128], kvn[:, c, cols], c == 0, c == 1, [f"D_wk{s}", "D_kvn"],
                               [psk(b)])

                        def fin():
                            ACT(kn[s][:, cols], PS[b][:], AF.Identity, [psk(b), "D_qkg"], [f"D_kn{s}"],
                                scale=qkg[:, 3:4])
                            ACT(sqk2[i % 2][:], PS[b][:], AF.Square, [psk(b)], [f"D_sqk{i % 2}"])
                        return fin
                    return t_k

                def mk_k2(i):
                    def t_k2():
                        b = bank(*SB)
                        for t4 in range(4):
                            MM(PS[b][:, t4:t4 + 1], sqk2[i % 2][:, t4 * 128:(t4 + 1) * 128], onesbf[:, 0:1], True, True,
                               [f"D_sqk{i % 2}", "onesbf"], [psk(b)])

                        def fin():
                            E("dve", "tensor_tensor", [psk(b), "ssqr"], ["D_ssqk"], out=ssqk[:, 4 * i:4 * i + 4],
                              in0=PS[b][:, 0:4], in1=ssqr[:, 4 * i:4 * i + 4], op=ALU.add)
                        return fin
                    return t_k2
                seq = []
                for i in range(ng):
                    seq.append(mk_k(i))
                    if i >= 1:
                        seq.append(mk_k2(i - 1))
                seq += [None, mk_k2(ng - 1), None]
                tl += seq

                def t_r1():
                    ACT(ssqk[:, 0:nkt], ssqk[:, 0:nkt], AF.Ln, ["D_ssqk"], ["D_ssqk"], bias=eps_c[:, 0:1],
                        scale=1.0 / 192)

                def t_r2():
                    ACT(ssqk[:, 0:nkt], ssqk[:, 0:nkt], AF.Exp, ["D_ssqk"], ["D_ssqk"], scale=-0.5)

                def t_r3():
                    E("dve", "tensor_scalar_mul", ["D_ssqk"], [f"D_rks{s}"], out=rks[s][:, 0:nkt], in0=ssqk[:, 0:nkt],
                      scalar1=SM_SCALE)
                tl += [t_r1, t_r2, t_r3]
                for i in range(ng):
                    def t_v(i=i):
                        b = bank(*SB)
                        pv = PS[b][:].rearrange("p (t d) -> p t d", t=4)
                        for t4 in range(4):
                            kt = 4 * i + t4
                            for c in range(2):
                                MM(pv[:, t4, :], kvn[:, c, kt * 128:(kt + 1) * 128], wk[s][:, c, 128:256], c == 0,
                                   c == 1, [f"D_wk{s}", "D_kvn"], [psk(b)])

                        def fin():
                            COPY(vv[s][:, 4 * i:4 * i + 4, :], pv, [psk(b)], [f"@D_v{s}"])
                        return fin
                    tl.append(t_v)
                return tl

            def q_tasks(h, qb, qs):
                s = h % 2
                qcols = slice(512 * qb, 512 * (qb + 1))

                def proj(dst, dkey, lo_, hi_, rows, sq, sqkey):
                    def t():
                        b = bank(*SB)
                        for c in range(3):
                            MM(PS[b][0:rows, :], wq[s][:, c, lo_:hi_], qn[:, c, qcols], c == 0, c == 2,
                               [f"D_wq{s}", "D_qn"], [psk(b)])

                        def fin():
                            ACT(dst[:], PS[b][0:rows, :], AF.Identity, [psk(b)], [dkey])
                            if sq is not None:
                                ACT(sq[:], PS[b][0:rows, :], AF.Square, [psk(b)], [sqkey])
                        return fin
                    return t

                def t_tab():
                    DMA("sp", ct[:], ropeC[j, :, qcols], [], ["D_ct0"])
                    DMA("sp", stt_[:], ropeS[j, :, qcols], [], ["D_stt0"])

                def t_ss():
                    b = bank(*SB)
                    MM(PS[b][:], onesbf[:], sqn[:], True, False, ["onesbf", "D_sqn"], [psk(b)])
                    MM(PS[b][:], onesbf[0:64, :], sqr[:], False, True, ["onesbf", "D_sqr"], [psk(b)])

                    def fin():
                        ACT(lnb[:], PS[b][:], AF.Ln, [psk(b)], ["D_ln"], bias=eps_c[:, 0:1], scale=1.0 / 192)
                    return fin

                def t_rs():
                    ACT(rsb[:], lnb[:], AF.Exp, ["D_ln"], ["D_rs"], scale=-0.5)

                def t_d1():
                    E("dve", "scalar_tensor_tensor", ["D_qrn", "D_qkg", "D_rs"], [f"D_QN{qs}"], out=QN[qs][:],
                      in0=qrn[:], scalar=qkg[:, 0:1], in1=rsb[:], op0=ALU.mult, op1=ALU.mult)

                def t_d2():
                    E("dve", "scalar_tensor_tensor", ["D_qrr", "D_qkg", "D_rs"], ["D_qrr"], out=qrr[:],
                      in0=qrr[:], scalar=qkg[0:64, 1:2], in1=rsb[0:64, :], op0=ALU.mult, op1=ALU.mult)

                def t_d3():
                    E("dve", "tensor_mul", ["D_qrr", "D_ct0"], ["D_qrr"], out=qrr[:], in0=qrr[:], in1=ct[:])

                def t_d4():
                    E("dve", "scalar_tensor_tensor", ["D_qrs", "D_qkg", "D_rs"], ["D_qrs"], out=qrs[:],
                      in0=qrs[:], scalar=qkg[0:64, 2:3], in1=rsb[0:64, :], op0=ALU.mult, op1=ALU.mult)

                def t_d5():
                    E("dve", "tensor_mul", ["D_qrs", "D_stt0"], ["D_qrs"], out=qrs[:], in0=qrs[:], in1=stt_[:])

                def t_d6():
                    E("dve", "tensor_tensor", ["D_qrr", "D_qrs"], [f"D_QR{qs}"], out=QR[qs][0:64, :], in0=qrr[:],
                      in1=qrs[:], op=ALU.add)
                return [t_tab, proj(qrn, "D_qrn", 0, 128, 128, sqn, "D_sqn"), proj(qrr, "D_qrr", 128, 192, 64, sqr, "D_sqr"),
                        proj(qrs, "D_qrs", 192, 256, 64, None, None), t_ss, t_rs, t_d1, t_d2, t_d3, t_d4, t_d5, t_d6]

            def epi_tasks(h, qb, qs):
                qcols = slice(512 * qb, 512 * (qb + 1))
                bo = 6 + qs
                pa, pak = pacc[qs], f"D_pacc{qs}"

                def t_e1():
                    b = bank(*SB)
                    MM(PS[b][:], ones32[:], pa[:], True, True, ["ones32", pak], [psk(b)])

                    def fin():
                        ACT(lnr[:], PS[b][:], AF.Ln, [psk(b)], ["D_lnr"])
                    return fin

                def t_e2():
                    ACT(rinv[:], lnr[:], AF.Exp, ["D_lnr"], ["D_rinv"], scale=-1.0)

                def t_e3():
                    E("dve", "tensor_tensor", [psk(bo), "D_rinv"], [f"D_ao{qs}"], out=aost[qs][:], in0=PS[bo][:],
                      in1=rinv[:], op=ALU.mult)
                    DMA("pool", sc["AO"][h, :, qcols], aost[qs][:], [f"D_ao{qs}"], [f"@AO{j}"])
                return [None, t_e1, t_e2, t_e3]

            def pop_bg():
                t = hi.popleft() if hi else (lo.popleft() if lo else None)
                return t() if t is not None else None

            def flash(h, qb, qs):
                s = h % 2
                bo = 6 + qs
                pa, pak = pacc[qs], f"D_pacc{qs}"

                def s_mm(jt):
                    b = bank(*SB)
                    kc = slice(128 * jt, 128 * (jt + 1))
                    MM(PS[b][:], kn[s][:, kc], QN[qs][:], True, False, [f"D_kn{s}", f"D_QN{qs}"], [psk(b)])
                    MM(PS[b][:], krg[:, kc], QR[qs][:], False, True, ["D_krg", f"D_QR{qs}"], [psk(b)])
                    return b

                def acc_pt(jt, p):
                    if jt == 0:
                        E("dve", "tensor_copy", [f"D_PT{p}"], [pak], out=pa[:], in_=PT[p][:])
                    else:
                        E("dve", "tensor_tensor", [f"D_PT{p}", pak], [pak], out=pa[:], in0=pa[:], in1=PT[p][:],
                          op=ALU.add)
                prev_p = None
                pend = [s_mm(t) for t in range(min(LOOK, nkt))]
                for jt in range(nkt):
                    fin = pop_bg()
                    if jt + LOOK < nkt:
                        pend.append(s_mm(jt + LOOK))
                    bcur = pend.pop(0)
                    p = ptc["i"] % NPT
                    ptc["i"] += 1
                    ACT(PT[p][:], PS[bcur][:], AF.Exp, [psk(bcur), f"D_rks{s}"], [f"D_PT{p}"],
                        scale=rks[s][:, jt:jt + 1])
                    MM(PS[bo][:], vv[s][:, jt, :], PT[p][:], jt == 0, jt == nkt - 1, [f"@D_v{s}", f"D_PT{p}"],
                       [psk(bo)], late=(f"D_PT{p}",))
                    if jt >= 1:
                        acc_pt(jt - 1, prev_p)
                    prev_p = p
                    if fin is not None:
                        fin()
                acc_pt(nkt - 1, prev_p)

            def flush(q):
                while q:
                    t = q.popleft()
                    if t is not None:
                        f = t()
                        if f is not None:
                            f()

            def run_all(tl):
                flush(deque(tl))

            blocks = [(h, qb) for h in range(8) for qb in range(8)]
            run_all(kv_tasks(0))
            run_all(q_tasks(0, 0, 0))
            for n, (h, qb) in enumerate(blocks):
                flush(hi)
                if qb == 0:
                    flush(lo)
                    if h + 1 < 8:
                        lo.extend(kv_tasks(h + 1))
                if n >= 1:
                    hp, qp = blocks[n - 1]
                    hi.extend(epi_tasks(hp, qp, (n - 1) % 2))
                if n + 1 < len(blocks):
                    h2, qb2 = blocks[n + 1]
                    if h2 != h:
                        hi.extend(lo)
                        lo.clear()
                    hi.extend(q_tasks(h2, qb2, (n + 1) % 2))
                flash(h, qb, n % 2)
            flush(hi)
            flush(lo)
            run_all(epi_tasks(*blocks[-1], (len(blocks) - 1) % 2))

    def stage_E(j):
        sc = S[j]
        with contextlib.ExitStack() as st:
            def sbt(name, shape, dt=F32):
                return st.enter_context(nc.sbuf_tensor(f"{name}_u{next(uid)}", list(shape), dt))
            xs = [sbt(f"E_xs{i}", [128, 2048], BF16) for i in range(3)]
            btm = [sbt(f"E_btm{i}", [128, 512], BF16) for i in range(3)]
            bt = [sbt(f"E_bt{i}", [128, 4, 128], BF16) for i in range(3)]
            ctt = [sbt(f"E_ct{i}", [128, 4, 128], BF16) for i in range(3)]
            dtl = [sbt(f"E_dt{i}", [128, 64]) for i in range(3)]
            al = [sbt(f"E_a{i}", [128, 64]) for i in range(3)]
            ybl = [sbt(f"E_yb{i}", [128, 2048]) for i in range(3)]
            zl = [sbt(f"E_z{i}", [128, 2048], BF16) for i in range(3)]
            acs = [sbt(f"E_acs{i}", [128, 32]) for i in range(2)]
            dsub = [sbt(f"E_dsub{i}", [128, 32]) for i in range(2)]
            ea = [sbt(f"E_ea{i}", [128, 32]) for i in range(2)]
            eds = [sbt(f"E_eds{i}", [128, 32]) for i in range(2)]
            etot = [sbt(f"E_etot{i}", [128, 32]) for i in range(2)]
            dtw = [sbt(f"E_dtw{i}", [128, 32]) for i in range(2)]
            abf = [sbt(f"E_abf{i}", [128, 32], BF16) for i in range(2)]
            xdt = [sbt(f"E_xdt{i}", [128, 2048], BF16) for i in range(2)]
            xdtw = [sbt(f"E_xdtw{i}", [128, 2048], BF16) for i in range(2)]
            R = [sbt(f"E_R{i}", [128, 32, 128], BF16) for i in range(1)]
            Em = [sbt(f"E_E{i}", [128, 32, 128], BF16) for i in range(1)]
            Mm = [sbt(f"E_M{i}", [128, 32, 128], BF16) for i in range(1)]
            cbm = [sbt(f"E_cbm{i}", [128, 4, 128], BF16) for i in range(1)]
            H = sbt("E_H", [128, 2048])
            Hbf = sbt("E_Hbf", [128, 2048], BF16)
            ytmp = [sbt(f"E_ytmp{i}", [128, 512]) for i in range(2)]
            Y = [sbt(f"E_Y{i}", [128, 2048]) for i in range(2)]
            gain = sbt("E_gain", [128, 2048])
            dsum = sbt("E_dsum", [128, 64])
            gss = sbt("E_gss", [128, 4])
            gln = sbt("E_gln", [128, 4])
            grs = sbt("E_grs", [128, 4])
            junk = sbt("E_junk", [128, 512])
            yn = sbt("E_yn", [128, 2048], BF16)
            ynst = [sbt(f"E_ynst{i}", [128, 16, 128], BF16) for i in range(2)]
            DMA("sp", gain[:], ssm_g[0:1, :].partition_broadcast(128), [], ["E_gain"])
            DMA("sp", dsum[:], dskip[j, 0:1, :].partition_broadcast(128), [], ["E_dsum"])
            E("dve", "tensor_tensor", ["E_dsum"], ["E_dsum"], out=dsum[:, 0:32], in0=dsum[:, 0:32], in1=dsum[:, 32:64],
              op=ALU.add)
            Dmat = sbt("E_Dmat", [128, 32, 128], BF16)
            E("dve", "tensor_tensor", ["E_dsum", "cstbf"], ["E_Dmat"], out=Dmat[:],
              in0=cbf(IDENT).unsqueeze(1).to_broadcast([128, 32, 128]),
              in1=dsum[:, 0:32].unsqueeze(2).to_broadcast([128, 32, 128]), op=ALU.mult)
            cnt = 0
            for sweep in ("b", "f"):
                di = 1 if sweep == "b" else 0
                tri32 = c32(GE if sweep == "b" else LE)
                tribf = cbf(GE if sweep == "b" else LE)
                Ubf = cbf(LT if sweep == "b" else GT)
                order = []
                if sweep == "b":
                    if j == 1:
                        order += [(T + 128 * c, True, c) for c in reversed(range(NCH))]
                    order += [(128 * c, False, c) for c in reversed(range(NCH))]
                else:
                    order += [(128 * c, False, c) for c in range(NCH)]
                E("dve", "memset", [], ["E_H"], H[:], 0.0)
                E("pool", "memset", [], ["E_Hbf"], Hbf[:], 0.0)
                def p1a(k):
                    r0, state_only, c = order[k]
                    s = k % 2
                    l = k % 3
                    rows = slice(r0, r0 + 128)
                    DMA("sp", xs[l][:], sc["xs_tm"][rows, :], [f"@xs_tm{j}"], [f"E_xs{l}"])
                    DMA("sp", btm[l][:], sc["B_tm"][rows, :], [f"@B_tm{j}"], [f"E_btm{l}"])
                    DMA("sp", dtl[l][:], sc["dt"][rows, :], [f"@dt{j}"], [f"E_dt{l}"])
                    DMA("sp", al[l][:], sc["a"][rows, :], [f"@a{j}"], [f"E_a{l}"])
                    if not state_only:
                        DMA("sp", bt[l][:], sc["BT"][:, :, rows].rearrange("g p t -> p g t"), [f"@BT{j}"], [f"E_bt{l}"])
                        DMA("sp", ctt[l][:], sc["CT"][:, :, rows].rearrange("g p t -> p g t"), [f"@CT{j}"],
                            [f"E_ct{l}"])
                        if sweep == "f":
                            DMA("sp", ybl[l][:], sc["yb"][rows, :], [f"@yb{j}"], [f"E_yb{l}"])
                            DMA("sp", zl[l][:], sc["z_tm"][rows, :], [f"@z_tm{j}"], [f"E_z{l}"])
                    a32 = al[l][:, 32 * di:32 * di + 32]
                    dt32 = dtl[l][:, 32 * di:32 * di + 32]
                    bsm = bank(4, 8)
                    MM(PS[bsm][:, 0:32], tri32, a32, True, True, ["cst32", f"E_a{l}"], [psk(bsm)])
                    MM(PS[bsm][:, 32:64], ones32[:], a32, True, True, ["ones32", f"E_a{l}"], [psk(bsm)])
                    E("dve", "tensor_copy", [psk(bsm)], [f"E_acs{s}"], out=acs[s][:], in_=PS[bsm][:, 0:32])
                    E("dve", "tensor_tensor", [psk(bsm), f"E_acs{s}"], [f"E_dsub{s}"], out=dsub[s][:], in0=PS[bsm][:, 32:64],
                      in1=acs[s][:], op=ALU.subtract)
                    ACT(ea[s][:], acs[s][:], AF.Exp, [f"E_acs{s}"], [f"E_ea{s}"])
                    ACT(eds[s][:], dsub[s][:], AF.Exp, [f"E_dsub{s}"], [f"E_eds{s}"])
                    ACT(etot[s][:], PS[bsm][:, 32:64], AF.Exp, [psk(bsm)], [f"E_etot{s}"])
                    E("dve", "tensor_tensor", [f"E_dt{l}", f"E_eds{s}"], [f"E_dtw{s}"], out=dtw[s][:], in0=dt32, in1=eds[s][:],
                      op=ALU.mult)
                    xs3 = xs[l][:].rearrange("p (h d) -> p h d", h=32)
                    E("dve", "tensor_tensor", [f"E_xs{l}", f"E_dtw{s}"], [f"E_xdtw{s}"],
                      out=xdtw[s][:].rearrange("p (h d) -> p h d", h=32), in0=xs3,
                      in1=dtw[s][:].unsqueeze(2).to_broadcast([128, 32, 64]), op=ALU.mult)
                    if state_only:
                        return
                    E("dve", "tensor_tensor", [f"E_xs{l}", f"E_dt{l}"], [f"E_xdt{s}"],
                      out=xdt[s][:].rearrange("p (h d) -> p h d", h=32), in0=xs3,
                      in1=dt32.unsqueeze(2).to_broadcast([128, 32, 64]), op=ALU.mult)
                    for h in range(32):
                        ACT(R[0][:, h, :], tribf, AF.Identity, ["cstbf", f"E_a{l}"], ["E_R0"], scale=a32[:, h:h + 1])
                    bcb = bank(4, 8)
                    pcb = PS[bcb][:].rearrange("p (g t) -> p g t", g=4)
                    for g in range(4):
                        MM(pcb[:, g, :], bt[l][:, g, :], ctt[l][:, g, :], True, True, [f"E_bt{l}", f"E_ct{l}"],
                           [psk(bcb)])
                    E("dve", "tensor_tensor", [psk(bcb), "cstbf"], ["E_cbm0"], out=cbm[0][:], in0=pcb,
                      in1=tribf.unsqueeze(1).to_broadcast([128, 4, 128]), op=ALU.mult)
                def p1b(k):
                    r0, state_only, c = order[k]
                    if state_only:
                        return
                    s = k % 2
                    l = k % 3
                    for q in range(8):
                        bd = bank(0, 4)
                        MM(PS[bd][:], Ubf, R[0][:, 4 * q:4 * q + 4, :].rearrange("p h t -> p (h t)"), True, True,
                           ["cstbf", "E_R0"], [psk(bd)])
                        ACT(Em[0][:, 4 * q:4 * q + 4, :].rearrange("p h t -> p (h t)"), PS[bd][:], AF.Exp, [psk(bd)],
                            ["E_E0"])
                    for g in range(4):
                        E("dve", "tensor_tensor", ["E_E0", "E_cbm0"], ["E_M0"], out=Mm[0][:, 8 * g:8 * g + 8, :],
                          in0=Em[0][:, 8 * g:8 * g + 8, :],
                          in1=cbm[0][:, g, :].unsqueeze(1).to_broadcast([128, 8, 128]), op=ALU.mult)

                def p2(k):
                    r0, state_only, c = order[k]
                    s = k % 2
                    l = k % 3
                    ys = s
                    rows = slice(r0, r0 + 128)
                    xs3 = xs[l][:].rearrange("p (h d) -> p h d", h=32)
                    if not state_only:
                        for g in range(4):
                            byd, byo = bank(0, 4), bank(0, 4)
                            for hh in range(8):
                                h = 8 * g + hh
                                MM(PS[byd][:, 64 * hh:64 * hh + 64], Mm[0][:, h, :], xdt[s][:, 64 * h:64 * h + 64], True,
                                   sweep == "b", ["E_M0", f"E_xdt{s}"], [psk(byd)])
                                if sweep == "f":
                                    MM(PS[byd][:, 64 * hh:64 * hh + 64], Dmat[:, h, :], xs[l][:, 64 * h:64 * h + 64], False,
                                       True, ["E_Dmat", f"E_xs{l}"], [psk(byd)])
                            MM(PS[byo][:], ctt[l][:, g, :], Hbf[:, 512 * g:512 * g + 512], True, True,
                               [f"E_ct{l}", "E_Hbf"], [psk(byo)])
                            E("dve", "tensor_tensor", [psk(byo), f"E_ea{s}"], [f"E_ytmp{s}"],
                              out=ytmp[s][:].rearrange("p (h d) -> p h d", h=8),
                              in0=PS[byo][:].rearrange("p (h d) -> p h d", h=8),
                              in1=ea[s][:, 8 * g:8 * g + 8].unsqueeze(2).to_broadcast([128, 8, 64]), op=ALU.mult)
                            E("dve", "tensor_tensor", [psk(byd), f"E_ytmp{s}"], [f"E_Y{ys}"],
                              out=Y[ys][:, 512 * g:512 * g + 512], in0=PS[byd][:], in1=ytmp[s][:], op=ALU.add)

                def p2s(k):
                    r0, state_only, c = order[k]
                    s = k % 2
                    l = k % 3
                    ys = s
                    rows = slice(r0, r0 + 128)
                    for g in range(4):
                        bst = bank(4, 8)
                        MM(PS[bst][:], btm[l][:, 128 * g:128 * g + 128], xdtw[s][:, 512 * g:512 * g + 512], True, True,
                           [f"E_btm{l}", f"E_xdtw{s}"], [psk(bst)])
                        hv = H[:, 512 * g:512 * g + 512]
                        E("dve", "tensor_tensor", ["E_H", f"E_etot{s}"], ["E_H"],
                          out=hv.rearrange("p (h d) -> p h d", h=8), in0=hv.rearrange("p (h d) -> p h d", h=8),
                          in1=etot[s][:, 8 * g:8 * g + 8].unsqueeze(2).to_broadcast([128, 8, 64]), op=ALU.mult)
                        E("dve", "tensor_tensor", ["E_H", psk(bst)], ["E_H"], out=hv, in0=hv, in1=PS[bst][:],
                          op=ALU.add)
                    ACT(Hbf[:], H[:], AF.Copy, ["E_H"], ["E_Hbf"])
                    if state_only:
                        return
                    if sweep == "b":
                        DMA("pool", sc["yb"][rows, :], Y[ys][:], [f"E_Y{ys}"], [f"@yb{j}"])
                        return
                    yv = Y[ys]
                    yk = f"E_Y{ys}"
                    E("dve", "tensor_tensor", [yk, f"E_yb{l}"], [yk], out=yv[:], in0=yv[:], in1=ybl[l][:], op=ALU.add)
                    E("dve", "tensor_tensor", [yk, f"E_z{l}"], [yk], out=yv[:], in0=yv[:], in1=zl[l][:], op=ALU.mult)

                def p2b(k):
                    r0, state_only, c = order[k]
                    if state_only or sweep == "b":
                        return
                    s = k % 2
                    ys = s
                    rows = slice(r0, r0 + 128)
                    yv = Y[ys]
                    yk = f"E_Y{ys}"
                    for g in range(4):
                        ACT(junk[:], yv[:, 512 * g:512 * g + 512], AF.Square, [yk], ["E_junk", "E_gss"],
                            accum_out=gss[:, g:g + 1])
                    ACT(gln[:], gss[:], AF.Ln, ["E_gss"], ["E_gln"], bias=eps_c[:, 0:1], scale=1.0 / 512)
                    ACT(grs[:], gln[:], AF.Exp, ["E_gln"], ["E_grs"], scale=-0.5)
                    for g in range(4):
                        E("dve", "scalar_tensor_tensor", [yk, "E_grs", "E_gain"], ["E_yn"],
                          out=yn[:, 512 * g:512 * g + 512], in0=yv[:, 512 * g:512 * g + 512], scalar=grs[:, g:g + 1],
                          in1=gain[:, 512 * g:512 * g + 512], op0=ALU.mult, op1=ALU.mult)
                    ns = c % 2
                    for half in range(2):
                        b = bank(4, 8)
                        pv = PS[b][:].bitcast(BF16).rearrange("p (k t) -> p k t", k=8)
                        for k8 in range(8):
                            cc = 8 * half + k8
                            TR(pv[:, k8, :], yn[:, 128 * cc:128 * cc + 128], cbf(IDENT), ["E_yn", "cstbf"], [psk(b)])
                        COPY(ynst[ns][:, 8 * half:8 * half + 8, :], pv, [psk(b)], [f"E_ynst{ns}"], eng="act")
                    DMA("pool", sc["ynT"][:, :, rows].rearrange("c p t -> p c t"), ynst[ns][:], [f"E_ynst{ns}"],
                        [f"@ynT{j}"])

                p1a(0)
                p1b(0)
                for k in range(len(order)):
                    if k + 1 < len(order):
                        p1a(k + 1)
                    p2(k)
                    if k + 1 < len(order):
                        p1b(k + 1)
                    p2s(k)
                    p2b(k)

    def stage_F(j):
        sc = S[j]
        with contextlib.ExitStack() as st:
            def sbt(name, shape, dt=F32):
                return st.enter_context(nc.sbuf_tensor(f"{name}_u{next(uid)}", list(shape), dt))
            w32s = [sbt(f"F_w32{i}", [128, 8, 512]) for i in range(2)]
            fcnt = {"i": 0}
            wpa = sbt("F_wpa", [128, 8, D], BF16)
            wpb = sbt("F_wpb", [128, 16, D], BF16)
            wo = sbt("F_wo", [128, 8, D], BF16)
            ao = sbt("F_ao", [128, 8, 512], BF16)
            sga = sbt("F_sga", [128, 8, 512], BF16)
            ynt = sbt("F_ynt", [128, 16, 512], BF16)
            sgg = sbt("F_sgg", [128, 16, 512], BF16)
            t1 = sbt("F_t1", [128, 512])
            t2 = sbt("F_t2", [128, 512])
            mg = sbt("F_mg", [128, 8, 512], BF16)
            xt = [sbt(f"F_xt{i}", [128, D]) for i in range(2)]
            yo = [sbt(f"F_yo{i}", [128, D]) for i in range(2)]
            for (src, dst, nk, key) in ((w_pa, wpa, 8, "F_wpa"), (w_pb, wpb, 16, "F_wpb"), (w_o, wo, 8, "F_wo")):
                sv = src.rearrange("(kc p) n -> p kc n", p=128)
                for k0 in range(0, nk, 8):
                    for nb in range(2):
                        wsl = fcnt["i"] % 2
                        fcnt["i"] += 1
                        DMA("sp", w32s[wsl][:], sv[:, k0:k0 + 8, 512 * nb:512 * nb + 512], [], [f"F_w32{wsl}"])
                        COPY(dst[:, k0:k0 + 8, 512 * nb:512 * nb + 512], w32s[wsl][:], [f"F_w32{wsl}"], ["@" + key])
            xcnt = 0
            for i in range(8):
                cols = slice(512 * i, 512 * i + 512)
                DMA("sp", ao[:], sc["AO"][:, :, cols].rearrange("h p t -> p h t"), [f"@AO{j}"], ["F_ao"])
                DMA("sp", sga[:], sc["sga"][:, :, cols].rearrange("h p t -> p h t"), [f"@sga{j}"], ["F_sga"])
                DMA("sp", ynt[:], sc["ynT"][:, :, cols].rearrange("h p t -> p h t"), [f"@ynT{j}"], ["F_ynt"])
                DMA("sp", sgg[:], sc["sig"][:, :, cols].rearrange("h p t -> p h t"), [f"@sig{j}"], ["F_sgg"])
                E("dve", "tensor_tensor", ["F_ao", "F_sga"], ["F_ao"], out=ao[:], in0=ao[:], in1=sga[:], op=ALU.mult)
                for dc in range(8):
                    ba, bb_ = bank(), bank()
                    for k in range(8):
                        MM(PS[ba][:], wpa[:, k, 128 * dc:128 * dc + 128], ao[:, k, :], k == 0, k == 7, ["@F_wpa", "F_ao"],
                           [psk(ba)])
                    for k in range(16):
                        MM(PS[bb_][:], wpb[:, k, 128 * dc:128 * dc + 128], ynt[:, k, :], k == 0, k == 15,
                           ["@F_wpb", "F_ynt"], [psk(bb_)])
                    E("dve", "tensor_tensor", [psk(ba), "F_sgg"], ["F_t1"], out=t1[:], in0=PS[ba][:], in1=sgg[:, dc, :],
                      op=ALU.mult)
                    E("dve", "tensor_tensor", [psk(bb_), "F_sgg"], ["F_t2"], out=t2[:], in0=PS[bb_][:],
                      in1=sgg[:, 8 + dc, :], op=ALU.mult)
                    E("dve", "tensor_tensor", ["F_t1", "F_t2"], ["F_mg"], out=mg[:, dc, :], in0=t1[:], in1=t2[:],
                      op=ALU.add)
                for tt in range(4):
                    xsl = xcnt % 2
                    xcnt += 1
                    r0 = 512 * i + 128 * tt
                    DMA("sp", xt[xsl][:], x_in[j, r0:r0 + 128, :], [], [f"F_xt{xsl}"])
                    for nb in range(2):
                        b = bank()
                        for k in range(8):
                            MM(PS[b][:], mg[:, k, 128 * tt:128 * tt + 128], wo[:, k, 512 * nb:512 * nb + 512], k == 0,
                               k == 7, ["F_mg", "@F_wo"], [psk(b)])
                        ysl = yo[xsl][:, 512 * nb:512 * nb + 512]
                        E("dve", "tensor_tensor", [psk(b), f"gateb{j}"], [f"F_yo{xsl}"], out=ysl, in0=PS[b][:],
                          in1=gateb[j][:, 512 * nb:512 * nb + 512], op=ALU.mult)
                        E("dve", "tensor_tensor", [f"F_yo{xsl}", f"F_xt{xsl}"], [f"F_yo{xsl}"], out=ysl, in0=ysl,
                          in1=xt[xsl][:, 512 * nb:512 * nb + 512], op=ALU.add)
                    DMA("pool", y_out[j, r0:r0 + 128, :], yo[xsl][:], [f"F_yo{xsl}"], ["@y_out"])

    with es:
        for j in range(2):
            with contextlib.ExitStack() as jst:
                ssqr = jst.enter_context(nc.sbuf_tensor(f"ssqr{j}", [128, 64], F32))
                gmod[j] = jst.enter_context(nc.sbuf_tensor(f"gmod{j}", [128, D], F32))
                shiftb[j] = jst.enter_context(nc.sbuf_tensor(f"shiftb{j}", [128, D], F32))
                gateb[j] = jst.enter_context(nc.sbuf_tensor(f"gateb{j}", [128, D], F32))
                if run("A"):
                    stage_A(j)
                passes = ([True] if j == 1 else []) + [False]
                for other in passes:
                    with contextlib.ExitStack() as pst:
                        hT = pst.enter_context(nc.sbuf_tensor(f"hT{j}{int(other)}", [128, 8, T + 4], BF16))
                        if run("B"):
                            xi = 2 if other else j
                            srcs = [(xi, 128 * t, 128, 2 + 128 * t) for t in range(NCH)]
                            if other:
                                srcs.append((1, T - 2, 2, 0))
                                E("dve", "memset", [], ["@hT"], hT[:, :, T + 2:T + 4], 0.0)
                            else:
                                E("dve", "memset", [], ["@hT"], hT[:, :, 0:2], 0.0)
                                if j == 1:
                                    srcs.append((2, 0, 2, T + 2))
                                else:
                                    E("dve", "memset", [], ["@hT"], hT[:, :, T + 2:T + 4], 0.0)
                            stage_B(j, hT, srcs)
                            P.barrier()
                        if run("C"):
                            stage_C_lat(j, hT, other, ssqr)
                            P.barrier()
                            stage_C_dt(j, hT, other)
                            P.barrier()
                            if not other:
                                stage_C_z(j, hT)
                                P.barrier()
                            stage_C_fm(j, hT, other)
                    P.barrier()
                if run("D"):
                    stage_D(j, ssqr)
                    P.barrier()
                if run("E"):
                    stage_E(j)
                    P.barrier()
                if run("F"):
                    stage_F(j)
                    P.barrier()
        P.barrier()
        P.op("sp", None, ["@y_out"], [])
        P.emit()
    return nc


def _rope_tables(pos):
    half = 32
    inv_freq = np.exp((-np.log(np.float32(10000.0)) * np.arange(half, dtype=np.float32) / np.float32(half)).astype(np.float32)).astype(np.float32)
    ang = (pos.astype(np.float32)[:, None] * inv_freq[None, :]).astype(np.float32)
    c, s = np.cos(ang).astype(np.float32), np.sin(ang).astype(np.float32)
    C = np.concatenate([c, c], axis=1).T
    Ssg = np.concatenate([-s, s], axis=1).T
    return np.ascontiguousarray(C), np.ascontiguousarray(Ssg)


def make_in_maps(x_prompt, x_sample, c_prompt, c_sample, norm_g, w_ada, b_ada, w_in, q_a_norm, w_q_up,
                 kv_a_norm, w_kv_up, q_norm, k_norm, w_proj_a, conv_w, conv_b, dt_bias_f, dt_bias_b,
                 a_log_f, a_log_b, d_f, d_b, ssm_norm, w_proj_b, w_out):
    f = lambda a: np.ascontiguousarray(np.asarray(a, dtype=np.float32))
    w_in0 = f(w_in[0])
    sw = np.concatenate([np.arange(32, 64), np.arange(0, 32)])
    wq = f(w_q_up[0]).reshape(384, 8, 192)
    wkv = f(w_kv_up[0]).reshape(256, 8, 256)
    qn_, kn_ = f(q_norm[0]), f(k_norm[0])
    qk_g = np.zeros((128, 6), np.float32)
    qk_g[:, 0] = qn_[:128]
    qk_g[:64, 1] = qn_[128:]
    qk_g[:64, 2] = qn_[128:][sw]
    qk_g[:, 3] = kn_[:128]
    qk_g[:64, 4] = kn_[128:]
    qk_g[:64, 5] = kn_[128:][sw]
    pidx = np.arange(128)[:, None]
    fidx = np.arange(128)[None, :]
    consts = np.concatenate([(pidx == fidx), (pidx <= fidx), (pidx >= fidx), (pidx > fidx), (pidx < fidx)],
                            axis=1).astype(np.float32)
    cw = f(conv_w[0])
    common = dict(
        w_ada=f(w_ada[0]), b_ada=f(b_ada), norm_g=f(norm_g), w_in=w_in0,
        w_krs=f(w_in0[:, C_KR:C_KR + 64][:, sw]),
        w_qn=f(wq[:, :, :128].transpose(1, 0, 2)), w_qr=f(wq[:, :, 128:].transpose(1, 0, 2)),
        w_qs=f(wq[:, :, 128:][:, :, sw].transpose(1, 0, 2)),
        w_kn=f(wkv[:, :, :128].transpose(1, 0, 2)), w_v=f(wkv[:, :, 128:].transpose(1, 0, 2)),
        gq_a=f(f(q_a_norm[0]).reshape(3, 128).T), gkv_a=f(f(kv_a_norm[0]).reshape(2, 128).T), qk_g=qk_g,
        conv_bc=f(f(conv_b[0]).reshape(24, 128).T), ssm_g=f(ssm_norm), w_pa=f(w_proj_a[0]), w_pb=f(w_proj_b[0]),
        w_o=f(w_out[0]), consts=consts,
    )
    wdt_f, wdt_b = w_in0[:, C_DTF:C_DTF + 32], w_in0[:, C_DTB:C_DTB + 32]
    fwd = dict(w_dt=np.concatenate([wdt_f, wdt_b], 1), dtb=np.concatenate([f(dt_bias_f[0]), f(dt_bias_b[0])]),
               alog=np.concatenate([f(a_log_f[0]), f(a_log_b[0])]), dskip=np.concatenate([f(d_f[0]), f(d_b[0])]),
               cw=cw)
    rev = dict(w_dt=np.concatenate([wdt_b, wdt_f], 1), dtb=np.concatenate([f(dt_bias_b[0]), f(dt_bias_f[0])]),
               alog=np.concatenate([f(a_log_b[0]), f(a_log_f[0])]), dskip=np.concatenate([f(d_b[0]), f(d_f[0])]),
               cw=cw[::-1])
    xp, xs_, cp, cs = f(x_prompt), f(x_sample), f(c_prompt), f(c_sample)
    Cp, Sp = _rope_tables(np.arange(8192))
    Cr, Sr = _rope_tables(np.arange(8191, -1, -1))
    in_maps = []
    for i in range(8):
        b, half = i // 2, i % 2
        if half == 0:
            own, oth, o1 = xs_[b, :T], xs_[b, T:], fwd
            C1, S1 = Cp, Sp
        else:
            own, oth, o1 = xs_[b, :T - 1:-1], xs_[b, T - 1::-1], rev
            C1, S1 = Cr, Sr
        jobs = (fwd, o1)
        m = dict(common)
        m["x"] = np.ascontiguousarray(np.stack([xp[i], own, oth]))
        m["cT"] = np.ascontiguousarray(np.stack([cp[i].reshape(8, 128).T, cs[b].reshape(8, 128).T]))
        m["w_dt"] = np.ascontiguousarray(np.stack([jb["w_dt"] for jb in jobs]))
        m["dtb"] = np.ascontiguousarray(np.stack([jb["dtb"][None] for jb in jobs]))
        m["alog"] = np.ascontiguousarray(np.stack([jb["alog"][None] for jb in jobs]))
        m["dskip"] = np.ascontiguousarray(np.stack([jb["dskip"][None] for jb in jobs]))
        m["conv_wc"] = np.ascontiguousarray(np.stack(
            [jb["cw"].reshape(5, 24, 128).transpose(2, 1, 0).reshape(128, 120) for jb in jobs]))
        m["ropeC"] = np.ascontiguousarray(np.stack([Cp, C1]))
        m["ropeS"] = np.ascontiguousarray(np.stack([Sp, S1]))
        in_maps.append(m)
    return in_maps


_NC_CACHE = {}


def kernel(**inputs):
    in_maps = make_in_maps(**inputs)
    if "nc" not in _NC_CACHE:
        _NC_CACHE["nc"] = build_program()
    res = run_bass_kernel_spmd(_NC_CACHE["nc"], in_maps, core_ids=list(range(8)))
    y_prompt = np.empty((8, T, D), np.float32)
    y_sample = np.empty((4, 2 * T, D), np.float32)
    for i in range(8):
        y = res.results[i]["y"]
        y_prompt[i] = y[0]
        b, half = i // 2, i % 2
        if half == 0:
            y_sample[b, :T] = y[1]
        else:
            y_sample[b, T:] = y[1][::-1]
    return (y_prompt, y_sample)
```

```python
import contextlib
import numpy as np
import concourse.bass as bass
import concourse.mybir as mybir
from concourse.bass_utils import run_bass_kernel_spmd

F32 = mybir.dt.float32
BF16 = mybir.dt.bfloat16
AF = mybir.ActivationFunctionType
ALU = mybir.AluOpType

D = 1024
T = 4096
NCH = 32
EPS = 1e-6
N_IN = 8960
C_Q, C_KV, C_KR, C_GA, C_Z, C_X, C_B, C_C, C_DTF, C_DTB, C_GGA, C_GGB = (
    0, 384, 640, 704, 1728, 3776, 5824, 6336, 6848, 6880, 6912, 7936)
SM_SCALE = 192 ** -0.5


class _Op:
    __slots__ = ("eng", "fn", "reads", "writes", "dma", "deps", "sig", "late", "early")

    def __init__(self, eng, fn, reads, writes, dma, late=()):
        self.eng, self.fn, self.reads, self.writes, self.dma = eng, fn, reads, writes, dma
        self.deps = set()
        self.sig = None
        self.late = frozenset(late)
        self.early = set()


class Prog:
    EPOCH = 24000
    NDMA = 16

    def __init__(self, nc, es):
        self.nc, self.es = nc, es
        self.ops = []
        self.h = {"pe": nc.tensor, "act": nc.scalar, "dve": nc.vector, "pool": nc.gpsimd, "sp": nc.sync}
        self.barriers = []

    def op(self, eng, fn, reads=(), writes=(), dma=False, late=()):
        self.ops.append(_Op(eng, fn, tuple(reads), tuple(writes), dma, late))

    def barrier(self):
        self.barriers.append(len(self.ops))

    def emit(self):
        nc, ops = self.nc, self.ops
        lastw, readers = {}, {}
        bset = set(self.barriers)
        last_eng = {}
        dma_since = []
        pend_bar = {}
        dma_hist = {}
        for i, o in enumerate(ops):
            if i in bset:
                deps = set(last_eng.values()) | set(dma_since)
                for e in self.h:
                    pend_bar[e] = pend_bar.get(e, set()) | deps
                dma_since = []
                lastw, readers = {}, {}
            if o.eng in pend_bar:
                o.deps |= pend_bar.pop(o.eng)
                o.early |= o.deps
            raw = set()
            for r in o.reads:
                lw = lastw.get(r)
                if lw is not None:
                    ds = lw if isinstance(lw, set) else {lw}
                    raw |= ds
                    if r not in o.late:
                        o.early |= ds
            o.deps |= raw
            for w in o.writes:
                ds = set(readers.get(w, ()))
                lw = lastw.get(w)
                if lw is not None and not isinstance(lw, set):
                    ds.add(lw)
                o.deps |= ds
                if w not in o.late:
                    o.early |= ds
            for r in o.reads:
                readers.setdefault(r, []).append(i)
            for w in o.writes:
                if w.startswith("@"):
                    lastw.setdefault(w, set()).add(i)
                else:
                    lastw[w] = i
                    readers[w] = []
            if o.dma:
                hist = dma_hist.setdefault(o.eng, [])
                if len(hist) >= self.NDMA:
                    o.deps.add(hist[-self.NDMA])
                    o.early.add(hist[-self.NDMA])
                hist.append(i)
                dma_since.append(i)
            else:
                last_eng[o.eng] = i
            o.deps.discard(i)
            o.deps = {d for d in o.deps if ops[d].dma or ops[d].eng != o.eng
                      or (d in raw and o.eng != "pe")}
        needed = set()
        for o in ops:
            needed |= o.deps
        ccount = {}
        csems = {}
        dcount = {}
        dsems = {}
        for i, o in enumerate(ops):
            if o.dma:
                n = dcount.get(o.eng, 0)
                dcount[o.eng] = n + 1
                pool = dsems.setdefault(o.eng, [])
                if len(pool) < self.NDMA:
                    pool.append(self.es.enter_context(nc.semaphore(f"d_{o.eng}_{len(pool)}")))
                o.sig = (pool[n % self.NDMA], 16 * (n // self.NDMA + 1))
            elif i in needed:
                n = ccount.get(o.eng, 0)
                ccount[o.eng] = n + 1
                lst = csems.setdefault(o.eng, [])
                if n // self.EPOCH >= len(lst):
                    lst.append(self.es.enter_context(nc.semaphore(f"c_{o.eng}_{len(lst)}")))
                o.sig = (lst[n // self.EPOCH], n % self.EPOCH + 1)
        waited = {e: {} for e in self.h}
        for i, o in enumerate(ops):
            h = self.h[o.eng]
            wl = {}
            early_k = set()
            for d in o.deps:
                sem, val = ops[d].sig
                k = id(sem)
                if d in o.early or not o.late:
                    early_k.add(k)
                if waited[o.eng].get(k, 0) < val and wl.get(k, (None, 0))[1] < val:
                    wl[k] = (sem, val)
            attach = None
            for k, (sem, val) in wl.items():
                if k not in early_k and attach is None:
                    attach = (sem, val)
                    continue
                h.wait_ge(sem, val)
                waited[o.eng][k] = val
            if o.fn is None:
                continue
            ins = o.fn()
            if attach is not None:
                ins._wait_ge(attach[0], attach[1])
                waited[o.eng][id(attach[0])] = attach[1]
            if o.sig is not None:
                ins.then_inc(o.sig[0], 16 if o.dma else 1)


def build_program(debug=False, stages=None):
    nc = bass.Bass("TRN2", target_bir_lowering=False)
    es = contextlib.ExitStack()
    P = Prog(nc, es)
    run = (lambda s: True) if stages is None else (lambda s: s in stages)

    def din(name, shape, dt=F32):
        return nc.dram_tensor(name, list(shape), dt, kind="ExternalInput").ap()

    def dscr(name, shape, dt):
        kind = "ExternalOutput" if (debug is True or (debug and name in debug)) else "Internal"
        return nc.dram_tensor(name, list(shape), dt, kind=kind).ap()

    x_in = din("x", [3, T, D])
    cT_in = din("cT", [2, 128, 8])
    w_ada = din("w_ada", [D, 3 * D])
    b_ada = din("b_ada", [1, 3 * D])
    norm_g = din("norm_g", [1, D])
    w_in = din("w_in", [D, N_IN])
    w_krs = din("w_krs", [D, 64])
    w_dt = din("w_dt", [2, D, 64])
    w_qn = din("w_qn", [8, 384, 128])
    w_qr = din("w_qr", [8, 384, 64])
    w_qs = din("w_qs", [8, 384, 64])
    w_kn = din("w_kn", [8, 256, 128])
    w_v = din("w_v", [8, 256, 128])
    gq_a = din("gq_a", [128, 3])
    gkv_a = din("gkv_a", [128, 2])
    qk_g = din("qk_g", [128, 6])
    conv_wc = din("conv_wc", [2, 128, 24 * 5])
    conv_bc = din("conv_bc", [128, 24])
    dtb = din("dtb", [2, 1, 64])
    alog = din("alog", [2, 1, 64])
    dskip = din("dskip", [2, 1, 64])
    ssm_g = din("ssm_g", [1, 2048])
    w_pa = din("w_pa", [D, D])
    w_pb = din("w_pb", [2048, D])
    w_o = din("w_o", [D, D])
    ropeC = din("ropeC", [2, 64, 8192])
    ropeS = din("ropeS", [2, 64, 8192])
    consts = din("consts", [128, 5 * 128])
    y_out = nc.dram_tensor("y", [2, T, D], F32, kind="ExternalOutput").ap()

    S = []
    for j in range(2):
        tk = T * (1 + j)
        S.append(dict(
            qnT=dscr(f"qnT{j}", [3, 128, T], BF16), kvnT=dscr(f"kvnT{j}", [2, 128, tk], BF16),
            KRg=dscr(f"KRg{j}", [64, tk], BF16),
            sga=dscr(f"sga{j}", [8, 128, T], BF16), sig=dscr(f"sig{j}", [16, 128, T], BF16),
            xs_tm=dscr(f"xstm{j}", [tk, 2048], BF16), B_tm=dscr(f"Btm{j}", [tk, 512], BF16),
            BT=dscr(f"BT{j}", [4, 128, T], BF16), CT=dscr(f"CT{j}", [4, 128, T], BF16),
            z_tm=dscr(f"ztm{j}", [T, 2048], BF16),
            dt=dscr(f"dt{j}", [tk, 64], F32), a=dscr(f"a{j}", [tk, 64], F32),
            AO=dscr(f"AO{j}", [8, 128, T], BF16), yb=dscr(f"yb{j}", [T, 2048], F32),
            ynT=dscr(f"ynT{j}", [16, 128, T], BF16),
        ))

    import itertools
    uid = itertools.count()

    def sb(name, shape, dt=F32):
        return es.enter_context(nc.sbuf_tensor(name, list(shape), dt))

    PS = [es.enter_context(nc.psum_tensor(f"ps{i}", [128, 512], F32)) for i in range(8)]
    rot = {"i": 0}

    def bank(lo=0, hi=8):
        n = hi - lo
        b = lo + rot.setdefault((lo, hi), 0) % n
        rot[(lo, hi)] += 1
        return b

    def psk(b):
        return f"ps{b}"

    def E(eng, meth, reads, writes, *a, **kw):
        h = P.h[eng]
        P.op(eng, lambda: getattr(h, meth)(*a, **kw), reads, writes)

    def DMA(eng, out, in_, reads, writes, slow=False):
        h = P.h[eng]
        if slow:
            P.op(eng, lambda: h.dma_start(out=out, in_=in_, allow_slow_non_contiguous=True), reads, writes, dma=True)
        else:
            P.op(eng, lambda: h.dma_start(out=out, in_=in_), reads, writes, dma=True)

    def MM(out, lhsT, rhs, start, stop, reads, writes, late=()):
        P.op("pe", lambda: nc.tensor.matmul(out, lhsT=lhsT, rhs=rhs, start=start, stop=stop), reads, writes, late=late)

    def TR(out, in_, ident, reads, writes):
        P.op("pe", lambda: nc.tensor.transpose(out, in_, ident), reads, writes)

    def ACT(out, in_, func, reads, writes, **kw):
        P.op("act", lambda: nc.scalar.activation(out=out, in_=in_, func=func, **kw), reads, writes)

    cp_flip = {"i": 0}

    def COPY(out, in_, reads, writes, eng=None):
        if eng is None:
            eng = ("act", "dve")[cp_flip["i"] % 2]
            cp_flip["i"] += 1
        if eng == "act":
            ACT(out, in_, AF.Copy, reads, writes)
        else:
            E(eng, "tensor_copy", reads, writes, out=out, in_=in_)

    def rstd_act(out, in_, scale, reads, writes, tmp, tmpk):
        ACT(tmp, in_, AF.Ln, reads, [tmpk], bias=eps_c[0:in_.shape[0], 0:1], scale=scale)
        ACT(out, tmp, AF.Exp, [tmpk], writes, scale=-0.5)

    cst32 = sb("cst32", [128, 5 * 128])
    cstbf = sb("cstbf", [128, 5 * 128], BF16)
    ones32 = sb("ones32", [128, 128])
    onesbf = sb("onesbf", [128, 128], BF16)
    eps_c = sb("eps_c", [128, 1])
    DMA("sp", cst32[:], consts[:, :], [], ["cst32"])
    E("dve", "tensor_copy", ["cst32"], ["cstbf"], out=cstbf[:], in_=cst32[:])
    E("dve", "memset", [], ["ones32"], ones32[:], 1.0)
    E("dve", "memset", [], ["onesbf"], onesbf[:], 1.0)
    E("dve", "memset", [], ["eps_c"], eps_c[:], EPS)
    IDENT, LE, GE, GT, LT = range(5)

    def c32(i):
        return cst32[:, i * 128:(i + 1) * 128]

    def cbf(i):
        return cstbf[:, i * 128:(i + 1) * 128]

    gmod, shiftb, gateb = {}, {}, {}

    def stage_A(j):
        with contextlib.ExitStack() as st:
            def sbt(name, shape, dt=F32):
                return st.enter_context(nc.sbuf_tensor(f"{name}_u{next(uid)}", list(shape), dt))
            ct = sbt("A_ct", [128, 8])
            sg = sbt("A_sg", [128, 8])
            csb = sbt("A_csb", [128, 8, 128])
            wbuf = [sbt(f"A_w{i}", [128, 8, 512]) for i in range(2)]
            bb = sbt("A_bb", [128, 3 * D])
            ngb = sbt("A_ngb", [128, D])
            DMA("sp", ct[:], cT_in[j], [], ["A_ct"])
            DMA("sp", bb[:], b_ada[0:1, :].partition_broadcast(128), [], ["A_bb"])
            DMA("sp", ngb[:], norm_g[0:1, :].partition_broadcast(128), [], ["A_ngb"])
            ACT(sg[:], ct[:], AF.Exp, ["A_ct"], ["A_sg"], scale=-1.0)
            E("dve", "tensor_scalar_add", ["A_sg"], ["A_sg"], out=sg[:], in0=sg[:], scalar1=1.0)
            E("dve", "reciprocal", ["A_sg"], ["A_sg"], out=sg[:], in_=sg[:])
            E("dve", "tensor_mul", ["A_sg", "A_ct"], ["A_sg"], out=sg[:], in0=sg[:], in1=ct[:])
            E("dve", "tensor_copy", ["A_sg"], ["A_csb"], out=csb[:],
              in_=sg[:].unsqueeze(2).to_broadcast([128, 8, 128]))
            wv = w_ada.rearrange("(kc p) n -> p kc n", p=128)
            for nb in range(6):
                wb = wbuf[nb % 2]
                wk = f"A_w{nb % 2}"
                DMA("sp", wb[:], wv[:, :, nb * 512:(nb + 1) * 512], [], [wk])
                b = bank()
                for k in range(8):
                    MM(PS[b][:], csb[:, k, :], wb[:, k, :], k == 0, k == 7, [wk, "A_csb"], [psk(b)])
                sec, off = nb // 2, (nb % 2) * 512
                dst = (shiftb[j], gmod[j], gateb[j])[sec]
                dk = (f"shiftb{j}", f"gmod{j}", f"gateb{j}")[sec]
                E("dve", "tensor_tensor", [psk(b), "A_bb"], [dk], out=dst[:, off:off + 512], in0=PS[b][:],
                  in1=bb[:, nb * 512:(nb + 1) * 512], op=ALU.add)
            E("dve", "scalar_tensor_tensor", [f"gmod{j}", "A_ngb"], [f"gmod{j}"], out=gmod[j][:], in0=gmod[j][:],
              scalar=1.0, in1=ngb[:], op0=ALU.add, op1=ALU.mult)
        P.barrier()

    def stage_B(j, hT, srcs):
        with contextlib.ExitStack() as st:
            def sbt(name, shape, dt=F32):
                return st.enter_context(nc.sbuf_tensor(f"{name}_u{next(uid)}", list(shape), dt))
            xt = [sbt(f"B_xt{i}", [128, D]) for i in range(2)]
            junk = sbt("B_junk", [128, D])
            htmp = sbt("B_htmp", [128, D])
            hb = [sbt(f"B_hb{i}", [128, D], BF16) for i in range(2)]
            ssq = [sbt(f"B_ssq{i}", [128, 1]) for i in range(2)]
            lnv = [sbt(f"B_ln{i}", [128, 1]) for i in range(2)]
            rs = [sbt(f"B_rs{i}", [128, 1]) for i in range(2)]
            pend = None
            for n, (xi, r0, nr, c0) in enumerate(srcs):
                s = n % 2
                DMA("sp", xt[s][0:nr, :], x_in[xi, r0:r0 + nr, :], [], [f"B_xt{s}"])
                ACT(junk[0:nr, :], xt[s][0:nr, :], AF.Square, [f"B_xt{s}"], ["B_junk", f"B_ssq{s}"],
                    accum_out=ssq[s][0:nr, :])
                ACT(lnv[s][0:nr, :], ssq[s][0:nr, :], AF.Ln, [f"B_ssq{s}"], [f"B_ln{s}"], bias=eps_c[0:nr, :],
                    scale=1.0 / D)
                ACT(rs[s][0:nr, :], lnv[s][0:nr, :], AF.Exp, [f"B_ln{s}"], [f"B_rs{s}"], scale=-0.5)
                E("dve", "scalar_tensor_tensor", [f"B_xt{s}", f"B_rs{s}", f"gmod{j}"], ["B_htmp"],
                  out=htmp[0:nr, :], in0=xt[s][0:nr, :], scalar=rs[s][0:nr, 0:1], in1=gmod[j][0:nr, :],
                  op0=ALU.mult, op1=ALU.mult)
                E("dve", "tensor_tensor", ["B_htmp", f"shiftb{j}"], [f"B_hb{s}"], out=hb[s][0:nr, :],
                  in0=htmp[0:nr, :], in1=shiftb[j][0:nr, :], op=ALU.add)
                b = bank()
                pv = PS[b][:].bitcast(BF16).rearrange("p (k t) -> p k t", k=8)
                for k in range(8):
                    TR(pv[:, k, 0:nr], hb[s][0:nr, k * 128:(k + 1) * 128], cbf(IDENT)[0:nr, 0:nr],
                       [f"B_hb{s}", "cstbf"], [psk(b)])
                if pend is not None:
                    COPY(pend[0], pend[1], [pend[2]], ["@hT"], eng="dve")
                pend = (hT[:, :, c0:c0 + nr], pv[:, :, 0:nr], psk(b))
            COPY(pend[0], pend[1], [pend[2]], ["@hT"], eng="dve")

    def load_w_chunk(st_w32, st_wbf, key32, keybf, src_ap, ncols):
        DMA("sp", st_w32[:, :, 0:ncols], src_ap.rearrange("(kc p) n -> p kc n", p=128), [], [key32])
        E("pool", "tensor_copy", [key32], [keybf], out=st_wbf[:, :, 0:ncols], in_=st_w32[:, :, 0:ncols])

    def stage_C_fm(j, hT, other):
        sc = S[j]
        tok0 = T if other else 0
        with contextlib.ExitStack() as st:
            def sbt(name, shape, dt=F32):
                return st.enter_context(nc.sbuf_tensor(f"{name}_u{next(uid)}", list(shape), dt))
            w32 = [sbt(f"C_w32{i}", [128, 8, 128]) for i in range(2)]
            wbf = [sbt(f"C_wbf{i}", [128, 8, 128], BF16) for i in range(2)]
            gst = [sbt(f"C_gst{i}", [128, 512], BF16) for i in range(3)]
            pbufs = [sbt(f"C_pbuf{i}", [128, T + 4]) for i in range(2)]
            caccs = [sbt(f"C_cacc{i}", [128, T]) for i in range(2)]
            xc = [sbt(f"C_xc{i}", [128, T], BF16) for i in range(2)]
            tmst = [sbt(f"C_tm{i}", [128, 8, 128], BF16) for i in range(2)]
            cw = sbt("C_cw", [128, 24 * 5])
            cb = sbt("C_cb", [128, 24])
            DMA("sp", cw[:], conv_wc[j], [], ["C_cw"])
            DMA("sp", cb[:], conv_bc[:, :], [], ["C_cb"])
            chunks = []
            if not other:
                chunks += [("sig", i, C_GGA + 128 * i) for i in range(16)]
                chunks += [("sga", i, C_GA + 128 * i) for i in range(8)]
            nx = 20 if other else 24
            chunks += [("xbc", i, C_X + 128 * i) for i in range(nx)]
            load_w_chunk(w32[0], wbf[0], "C_w320", "C_wbf0", w_in[:, chunks[0][2]:chunks[0][2] + 128], 128)
            cnts = {"g": 0, "t": 0}

            def X(n):
                kind, ci, col = chunks[n]
                s = n % 2
                pbuf = pbufs[s]
                pk = f"@C_pbuf{s}"
                cacc = caccs[s]
                ck = f"C_cacc{s}"
                if n + 1 < len(chunks):
                    c2 = chunks[n + 1][2]
                    load_w_chunk(w32[1 - s], wbf[1 - s], f"C_w32{1 - s}", f"C_wbf{1 - s}", w_in[:, c2:c2 + 128], 128)
                for i in range(8):
                    b = bank()
                    for k in range(8):
                        MM(PS[b][:], wbf[s][:, k, :], hT[:, k, 2 + 512 * i:2 + 512 * (i + 1)], k == 0, k == 7,
                           [f"C_wbf{s}", "@hT"], [psk(b)])
                    if kind in ("sig", "sga"):
                        g = cnts["g"] % 3
                        cnts["g"] += 1
                        ACT(gst[g][:], PS[b][:], AF.Sigmoid if kind == "sig" else AF.Silu, [psk(b)], [f"C_gst{g}"])
                        dst = sc[kind][ci, :, 512 * i:512 * (i + 1)]
                        DMA("pool", dst, gst[g][:], [f"C_gst{g}"], [f"@{kind}{j}"])
                    else:
                        COPY(pbuf[:, 2 + 512 * i:2 + 512 * (i + 1)], PS[b][:], [psk(b)], [pk], eng="act")
                if kind != "xbc":
                    return
                b = bank()
                for hh, c0 in enumerate((0, T + 2)):
                    for k in range(8):
                        MM(PS[b][:, 2 * hh:2 * hh + 2], wbf[s][:, k, :], hT[:, k, c0:c0 + 2], k == 0, k == 7,
                           [f"C_wbf{s}", "@hT"], [psk(b)])
                E("dve", "tensor_copy", [psk(b)], [pk], out=pbuf[:, 0:2], in_=PS[b][:, 0:2])
                E("dve", "tensor_copy", [psk(b)], [pk], out=pbuf[:, T + 2:T + 4], in_=PS[b][:, 2:4])
                ACT(cacc[:], pbuf[:, 0:T], AF.Identity, [pk, "C_cw", "C_cb"], [ck],
                    scale=cw[:, ci * 5:ci * 5 + 1], bias=cb[:, ci:ci + 1])
                for w in range(1, 5):
                    E("dve", "scalar_tensor_tensor", [pk, "C_cw", ck], [ck], out=cacc[:],
                      in0=pbuf[:, w:T + w], scalar=cw[:, ci * 5 + w:ci * 5 + w + 1], in1=cacc[:],
                      op0=ALU.mult, op1=ALU.add)

            def Y(n):
                kind, ci, col = chunks[n]
                if kind != "xbc":
                    return
                s = n % 2
                cacc = caccs[s]
                ck = f"C_cacc{s}"
                xs_ = n % 2
                ACT(xc[xs_][:], cacc[:], AF.Silu, [ck], [f"C_xc{xs_}"])
                if ci >= 16 and not other:
                    g = (ci - 16) % 4
                    dst = (sc["BT"] if ci < 20 else sc["CT"])[g]
                    DMA("pool", dst, xc[xs_][:], [f"C_xc{xs_}"], [f"@{'BT' if ci < 20 else 'CT'}{j}"])
                if ci < 20:
                    if ci < 16:
                        dv = sc["xs_tm"].rearrange("(t p) c -> p t c", p=128)
                        dkey, ccol = f"@xs_tm{j}", ci * 128
                    else:
                        dv = sc["B_tm"].rearrange("(t p) c -> p t c", p=128)
                        dkey, ccol = f"@B_tm{j}", (ci - 16) * 128
                    for tg in range(4):
                        b = bank()
                        pv = PS[b][:].bitcast(BF16).rearrange("p (k t) -> p k t", k=8)
                        for t8 in range(8):
                            tt = tg * 8 + t8
                            TR(pv[:, t8, :], xc[xs_][:, tt * 128:(tt + 1) * 128], cbf(IDENT),
                               [f"C_xc{xs_}", "cstbf"], [psk(b)])
                        ts_ = cnts["t"] % 2
                        cnts["t"] += 1
                        COPY(tmst[ts_][:], pv, [psk(b)], [f"C_tm{ts_}"], eng="act")
                        t0 = tok0 // 128 + tg * 8
                        DMA("pool", dv[:, t0:t0 + 8, ccol:ccol + 128], tmst[ts_][:], [f"C_tm{ts_}"], [dkey])

            X(0)
            for n in range(len(chunks)):
                if n + 1 < len(chunks):
                    X(n + 1)
                Y(n)

    def stage_C_z(j, hT):
        sc = S[j]
        with contextlib.ExitStack() as st:
            def sbt(name, shape, dt=F32):
                return st.enter_context(nc.sbuf_tensor(f"{name}_u{next(uid)}", list(shape), dt))
            w32 = [sbt(f"Z_w32{i}", [128, 8, 512]) for i in range(2)]
            wbf = [sbt(f"Z_wbf{i}", [128, 8, 512], BF16) for i in range(2)]
            zst = [sbt(f"Z_st{i}", [128, 512], BF16) for i in range(3)]
            load_w_chunk(w32[0], wbf[0], "Z_w320", "Z_wbf0", w_in[:, C_Z:C_Z + 512], 512)
            for nb in range(4):
                s = nb % 2
                if nb + 1 < 4:
                    load_w_chunk(w32[1 - s], wbf[1 - s], f"Z_w32{1 - s}", f"Z_wbf{1 - s}",
                                 w_in[:, C_Z + 512 * (nb + 1):C_Z + 512 * (nb + 2)], 512)
                for t in range(NCH):
                    b = bank()
                    for k in range(8):
                        MM(PS[b][:], hT[:, k, 2 + 128 * t:2 + 128 * (t + 1)], wbf[s][:, k, :], k == 0, k == 7,
                           [f"Z_wbf{s}", "@hT"], [psk(b)])
                    g = (nb * NCH + t) % 3
                    ACT(zst[g][:], PS[b][:], AF.Silu, [psk(b)], [f"Z_st{g}"])
                    DMA("pool", sc["z_tm"][128 * t:128 * (t + 1), 512 * nb:512 * (nb + 1)], zst[g][:], [f"Z_st{g}"],
                        [f"@z_tm{j}"])

    def stage_C_dt(j, hT, other):
        sc = S[j]
        tok0 = T if other else 0
        with contextlib.ExitStack() as st:
            def sbt(name, shape, dt=F32):
                return st.enter_context(nc.sbuf_tensor(f"{name}_u{next(uid)}", list(shape), dt))
            w32 = sbt("T_w32", [128, 8, 64])
            wbf = sbt("T_wbf", [128, 8, 64], BF16)
            bias = sbt("T_bias", [128, 64])
            alg = sbt("T_alg", [128, 64])
            v = [sbt(f"T_v{i}", [128, 8, 64]) for i in range(2)]
            av = [sbt(f"T_av{i}", [128, 8, 64]) for i in range(2)]
            dtt = [sbt(f"T_dt{i}", [128, 8, 64]) for i in range(2)]
            aa = [sbt(f"T_a{i}", [128, 8, 64]) for i in range(2)]
            load_w_chunk(w32, wbf, "T_w32", "T_wbf", w_dt[j], 64)
            DMA("sp", bias[:], dtb[j, 0:1, :].partition_broadcast(128), [], ["T_bias"])
            DMA("sp", alg[:], alog[j, 0:1, :].partition_broadcast(128), [], ["T_alg"])
            ACT(alg[:], alg[:], AF.Exp, ["T_alg"], ["T_alg"])
            for tb in range(4):
                s = tb % 2
                b = bank()
                pv = PS[b][:].rearrange("p (t c) -> p t c", t=8)
                for t8 in range(8):
                    t = tb * 8 + t8
                    for k in range(8):
                        MM(pv[:, t8, :], hT[:, k, 2 + 128 * t:2 + 128 * (t + 1)], wbf[:, k, :], k == 0, k == 7,
                           ["T_wbf", "@hT"], [psk(b)])
                E("dve", "tensor_tensor", [psk(b), "T_bias"], [f"T_v{s}"], out=v[s][:], in0=pv,
                  in1=bias[:].unsqueeze(1).to_broadcast([128, 8, 64]), op=ALU.add)
                E("dve", "scalar_tensor_tensor", [f"T_v{s}"], [f"T_av{s}"], out=av[s][:], in0=v[s][:], scalar=-1.0,
                  in1=v[s][:], op0=ALU.mult, op1=ALU.max)
                ACT(av[s][:], av[s][:], AF.Exp, [f"T_av{s}"], [f"T_av{s}"], scale=-1.0)
                ACT(av[s][:], av[s][:], AF.Ln, [f"T_av{s}"], [f"T_av{s}"], bias=1.0, scale=1.0)
                E("dve", "scalar_tensor_tensor", [f"T_v{s}", f"T_av{s}"], [f"T_dt{s}"], out=dtt[s][:], in0=v[s][:],
                  scalar=0.0, in1=av[s][:], op0=ALU.max, op1=ALU.add)
                E("dve", "scalar_tensor_tensor", [f"T_dt{s}", "T_alg"], [f"T_a{s}"], out=aa[s][:], in0=dtt[s][:],
                  scalar=-1.0, in1=alg[:].unsqueeze(1).to_broadcast([128, 8, 64]), op0=ALU.mult, op1=ALU.mult)
                r0 = tok0 + tb * 1024
                DMA("pool", sc["dt"][r0:r0 + 1024, :].rearrange("(t p) c -> p t c", p=128), dtt[s][:], [f"T_dt{s}"],
                    [f"@dt{j}"])
                DMA("pool", sc["a"][r0:r0 + 1024, :].rearrange("(t p) c -> p t c", p=128), aa[s][:], [f"T_a{s}"],
                    [f"@a{j}"])

    def stage_C_lat(j, hT, other, ssqr):
        sc = S[j]
        tok0 = T if other else 0
        with contextlib.ExitStack() as st:
            def sbt(name, shape, dt=F32):
                return st.enter_context(nc.sbuf_tensor(f"{name}_u{next(uid)}", list(shape), dt))
            w32 = sbt("L_w32", [128, 8, 768])
            wbf = sbt("L_wbf", [128, 8, 768], BF16)
            ga = sbt("L_ga", [128, 5])
            qkg = sbt("L_qkg", [128, 6])
            sq = [sbt(f"L_sq{i}", [128, 512], BF16) for i in range(3)]
            lnb = sbt("L_ln", [128, 512])
            rsb = sbt("L_rs", [128, 512])
            ost = [sbt(f"L_ost{i}", [128, 3, 512], BF16) for i in range(2)]
            ct = [sbt(f"L_ct{i}", [64, 512]) for i in range(2)]
            stt_ = [sbt(f"L_stt{i}", [64, 512]) for i in range(2)]
            t1 = sbt("L_t1", [64, 512])
            t2 = sbt("L_t2", [64, 512])
            krst = [sbt(f"L_krst{i}", [64, 512], BF16) for i in range(2)]
            DMA("sp", w32[:, :, 0:704], w_in[:, 0:704].rearrange("(kc p) n -> p kc n", p=128), [], ["L_w32"])
            DMA("sp", w32[:, :, 704:768], w_krs[:, :].rearrange("(kc p) n -> p kc n", p=128), [], ["L_w32"])
            E("pool", "tensor_copy", ["L_w32"], ["L_wbf"], out=wbf[:], in_=w32[:])
            DMA("sp", ga[:, 0:3], gq_a[:, :], [], ["L_ga"])
            DMA("sp", ga[:, 3:5], gkv_a[:, :], [], ["L_ga"])
            DMA("sp", qkg[:], qk_g[:, :], [], ["L_qkg"])
            groups = ([] if other else [("qnT", 0, 3, 384.0, 0)]) + [("kvnT", 3, 2, 256.0, 3)]
            for i in range(8):
                cols = slice(2 + 512 * i, 2 + 512 * (i + 1))
                s = i % 2
                for (name, c0, ncn, nf, g0) in groups:
                    bs = []
                    for c in range(ncn):
                        b = bank()
                        bs.append(b)
                        for k in range(8):
                            MM(PS[b][:], wbf[:, k, (c0 + c) * 128:(c0 + c + 1) * 128], hT[:, k, cols], k == 0, k == 7,
                               ["L_wbf", "@hT"], [psk(b)])
                        ACT(sq[c][:], PS[b][:], AF.Square, [psk(b)], [f"L_sq{c}"])
                    bq = bank()
                    for c in range(ncn):
                        MM(PS[bq][:], onesbf[:], sq[c][:], c == 0, c == ncn - 1, ["onesbf", f"L_sq{c}"], [psk(bq)])
                    rstd_act(rsb[:], PS[bq][:], 1.0 / nf, [psk(bq)], ["L_rs"], lnb[:], "L_ln")
                    for c in range(ncn):
                        E("dve", "scalar_tensor_tensor", [psk(bs[c]), "L_ga", "L_rs"], [f"L_ost{s}"],
                          out=ost[s][:, c, :], in0=PS[bs[c]][:], scalar=ga[:, g0 + c:g0 + c + 1], in1=rsb[:],
                          op0=ALU.mult, op1=ALU.mult)
                    tcols = slice(tok0 + 512 * i, tok0 + 512 * (i + 1))
                    DMA("pool", sc[name][:, :, tcols].rearrange("c p t -> p c t"), ost[s][:, 0:ncn, :],
                        [f"L_ost{s}"], [f"@{name}{j}"])
                bk, bks = bank(), bank()
                for k in range(8):
                    MM(PS[bk][0:64, :], wbf[:, k, 640:704], hT[:, k, cols], k == 0, k == 7, ["L_wbf", "@hT"], [psk(bk)])
                for k in range(8):
                    MM(PS[bks][0:64, :], wbf[:, k, 704:768], hT[:, k, cols], k == 0, k == 7, ["L_wbf", "@hT"],
                       [psk(bks)])
                ACT(sq[0][0:64, :], PS[bk][0:64, :], AF.Square, [psk(bk)], ["L_sq0"])
                bq = bank()
                for t4 in range(4):
                    MM(PS[bq][:, t4:t4 + 1], sq[0][0:64, t4 * 128:(t4 + 1) * 128], onesbf[0:64, 0:1], True, True,
                       ["L_sq0", "onesbf"], [psk(bq)])
                kt0 = (tok0 + 512 * i) // 128
                E("dve", "tensor_copy", [psk(bq)], ["ssqr"], out=ssqr[:, kt0:kt0 + 4], in_=PS[bq][:, 0:4])
                pos = slice(tok0 + 512 * i, tok0 + 512 * (i + 1))
                DMA("sp", ct[s][:], ropeC[j, :, pos], [], [f"L_ct{s}"])
                DMA("sp", stt_[s][:], ropeS[j, :, pos], [], [f"L_stt{s}"])
                E("dve", "scalar_tensor_tensor", [psk(bk), "L_qkg", f"L_ct{s}"], ["L_t1"], out=t1[:],
                  in0=PS[bk][0:64, :], scalar=qkg[0:64, 4:5], in1=ct[s][:], op0=ALU.mult, op1=ALU.mult)
                E("dve", "scalar_tensor_tensor", [psk(bks), "L_qkg", f"L_stt{s}"], ["L_t2"], out=t2[:],
                  in0=PS[bks][0:64, :], scalar=qkg[0:64, 5:6], in1=stt_[s][:], op0=ALU.mult, op1=ALU.mult)
                E("dve", "tensor_tensor", ["L_t1", "L_t2"], [f"L_krst{s}"], out=krst[s][:], in0=t1[:], in1=t2[:],
                  op=ALU.add)
                DMA("pool", sc["KRg"][:, pos], krst[s][:], [f"L_krst{s}"], [f"@KRg{j}"])

    def stage_D(j, ssqr):
        from collections import deque
        sc = S[j]
        tk = T * (1 + j)
        nkt = tk // 128
        LOOK = 2
        SB = (0, 6)
        with contextlib.ExitStack() as st:
            def sbt(name, shape, dt=F32):
                return st.enter_context(nc.sbuf_tensor(f"{name}_u{next(uid)}", list(shape), dt))
            qn = sbt("D_qn", [128, 3, T], BF16)
            kvn = sbt("D_kvn", [128, 2, tk], BF16)
            krg = sbt("D_krg", [128, tk], BF16)
            kn = [sbt(f"D_kn{i}", [128, tk], BF16) for i in range(2)]
            vv = [sbt(f"D_v{i}", [128, nkt, 128], BF16) for i in range(2)]
            qkg = sbt("D_qkg", [128, 6])
            w32 = sbt("D_w32", [128, 3 * 256 + 2 * 256])
            wq = [sbt(f"D_wq{i}", [128, 3, 256], BF16) for i in range(2)]
            wk = [sbt(f"D_wk{i}", [128, 2, 256], BF16) for i in range(2)]
            sqk = sbt("D_sqk0", [128, 512], BF16)
            ssqk = sbt("D_ssqk", [128, 64])
            rks = [sbt(f"D_rks{i}", [128, 64]) for i in range(2)]
            qrn = sbt("D_qrn", [128, 512])
            qrr = sbt("D_qrr", [64, 512])
            qrs = sbt("D_qrs", [64, 512])
            sqn = sbt("D_sqn", [128, 512], BF16)
            sqr = sbt("D_sqr", [64, 512], BF16)
            lnb = sbt("D_ln", [128, 512])
            rsb = sbt("D_rs", [128, 512])
            QN = [sbt(f"D_QN{i}", [128, 512], BF16) for i in range(2)]
            QR = [sbt(f"D_QR{i}", [128, 512], BF16) for i in range(2)]
            ct = sbt("D_ct0", [64, 512])
            stt_ = sbt("D_stt0", [64, 512])
            NPT = 4
            PT = [sbt(f"D_PT{i}", [128, 512], BF16) for i in range(NPT)]
            pacc = [sbt(f"D_pacc{i}", [128, 512]) for i in range(2)]
            lnr = sbt("D_lnr", [128, 512])
            rinv = sbt("D_rinv", [128, 512])
            aost = [sbt(f"D_ao{i}", [128, 512], BF16) for i in range(2)]
            for c in range(3):
                DMA("sp", qn[:, c, :], sc["qnT"][c], [f"@qnT{j}"], ["D_qn"])
            for c in range(2):
                DMA("sp", kvn[:, c, :], sc["kvnT"][c], [f"@kvnT{j}"], ["D_kvn"])
            E("dve", "memset", [], ["D_krg"], krg[64:128, :], 0.0)
            DMA("sp", krg[0:64, :], sc["KRg"][:, :], [f"@KRg{j}"], ["D_krg"])
            DMA("sp", qkg[:], qk_g[:, :], [], ["D_qkg"])
            for i in range(2):
                E("dve", "memset", [], [f"D_QR{i}"], QR[i][64:128, :], 0.0)
            ptc = {"i": 0}
            hi, lo = deque(), deque()

            sqk2 = [sqk, sbt("D_sqk1", [128, 512], BF16)]

            def kv_tasks(h):
                s = h % 2
                tl = []

                def t_w():
                    wv32 = w32[:, 0:768].rearrange("p (c n) -> p c n", c=3)
                    DMA("sp", wv32[:, :, 0:128], w_qn[h].rearrange("(c p) n -> p c n", p=128), [], ["D_w32"])
                    DMA("sp", wv32[:, :, 128:192], w_qr[h].rearrange("(c p) n -> p c n", p=128), [], ["D_w32"])
                    DMA("sp", wv32[:, :, 192:256], w_qs[h].rearrange("(c p) n -> p c n", p=128), [], ["D_w32"])
                    wk32 = w32[:, 768:1280].rearrange("p (c n) -> p c n", c=2)
                    DMA("sp", wk32[:, :, 0:128], w_kn[h].rearrange("(c p) n -> p c n", p=128), [], ["D_w32"])
                    DMA("sp", wk32[:, :, 128:256], w_v[h].rearrange("(c p) n -> p c n", p=128), [], ["D_w32"])
                    E("pool", "tensor_copy", ["D_w32"], [f"D_wq{s}"], out=wq[s][:], in_=wv32)
                    E("pool", "tensor_copy", ["D_w32"], [f"D_wk{s}"], out=wk[s][:], in_=wk32)
                tl += [t_w] + [None] * 8
                ng = tk // 512

                def mk_k(i):
                    def t_k():
                        b = bank(*SB)
                        cols = slice(512 * i, 512 * (i + 1))
                        for c in range(2):
                            MM(PS[b][:], wk[s][:, c, 0:128], kvn[:, c, cols], c == 0, c == 1, [f"D_wk{s}", "D_kvn"],
                               [psk(b)])

                        def fin():
                            ACT(kn[s][:, cols], PS[b][:], AF.Identity, [psk(b), "D_qkg"], [f"D_kn{s}"],
                                scale=qkg[:, 3:4])
                            ACT(sqk2[i % 2][:], PS[b][:], AF.Square, [psk(b)], [f"D_sqk{i % 2}"])
                        return fin
                    return t_k

                def mk_k2(i):
                    def t_k2():
                        b = bank(*SB)
                        for t4 in range(4):
                            MM(PS[b][:, t4:t4 + 1], sqk2[i % 2][:, t4 * 128:(t4 + 1) * 128], onesbf[:, 0:1], True, True,
                               [f"D_sqk{i % 2}", "onesbf"], [psk(b)])

                        def fin():
                            E("dve", "tensor_tensor", [psk(b), "ssqr"], ["D_ssqk"], out=ssqk[:, 4 * i:4 * i + 4],
                              in0=PS[b][:, 0:4], in1=ssqr[:, 4 * i:4 * i + 4], op=ALU.add)
                        return fin
                    return t_k2
                seq = []
                for i in range(ng):
                    seq.append(mk_k(i))
                    if i >= 1:
                        seq.append(mk_k2(i - 1))
                seq += [None, mk_k2(ng - 1), None]
                tl += seq

                def t_r1():
                    ACT(ssqk[:, 0:nkt], ssqk[:, 0:nkt], AF.Ln, ["D_ssqk"], ["D_ssqk"], bias=eps_c[:, 0:1],
                        scale=1.0 / 192)

                def t_r2():
                    ACT(ssqk[:, 0:nkt], ssqk[:, 0:nkt], AF.Exp, ["D_ssqk"], ["D_ssqk"], scale=-0.5)

                def t_r3():
                    E("dve", "tensor_scalar_mul", ["D_ssqk"], [f"D_rks{s}"], out=rks[s][:, 0:nkt], in0=ssqk[:, 0:nkt],
                      scalar1=SM_SCALE)
                tl += [t_r1, t_r2, t_r3]
                for i in range(ng):
                    def t_v(i=i):
                        b = bank(*SB)
                        pv = PS[b][:].rearrange("p (t d) -> p t d", t=4)
                        for t4 in range(4):
                            kt = 4 * i + t4
                            for c in range(2):
                                MM(pv[:, t4, :], kvn[:, c, kt * 128:(kt + 1) * 128], wk[s][:, c, 128:256], c == 0,
                                   c == 1, [f"D_wk{s}", "D_kvn"], [psk(b)])

                        def fin():
                            COPY(vv[s][:, 4 * i:4 * i + 4, :], pv, [psk(b)], [f"@D_v{s}"])
                        return fin
                    tl.append(t_v)
                return tl

            def q_tasks(h, qb, qs):
                s = h % 2
                qcols = slice(512 * qb, 512 * (qb + 1))

                def proj(dst, dkey, lo_, hi_, rows, sq, sqkey):
                    def t():
                        b = bank(*SB)
                        for c in range(3):
                            MM(PS[b][0:rows, :], wq[s][:, c, lo_:hi_], qn[:, c, qcols], c == 0, c == 2,
                               [f"D_wq{s}", "D_qn"], [psk(b)])

                        def fin():
                            ACT(dst[:], PS[b][0:rows, :], AF.Identity, [psk(b)], [dkey])
                            if sq is not None:
                                ACT(sq[:], PS[b][0:rows, :], AF.Square, [psk(b)], [sqkey])
                        return fin
                    return t

                def t_tab():
                    DMA("sp", ct[:], ropeC[j, :, qcols], [], ["D_ct0"])
                    DMA("sp", stt_[:], ropeS[j, :, qcols], [], ["D_stt0"])

                def t_ss():
                    b = bank(*SB)
                    MM(PS[b][:], onesbf[:], sqn[:], True, False, ["onesbf", "D_sqn"], [psk(b)])
                    MM(PS[b][:], onesbf[0:64, :], sqr[:], False, True, ["onesbf", "D_sqr"], [psk(b)])

                    def fin():
                        ACT(lnb[:], PS[b][:], AF.Ln, [psk(b)], ["D_ln"], bias=eps_c[:, 0:1], scale=1.0 / 192)
                    return fin

                def t_rs():
                    ACT(rsb[:], lnb[:], AF.Exp, ["D_ln"], ["D_rs"], scale=-0.5)

                def t_d1():
                    E("dve", "scalar_tensor_tensor", ["D_qrn", "D_qkg", "D_rs"], [f"D_QN{qs}"], out=QN[qs][:],
                      in0=qrn[:], scalar=qkg[:, 0:1], in1=rsb[:], op0=ALU.mult, op1=ALU.mult)

                def t_d2():
                    E("dve", "scalar_tensor_tensor", ["D_qrr", "D_qkg", "D_rs"], ["D_qrr"], out=qrr[:],
                      in0=qrr[:], scalar=qkg[0:64, 1:2], in1=rsb[0:64, :], op0=ALU.mult, op1=ALU.mult)

                def t_d3():
                    E("dve", "tensor_mul", ["D_qrr", "D_ct0"], ["D_qrr"], out=qrr[:], in0=qrr[:], in1=ct[:])

                def t_d4():
                    E("dve", "scalar_tensor_tensor", ["D_qrs", "D_qkg", "D_rs"], ["D_qrs"], out=qrs[:],
                      in0=qrs[:], scalar=qkg[0:64, 2:3], in1=rsb[0:64, :], op0=ALU.mult, op1=ALU.mult)

                def t_d5():
                    E("dve", "tensor_mul", ["D_qrs", "D_stt0"], ["D_qrs"], out=qrs[:], in0=qrs[:], in1=stt_[:])

                def t_d6():
                    E("dve", "tensor_tensor", ["D_qrr", "D_qrs"], [f"D_QR{qs}"], out=QR[qs][0:64, :], in0=qrr[:],
                      in1=qrs[:], op=ALU.add)
                return [t_tab, proj(qrn, "D_qrn", 0, 128, 128, sqn, "D_sqn"), proj(qrr, "D_qrr", 128, 192, 64, sqr, "D_sqr"),
                        proj(qrs, "D_qrs", 192, 256, 64, None, None), t_ss, t_rs, t_d1, t_d2, t_d3, t_d4, t_d5, t_d6]

            def epi_tasks(h, qb, qs):
                qcols = slice(512 * qb, 512 * (qb + 1))
                bo = 6 + qs
                pa, pak = pacc[qs], f"D_pacc{qs}"

                def t_e1():
                    b = bank(*SB)
                    MM(PS[b][:], ones32[:], pa[:], True, True, ["ones32", pak], [psk(b)])

                    def fin():
                        ACT(lnr[:], PS[b][:], AF.Ln, [psk(b)], ["D_lnr"])
                    return fin

                def t_e2():
                    ACT(rinv[:], lnr[:], AF.Exp, ["D_lnr"], ["D_rinv"], scale=-1.0)

                def t_e3():
                    E("dve", "tensor_tensor", [psk(bo), "D_rinv"], [f"D_ao{qs}"], out=aost[qs][:], in0=PS[bo][:],
                      in1=rinv[:], op=ALU.mult)
                    DMA("pool", sc["AO"][h, :, qcols], aost[qs][:], [f"D_ao{qs}"], [f"@AO{j}"])
                return [None, t_e1, t_e2, t_e3]

            def pop_bg():
                t = hi.popleft() if hi else (lo.popleft() if lo else None)
                return t() if t is not None else None

            def flash(h, qb, qs):
                s = h % 2
                bo = 6 + qs
                pa, pak = pacc[qs], f"D_pacc{qs}"

                def s_mm(jt):
                    b = bank(*SB)
                    kc = slice(128 * jt, 128 * (jt + 1))
                    MM(PS[b][:], kn[s][:, kc], QN[qs][:], True, False, [f"D_kn{s}", f"D_QN{qs}"], [psk(b)])
                    MM(PS[b][:], krg[:, kc], QR[qs][:], False, True, ["D_krg", f"D_QR{qs}"], [psk(b)])
                    return b

                def acc_pt(jt, p):
                    if jt == 0:
                        E("dve", "tensor_copy", [f"D_PT{p}"], [pak], out=pa[:], in_=PT[p][:])
                    else:
                        E("dve", "tensor_tensor", [f"D_PT{p}", pak], [pak], out=pa[:], in0=pa[:], in1=PT[p][:],
                          op=ALU.add)
                prev_p = None
                pend = [s_mm(t) for t in range(min(LOOK, nkt))]
                for jt in range(nkt):
                    fin = pop_bg()
                    if jt + LOOK < nkt:
                        pend.append(s_mm(jt + LOOK))
                    bcur = pend.pop(0)
                    p = ptc["i"] % NPT
                    ptc["i"] += 1
                    ACT(PT[p][:], PS[bcur][:], AF.Exp, [psk(bcur), f"D_rks{s}"], [f"D_PT{p}"],
                        scale=rks[s][:, jt:jt + 1])
                    MM(PS[bo][:], vv[s][:, jt, :], PT[p][:], jt == 0, jt == nkt - 1, [f"@D_v{s}", f"D_PT{p}"],
                       [psk(bo)], late=(f"D_PT{p}",))
                    if jt >= 1:
                        acc_pt(jt - 1, prev_p)
                    prev_p = p
                    if fin is not None:
                        fin()
                acc_pt(nkt - 1, prev_p)

            def flush(q):
                while q:
                    t = q.popleft()
                    if t is not None:
                        f = t()
                        if f is not None:
                            f()

            def run_all(tl):
                flush(deque(tl))

            blocks = [(h, qb) for h in range(8) for qb in range(8)]
            run_all(kv_tasks(0))
            run_all(q_tasks(0, 0, 0))
            for n, (h, qb) in enumerate(blocks):
                flush(hi)
                if qb == 0:
                    flush(lo)
                    if h + 1 < 8:
                        lo.extend(kv_tasks(h + 1))
                if n >= 1:
                    hp, qp = blocks[n - 1]
                    hi.extend(epi_tasks(hp, qp, (n - 1) % 2))
                if n + 1 < len(blocks):
                    h2, qb2 = blocks[n + 1]
                    if h2 != h:
                        hi.extend(lo)
                        lo.clear()
                    hi.extend(q_tasks(h2, qb2, (n + 1) % 2))
                flash(h, qb, n % 2)
            flush(hi)
            flush(lo)
            run_all(epi_tasks(*blocks[-1], (len(blocks) - 1) % 2))

    def stage_E(j):
        sc = S[j]
        with contextlib.ExitStack() as st:
            def sbt(name, shape, dt=F32):
                return st.enter_context(nc.sbuf_tensor(f"{name}_u{next(uid)}", list(shape), dt))
            xs = [sbt(f"E_xs{i}", [128, 2048], BF16) for i in range(3)]
            btm = [sbt(f"E_btm{i}", [128, 512], BF16) for i in range(3)]
            bt = [sbt(f"E_bt{i}", [128, 4, 128], BF16) for i in range(3)]
            ctt = [sbt(f"E_ct{i}", [128, 4, 128], BF16) for i in range(3)]
            dtl = [sbt(f"E_dt{i}", [128, 64]) for i in range(3)]
            al = [sbt(f"E_a{i}", [128, 64]) for i in range(3)]
            ybl = [sbt(f"E_yb{i}", [128, 2048]) for i in range(3)]
            zl = [sbt(f"E_z{i}", [128, 2048], BF16) for i in range(3)]
            acs = [sbt(f"E_acs{i}", [128, 32]) for i in range(2)]
            dsub = [sbt(f"E_dsub{i}", [128, 32]) for i in range(2)]
            ea = [sbt(f"E_ea{i}", [128, 32]) for i in range(2)]
            eds = [sbt(f"E_eds{i}", [128, 32]) for i in range(2)]
            etot = [sbt(f"E_etot{i}", [128, 32]) for i in range(2)]
            dtw = [sbt(f"E_dtw{i}", [128, 32]) for i in range(2)]
            abf = [sbt(f"E_abf{i}", [128, 32], BF16) for i in range(2)]
            xdt = [sbt(f"E_xdt{i}", [128, 2048], BF16) for i in range(2)]
            xdtw = [sbt(f"E_xdtw{i}", [128, 2048], BF16) for i in range(2)]
            R = [sbt(f"E_R{i}", [128, 32, 128], BF16) for i in range(1)]
            Em = [sbt(f"E_E{i}", [128, 32, 128], BF16) for i in range(1)]
            Mm = [sbt(f"E_M{i}", [128, 32, 128], BF16) for i in range(1)]
            cbm = [sbt(f"E_cbm{i}", [128, 4, 128], BF16) for i in range(1)]
            H = sbt("E_H", [128, 2048])
            Hbf = sbt("E_Hbf", [128, 2048], BF16)
            ytmp = [sbt(f"E_ytmp{i}", [128, 512]) for i in range(2)]
            Y = [sbt(f"E_Y{i}", [128, 2048]) for i in range(2)]
            gain = sbt("E_gain", [128, 2048])
            dsum = sbt("E_dsum", [128, 64])
            gss = sbt("E_gss", [128, 4])
            gln = sbt("E_gln", [128, 4])
            grs = sbt("E_grs", [128, 4])
            junk = sbt("E_junk", [128, 512])
            yn = sbt("E_yn", [128, 2048], BF16)
            ynst = [sbt(f"E_ynst{i}", [128, 16, 128], BF16) for i in range(2)]
            DMA("sp", gain[:], ssm_g[0:1, :].partition_broadcast(128), [], ["E_gain"])
            DMA("sp", dsum[:], dskip[j, 0:1, :].partition_broadcast(128), [], ["E_dsum"])
            E("dve", "tensor_tensor", ["E_dsum"], ["E_dsum"], out=dsum[:, 0:32], in0=dsum[:, 0:32], in1=dsum[:, 32:64],
              op=ALU.add)
            Dmat = sbt("E_Dmat", [128, 32, 128], BF16)
            E("dve", "tensor_tensor", ["E_dsum", "cstbf"], ["E_Dmat"], out=Dmat[:],
              in0=cbf(IDENT).unsqueeze(1).to_broadcast([128, 32, 128]),
              in1=dsum[:, 0:32].unsqueeze(2).to_broadcast([128, 32, 128]), op=ALU.mult)
            cnt = 0
            for sweep in ("b", "f"):
                di = 1 if sweep == "b" else 0
                tri32 = c32(GE if sweep == "b" else LE)
                tribf = cbf(GE if sweep == "b" else LE)
                Ubf = cbf(LT if sweep == "b" else GT)
                order = []
                if sweep == "b":
                    if j == 1:
                        order += [(T + 128 * c, True, c) for c in reversed(range(NCH))]
                    order += [(128 * c, False, c) for c in reversed(range(NCH))]
                else:
                    order += [(128 * c, False, c) for c in range(NCH)]
                E("dve", "memset", [], ["E_H"], H[:], 0.0)
                E("pool", "memset", [], ["E_Hbf"], Hbf[:], 0.0)
                def p1a(k):
                    r0, state_only, c = order[k]
                    s = k % 2
                    l = k % 3
                    rows = slice(r0, r0 + 128)
                    DMA("sp", xs[l][:], sc["xs_tm"][rows, :], [f"@xs_tm{j}"], [f"E_xs{l}"])
                    DMA("sp", btm[l][:], sc["B_tm"][rows, :], [f"@B_tm{j}"], [f"E_btm{l}"])
                    DMA("sp", dtl[l][:], sc["dt"][rows, :], [f"@dt{j}"], [f"E_dt{l}"])
                    DMA("sp", al[l][:], sc["a"][rows, :], [f"@a{j}"], [f"E_a{l}"])
                    if not state_only:
                        DMA("sp", bt[l][:], sc["BT"][:, :, rows].rearrange("g p t -> p g t"), [f"@BT{j}"], [f"E_bt{l}"])
                        DMA("sp", ctt[l][:], sc["CT"][:, :, rows].rearrange("g p t -> p g t"), [f"@CT{j}"],
                            [f"E_ct{l}"])
                        if sweep == "f":
                            DMA("sp", ybl[l][:], sc["yb"][rows, :], [f"@yb{j}"], [f"E_yb{l}"])
                            DMA("sp", zl[l][:], sc["z_tm"][rows, :], [f"@z_tm{j}"], [f"E_z{l}"])
                    a32 = al[l][:, 32 * di:32 * di + 32]
                    dt32 = dtl[l][:, 32 * di:32 * di + 32]
                    bsm = bank(4, 8)
                    MM(PS[bsm][:, 0:32], tri32, a32, True, True, ["cst32", f"E_a{l}"], [psk(bsm)])
                    MM(PS[bsm][:, 32:64], ones32[:], a32, True, True, ["ones32", f"E_a{l}"], [psk(bsm)])
                    E("dve", "tensor_copy", [psk(bsm)], [f"E_acs{s}"], out=acs[s][:], in_=PS[bsm][:, 0:32])
                    E("dve", "tensor_tensor", [psk(bsm), f"E_acs{s}"], [f"E_dsub{s}"], out=dsub[s][:], in0=PS[bsm][:, 32:64],
                      in1=acs[s][:], op=ALU.subtract)
                    ACT(ea[s][:], acs[s][:], AF.Exp, [f"E_acs{s}"], [f"E_ea{s}"])
                    ACT(eds[s][:], dsub[s][:], AF.Exp, [f"E_dsub{s}"], [f"E_eds{s}"])
                    ACT(etot[s][:], PS[bsm][:, 32:64], AF.Exp, [psk(bsm)], [f"E_etot{s}"])
                    E("dve", "tensor_tensor", [f"E_dt{l}", f"E_eds{s}"], [f"E_dtw{s}"], out=dtw[s][:], in0=dt32, in1=eds[s][:],
                      op=ALU.mult)
                    xs3 = xs[l][:].rearrange("p (h d) -> p h d", h=32)
                    E("dve", "tensor_tensor", [f"E_xs{l}", f"E_dtw{s}"], [f"E_xdtw{s}"],
                      out=xdtw[s][:].rearrange("p (h d) -> p h d", h=32), in0=xs3,
                      in1=dtw[s][:].unsqueeze(2).to_broadcast([128, 32, 64]), op=ALU.mult)
                    if state_only:
                        return
                    E("dve", "tensor_tensor", [f"E_xs{l}", f"E_dt{l}"], [f"E_xdt{s}"],
                      out=xdt[s][:].rearrange("p (h d) -> p h d", h=32), in0=xs3,
                      in1=dt32.unsqueeze(2).to_broadcast([128, 32, 64]), op=ALU.mult)
                    for h in range(32):
                        ACT(R[0][:, h, :], tribf, AF.Identity, ["cstbf", f"E_a{l}"], ["E_R0"], scale=a32[:, h:h + 1])
                    bcb = bank(4, 8)
                    pcb = PS[bcb][:].rearrange("p (g t) -> p g t", g=4)
                    for g in range(4):
                        MM(pcb[:, g, :], bt[l][:, g, :], ctt[l][:, g, :], True, True, [f"E_bt{l}", f"E_ct{l}"],
                           [psk(bcb)])
                    E("dve", "tensor_tensor", [psk(bcb), "cstbf"], ["E_cbm0"], out=cbm[0][:], in0=pcb,
                      in1=tribf.unsqueeze(1).to_broadcast([128, 4, 128]), op=ALU.mult)
                def p1b(k):
                    r0, state_only, c = order[k]
                    if state_only:
                        return
                    s = k % 2
                    l = k % 3
                    for q in range(8):
                        bd = bank(0, 4)
                        MM(PS[bd][:], Ubf, R[0][:, 4 * q:4 * q + 4, :].rearrange("p h t -> p (h t)"), True, True,
                           ["cstbf", "E_R0"], [psk(bd)])
                        ACT(Em[0][:, 4 * q:4 * q + 4, :].rearrange("p h t -> p (h t)"), PS[bd][:], AF.Exp, [psk(bd)],
                            ["E_E0"])
                    for g in range(4):
                        E("dve", "tensor_tensor", ["E_E0", "E_cbm0"], ["E_M0"], out=Mm[0][:, 8 * g:8 * g + 8, :],
                          in0=Em[0][:, 8 * g:8 * g + 8, :],
                          in1=cbm[0][:, g, :].unsqueeze(1).to_broadcast([128, 8, 128]), op=ALU.mult)

                def p2(k):
                    r0, state_only, c = order[k]
                    s = k % 2
                    l = k % 3
                    ys = s
                    rows = slice(r0, r0 + 128)
                    xs3 = xs[l][:].rearrange("p (h d) -> p h d", h=32)
                    if not state_only:
                        for g in range(4):
                            byd, byo = bank(0, 4), bank(0, 4)
                            for hh in range(8):
                                h = 8 * g + hh
                                MM(PS[byd][:, 64 * hh:64 * hh + 64], Mm[0][:, h, :], xdt[s][:, 64 * h:64 * h + 64], True,
                                   sweep == "b", ["E_M0", f"E_xdt{s}"], [psk(byd)])
                                if sweep == "f":
                                    MM(PS[byd][:, 64 * hh:64 * hh + 64], Dmat[:, h, :], xs[l][:, 64 * h:64 * h + 64], False,
                                       True, ["E_Dmat", f"E_xs{l}"], [psk(byd)])
                            MM(PS[byo][:], ctt[l][:, g, :], Hbf[:, 512 * g:512 * g + 512], True, True,
                               [f"E_ct{l}", "E_Hbf"], [psk(byo)])
                            E("dve", "tensor_tensor", [psk(byo), f"E_ea{s}"], [f"E_ytmp{s}"],
                              out=ytmp[s][:].rearrange("p (h d) -> p h d", h=8),
                              in0=PS[byo][:].rearrange("p (h d) -> p h d", h=8),
                              in1=ea[s][:, 8 * g:8 * g + 8].unsqueeze(2).to_broadcast([128, 8, 64]), op=ALU.mult)
                            E("dve", "tensor_tensor", [psk(byd), f"E_ytmp{s}"], [f"E_Y{ys}"],
                              out=Y[ys][:, 512 * g:512 * g + 512], in0=PS[byd][:], in1=ytmp[s][:], op=ALU.add)
                    for g in range(4):
                        bst = bank(4, 8)
                        MM(PS[bst][:], btm[l][:, 128 * g:128 * g + 128], xdtw[s][:, 512 * g:512 * g + 512], True, True,
                           [f"E_btm{l}", f"E_xdtw{s}"], [psk(bst)])
                        hv = H[:, 512 * g:512 * g + 512]
                        E("dve", "tensor_tensor", ["E_H", f"E_etot{s}"], ["E_H"],
                          out=hv.rearrange("p (h d) -> p h d", h=8), in0=hv.rearrange("p (h d) -> p h d", h=8),
                          in1=etot[s][:, 8 * g:8 * g + 8].unsqueeze(2).to_broadcast([128, 8, 64]), op=ALU.mult)
                        E("dve", "tensor_tensor", ["E_H", psk(bst)], ["E_H"], out=hv, in0=hv, in1=PS[bst][:],
                          op=ALU.add)
                    ACT(Hbf[:], H[:], AF.Copy, ["E_H"], ["E_Hbf"])
                    if state_only:
                        return
                    if sweep == "b":
                        DMA("pool", sc["yb"][rows, :], Y[ys][:], [f"E_Y{ys}"], [f"@yb{j}"])
                        return
                    yv = Y[ys]
                    yk = f"E_Y{ys}"
                    E("dve", "tensor_tensor", [yk, f"E_yb{l}"], [yk], out=yv[:], in0=yv[:], in1=ybl[l][:], op=ALU.add)
                    E("dve", "tensor_tensor", [yk, f"E_z{l}"], [yk], out=yv[:], in0=yv[:], in1=zl[l][:], op=ALU.mult)

                def p2b(k):
                    r0, state_only, c = order[k]
                    if state_only or sweep == "b":
                        return
                    s = k % 2
                    ys = s
                    rows = slice(r0, r0 + 128)
                    yv = Y[ys]
                    yk = f"E_Y{ys}"
                    for g in range(4):
                        ACT(junk[:], yv[:, 512 * g:512 * g + 512], AF.Square, [yk], ["E_junk", "E_gss"],
                            accum_out=gss[:, g:g + 1])
                    ACT(gln[:], gss[:], AF.Ln, ["E_gss"], ["E_gln"], bias=eps_c[:, 0:1], scale=1.0 / 512)
                    ACT(grs[:], gln[:], AF.Exp, ["E_gln"], ["E_grs"], scale=-0.5)
                    for g in range(4):
                        E("dve", "scalar_tensor_tensor", [yk, "E_grs", "E_gain"], ["E_yn"],
                          out=yn[:, 512 * g:512 * g + 512], in0=yv[:, 512 * g:512 * g + 512], scalar=grs[:, g:g + 1],
                          in1=gain[:, 512 * g:512 * g + 512], op0=ALU.mult, op1=ALU.mult)
                    ns = c % 2
                    for half in range(2):
                        b = bank(4, 8)
                        pv = PS[b][:].bitcast(BF16).rearrange("p (k t) -> p k t", k=8)
                        for k8 in range(8):
                            cc = 8 * half + k8
                            TR(pv[:, k8, :], yn[:, 128 * cc:128 * cc + 128], cbf(IDENT), ["E_yn", "cstbf"], [psk(b)])
                        COPY(ynst[ns][:, 8 * half:8 * half + 8, :], pv, [psk(b)], [f"E_ynst{ns}"], eng="act")
                    DMA("pool", sc["ynT"][:, :, rows].rearrange("c p t -> p c t"), ynst[ns][:], [f"E_ynst{ns}"],
                        [f"@ynT{j}"])

                p1a(0)
                p1b(0)
                for k in range(len(order)):
                    if k + 1 < len(order):
                        p1a(k + 1)
                    p2(k)
                    if k + 1 < len(order):
                        p1b(k + 1)
                    p2b(k)

    def stage_F(j):
        sc = S[j]
        with contextlib.ExitStack() as st:
            def sbt(name, shape, dt=F32):
                return st.enter_context(nc.sbuf_tensor(f"{name}_u{next(uid)}", list(shape), dt))
            w32s = [sbt(f"F_w32{i}", [128, 8, 512]) for i in range(2)]
            fcnt = {"i": 0}
            wpa = sbt("F_wpa", [128, 8, D], BF16)
            wpb = sbt("F_wpb", [128, 16, D], BF16)
            wo = sbt("F_wo", [128, 8, D], BF16)
            ao = sbt("F_ao", [128, 8, 512], BF16)
            sga = sbt("F_sga", [128, 8, 512], BF16)
            ynt = sbt("F_ynt", [128, 16, 512], BF16)
            sgg = sbt("F_sgg", [128, 16, 512], BF16)
            t1 = sbt("F_t1", [128, 512])
            t2 = sbt("F_t2", [128, 512])
            mg = sbt("F_mg", [128, 8, 512], BF16)
            xt = [sbt(f"F_xt{i}", [128, D]) for i in range(2)]
            yo = [sbt(f"F_yo{i}", [128, D]) for i in range(2)]
            for (src, dst, nk, key) in ((w_pa, wpa, 8, "F_wpa"), (w_pb, wpb, 16, "F_wpb"), (w_o, wo, 8, "F_wo")):
                sv = src.rearrange("(kc p) n -> p kc n", p=128)
                for k0 in range(0, nk, 8):
                    for nb in range(2):
                        wsl = fcnt["i"] % 2
                        fcnt["i"] += 1
                        DMA("sp", w32s[wsl][:], sv[:, k0:k0 + 8, 512 * nb:512 * nb + 512], [], [f"F_w32{wsl}"])
                        COPY(dst[:, k0:k0 + 8, 512 * nb:512 * nb + 512], w32s[wsl][:], [f"F_w32{wsl}"], ["@" + key])
            xcnt = 0
            for i in range(8):
                cols = slice(512 * i, 512 * i + 512)
                DMA("sp", ao[:], sc["AO"][:, :, cols].rearrange("h p t -> p h t"), [f"@AO{j}"], ["F_ao"])
                DMA("sp", sga[:], sc["sga"][:, :, cols].rearrange("h p t -> p h t"), [f"@sga{j}"], ["F_sga"])
                DMA("sp", ynt[:], sc["ynT"][:, :, cols].rearrange("h p t -> p h t"), [f"@ynT{j}"], ["F_ynt"])
                DMA("sp", sgg[:], sc["sig"][:, :, cols].rearrange("h p t -> p h t"), [f"@sig{j}"], ["F_sgg"])
                E("dve", "tensor_tensor", ["F_ao", "F_sga"], ["F_ao"], out=ao[:], in0=ao[:], in1=sga[:], op=ALU.mult)
                for dc in range(8):
                    ba, bb_ = bank(), bank()
                    for k in range(8):
                        MM(PS[ba][:], wpa[:, k, 128 * dc:128 * dc + 128], ao[:, k, :], k == 0, k == 7, ["@F_wpa", "F_ao"],
                           [psk(ba)])
                    for k in range(16):
                        MM(PS[bb_][:], wpb[:, k, 128 * dc:128 * dc + 128], ynt[:, k, :], k == 0, k == 15,
                           ["@F_wpb", "F_ynt"], [psk(bb_)])
                    E("dve", "tensor_tensor", [psk(ba), "F_sgg"], ["F_t1"], out=t1[:], in0=PS[ba][:], in1=sgg[:, dc, :],
                      op=ALU.mult)
                    E("dve", "tensor_tensor", [psk(bb_), "F_sgg"], ["F_t2"], out=t2[:], in0=PS[bb_][:],
                      in1=sgg[:, 8 + dc, :], op=ALU.mult)
                    E("dve", "tensor_tensor", ["F_t1", "F_t2"], ["F_mg"], out=mg[:, dc, :], in0=t1[:], in1=t2[:],
                      op=ALU.add)
                for tt in range(4):
                    xsl = xcnt % 2
                    xcnt += 1
                    r0 = 512 * i + 128 * tt
                    DMA("sp", xt[xsl][:], x_in[j, r0:r0 + 128, :], [], [f"F_xt{xsl}"])
                    for nb in range(2):
                        b = bank()
                        for k in range(8):
                            MM(PS[b][:], mg[:, k, 128 * tt:128 * tt + 128], wo[:, k, 512 * nb:512 * nb + 512], k == 0,
                               k == 7, ["F_mg", "@F_wo"], [psk(b)])
                        ysl = yo[xsl][:, 512 * nb:512 * nb + 512]
                        E("dve", "tensor_tensor", [psk(b), f"gateb{j}"], [f"F_yo{xsl}"], out=ysl, in0=PS[b][:],
                          in1=gateb[j][:, 512 * nb:512 * nb + 512], op=ALU.mult)
                        E("dve", "tensor_tensor", [f"F_yo{xsl}", f"F_xt{xsl}"], [f"F_yo{xsl}"], out=ysl, in0=ysl,
                          in1=xt[xsl][:, 512 * nb:512 * nb + 512], op=ALU.add)
                    DMA("pool", y_out[j, r0:r0 + 128, :], yo[xsl][:], [f"F_yo{xsl}"], ["@y_out"])

    with es:
        for j in range(2):
            with contextlib.ExitStack() as jst:
                ssqr = jst.enter_context(nc.sbuf_tensor(f"ssqr{j}", [128, 64], F32))
                gmod[j] = jst.enter_context(nc.sbuf_tensor(f"gmod{j}", [128, D], F32))
                shiftb[j] = jst.enter_context(nc.sbuf_tensor(f"shiftb{j}", [128, D], F32))
                gateb[j] = jst.enter_context(nc.sbuf_tensor(f"gateb{j}", [128, D], F32))
                if run("A"):
                    stage_A(j)
                passes = ([True] if j == 1 else []) + [False]
                for other in passes:
                    with contextlib.ExitStack() as pst:
                        hT = pst.enter_context(nc.sbuf_tensor(f"hT{j}{int(other)}", [128, 8, T + 4], BF16))
                        if run("B"):
                            xi = 2 if other else j
                            srcs = [(xi, 128 * t, 128, 2 + 128 * t) for t in range(NCH)]
                            if other:
                                srcs.append((1, T - 2, 2, 0))
                                E("dve", "memset", [], ["@hT"], hT[:, :, T + 2:T + 4], 0.0)
                            else:
                                E("dve", "memset", [], ["@hT"], hT[:, :, 0:2], 0.0)
                                if j == 1:
                                    srcs.append((2, 0, 2, T + 2))
                                else:
                                    E("dve", "memset", [], ["@hT"], hT[:, :, T + 2:T + 4], 0.0)
                            stage_B(j, hT, srcs)
                            P.barrier()
                        if run("C"):
                            stage_C_lat(j, hT, other, ssqr)
                            P.barrier()
                            stage_C_dt(j, hT, other)
                            P.barrier()
                            if not other:
                                stage_C_z(j, hT)
                                P.barrier()
                            stage_C_fm(j, hT, other)
                    P.barrier()
                if run("D"):
                    stage_D(j, ssqr)
                    P.barrier()
                if run("E"):
                    stage_E(j)
                    P.barrier()
                if run("F"):
                    stage_F(j)
                    P.barrier()
        P.barrier()
        P.op("sp", None, ["@y_out"], [])
        P.emit()
    return nc


def _rope_tables(pos):
    half = 32
    inv_freq = np.exp((-np.log(np.float32(10000.0)) * np.arange(half, dtype=np.float32) / np.float32(half)).astype(np.float32)).astype(np.float32)
    ang = (pos.astype(np.float32)[:, None] * inv_freq[None, :]).astype(np.float32)
    c, s = np.cos(ang).astype(np.float32), np.sin(ang).astype(np.float32)
    C = np.concatenate([c, c], axis=1).T
    Ssg = np.concatenate([-s, s], axis=1).T
    return np.ascontiguousarray(C), np.ascontiguousarray(Ssg)


def make_in_maps(x_prompt, x_sample, c_prompt, c_sample, norm_g, w_ada, b_ada, w_in, q_a_norm, w_q_up,
                 kv_a_norm, w_kv_up, q_norm, k_norm, w_proj_a, conv_w, conv_b, dt_bias_f, dt_bias_b,
                 a_log_f, a_log_b, d_f, d_b, ssm_norm, w_proj_b, w_out):
    f = lambda a: np.ascontiguousarray(np.asarray(a, dtype=np.float32))
    w_in0 = f(w_in[0])
    sw = np.concatenate([np.arange(32, 64), np.arange(0, 32)])
    wq = f(w_q_up[0]).reshape(384, 8, 192)
    wkv = f(w_kv_up[0]).reshape(256, 8, 256)
    qn_, kn_ = f(q_norm[0]), f(k_norm[0])
    qk_g = np.zeros((128, 6), np.float32)
    qk_g[:, 0] = qn_[:128]
    qk_g[:64, 1] = qn_[128:]
    qk_g[:64, 2] = qn_[128:][sw]
    qk_g[:, 3] = kn_[:128]
    qk_g[:64, 4] = kn_[128:]
    qk_g[:64, 5] = kn_[128:][sw]
    pidx = np.arange(128)[:, None]
    fidx = np.arange(128)[None, :]
    consts = np.concatenate([(pidx == fidx), (pidx <= fidx), (pidx >= fidx), (pidx > fidx), (pidx < fidx)],
                            axis=1).astype(np.float32)
    cw = f(conv_w[0])
    common = dict(
        w_ada=f(w_ada[0]), b_ada=f(b_ada), norm_g=f(norm_g), w_in=w_in0,
        w_krs=f(w_in0[:, C_KR:C_KR + 64][:, sw]),
        w_qn=f(wq[:, :, :128].transpose(1, 0, 2)), w_qr=f(wq[:, :, 128:].transpose(1, 0, 2)),
        w_qs=f(wq[:, :, 128:][:, :, sw].transpose(1, 0, 2)),
        w_kn=f(wkv[:, :, :128].transpose(1, 0, 2)), w_v=f(wkv[:, :, 128:].transpose(1, 0, 2)),
        gq_a=f(f(q_a_norm[0]).reshape(3, 128).T), gkv_a=f(f(kv_a_norm[0]).reshape(2, 128).T), qk_g=qk_g,
        conv_bc=f(f(conv_b[0]).reshape(24, 128).T), ssm_g=f(ssm_norm), w_pa=f(w_proj_a[0]), w_pb=f(w_proj_b[0]),
        w_o=f(w_out[0]), consts=consts,
    )
    wdt_f, wdt_b = w_in0[:, C_DTF:C_DTF + 32], w_in0[:, C_DTB:C_DTB + 32]
    fwd = dict(w_dt=np.concatenate([wdt_f, wdt_b], 1), dtb=np.concatenate([f(dt_bias_f[0]), f(dt_bias_b[0])]),
               alog=np.concatenate([f(a_log_f[0]), f(a_log_b[0])]), dskip=np.concatenate([f(d_f[0]), f(d_b[0])]),
               cw=cw)
    rev = dict(w_dt=np.concatenate([wdt_b, wdt_f], 1), dtb=np.concatenate([f(dt_bias_b[0]), f(dt_bias_f[0])]),
               alog=np.concatenate([f(a_log_b[0]), f(a_log_f[0])]), dskip=np.concatenate([f(d_b[0]), f(d_f[0])]),
               cw=cw[::-1])
    xp, xs_, cp, cs = f(x_prompt), f(x_sample), f(c_prompt), f(c_sample)
    Cp, Sp = _rope_tables(np.arange(8192))
    Cr, Sr = _rope_tables(np.arange(8191, -1, -1))
    in_maps = []
    for i in range(8):
        b, half = i // 2, i % 2
        if half == 0:
            own, oth, o1 = xs_[b, :T], xs_[b, T:], fwd
            C1, S1 = Cp, Sp
        else:
            own, oth, o1 = xs_[b, :T - 1:-1], xs_[b, T - 1::-1], rev
            C1, S1 = Cr, Sr
        jobs = (fwd, o1)
        m = dict(common)
        m["x"] = np.ascontiguousarray(np.stack([xp[i], own, oth]))
        m["cT"] = np.ascontiguousarray(np.stack([cp[i].reshape(8, 128).T, cs[b].reshape(8, 128).T]))
        m["w_dt"] = np.ascontiguousarray(np.stack([jb["w_dt"] for jb in jobs]))
        m["dtb"] = np.ascontiguousarray(np.stack([jb["dtb"][None] for jb in jobs]))
        m["alog"] = np.ascontiguousarray(np.stack([jb["alog"][None] for jb in jobs]))
        m["dskip"] = np.ascontiguousarray(np.stack([jb["dskip"][None] for jb in jobs]))
        m["conv_wc"] = np.ascontiguousarray(np.stack(
            [jb["cw"].reshape(5, 24, 128).transpose(2, 1, 0).reshape(128, 120) for jb in jobs]))
        m["ropeC"] = np.ascontiguousarray(np.stack([Cp, C1]))
        m["ropeS"] = np.ascontiguousarray(np.stack([Sp, S1]))
        in_maps.append(m)
    return in_maps


_NC_CACHE = {}


def kernel(**inputs):
    in_maps = make_in_maps(**inputs)
    if "nc" not in _NC_CACHE:
        _NC_CACHE["nc"] = build_program()
    res = run_bass_kernel_spmd(_NC_CACHE["nc"], in_maps, core_ids=list(range(8)))
    y_prompt = np.empty((8, T, D), np.float32)
    y_sample = np.empty((4, 2 * T, D), np.float32)
    for i in range(8):
        y = res.results[i]["y"]
        y_prompt[i] = y[0]
        b, half = i // 2, i % 2
        if half == 0:
            y_sample[b, :T] = y[1]
        else:
            y_sample[b, T:] = y[1][::-1]
    return (y_prompt, y_sample)
```

```python
import contextlib
import numpy as np
import concourse.bass as bass
import concourse.mybir as mybir
from concourse.bass_utils import run_bass_kernel_spmd

F32 = mybir.dt.float32
BF16 = mybir.dt.bfloat16
AF = mybir.ActivationFunctionType
ALU = mybir.AluOpType

D = 1024
T = 4096
NCH = 32
EPS = 1e-6
N_IN = 8960
C_Q, C_KV, C_KR, C_GA, C_Z, C_X, C_B, C_C, C_DTF, C_DTB, C_GGA, C_GGB = (
    0, 384, 640, 704, 1728, 3776, 5824, 6336, 6848, 6880, 6912, 7936)
SM_SCALE = 192 ** -0.5


class _Op:
    __slots__ = ("eng", "fn", "reads", "writes", "dma", "deps", "sig", "late", "early")

    def __init__(self, eng, fn, reads, writes, dma, late=()):
        self.eng, self.fn, self.reads, self.writes, self.dma = eng, fn, reads, writes, dma
        self.deps = set()
        self.sig = None
        self.late = frozenset(late)
        self.early = set()


class Prog:
    EPOCH = 24000
    NDMA = 16

    def __init__(self, nc, es):
        self.nc, self.es = nc, es
        self.ops = []
        self.h = {"pe": nc.tensor, "act": nc.scalar, "dve": nc.vector, "pool": nc.gpsimd, "sp": nc.sync}
        self.barriers = []

    def op(self, eng, fn, reads=(), writes=(), dma=False, late=()):
        self.ops.append(_Op(eng, fn, tuple(reads), tuple(writes), dma, late))

    def barrier(self):
        self.barriers.append(len(self.ops))

    def emit(self):
        nc, ops = self.nc, self.ops
        lastw, readers = {}, {}
        bset = set(self.barriers)
        last_eng = {}
        dma_since = []
        pend_bar = {}
        dma_hist = {}
        for i, o in enumerate(ops):
            if i in bset:
                deps = set(last_eng.values()) | set(dma_since)
                for e in self.h:
                    pend_bar[e] = pend_bar.get(e, set()) | deps
                dma_since = []
                lastw, readers = {}, {}
            if o.eng in pend_bar:
                o.deps |= pend_bar.pop(o.eng)
                o.early |= o.deps
            raw = set()
            for r in o.reads:
                lw = lastw.get(r)
                if lw is not None:
                    ds = lw if isinstance(lw, set) else {lw}
                    raw |= ds
                    if r not in o.late:
                        o.early |= ds
            o.deps |= raw
            for w in o.writes:
                ds = set(readers.get(w, ()))
                lw = lastw.get(w)
                if lw is not None and not isinstance(lw, set):
                    ds.add(lw)
                o.deps |= ds
                if w not in o.late:
                    o.early |= ds
            for r in o.reads:
                readers.setdefault(r, []).append(i)
            for w in o.writes:
                if w.startswith("@"):
                    lastw.setdefault(w, set()).add(i)
                else:
                    lastw[w] = i
                    readers[w] = []
            if o.dma:
                hist = dma_hist.setdefault(o.eng, [])
                if len(hist) >= self.NDMA:
                    o.deps.add(hist[-self.NDMA])
                    o.early.add(hist[-self.NDMA])
                hist.append(i)
                dma_since.append(i)
            else:
                last_eng[o.eng] = i
            o.deps.discard(i)
            o.deps = {d for d in o.deps if ops[d].dma or ops[d].eng != o.eng
                      or (d in raw and o.eng != "pe")}
        needed = set()
        for o in ops:
            needed |= o.deps
        ccount = {}
        csems = {}
        dcount = {}
        dsems = {}
        for i, o in enumerate(ops):
            if o.dma:
                n = dcount.get(o.eng, 0)
                dcount[o.eng] = n + 1
                pool = dsems.setdefault(o.eng, [])
                if len(pool) < self.NDMA:
                    pool.append(self.es.enter_context(nc.semaphore(f"d_{o.eng}_{len(pool)}")))
                o.sig = (pool[n % self.NDMA], 16 * (n // self.NDMA + 1))
            elif i in needed:
                n = ccount.get(o.eng, 0)
                ccount[o.eng] = n + 1
                lst = csems.setdefault(o.eng, [])
                if n // self.EPOCH >= len(lst):
                    lst.append(self.es.enter_context(nc.semaphore(f"c_{o.eng}_{len(lst)}")))
                o.sig = (lst[n // self.EPOCH], n % self.EPOCH + 1)
        waited = {e: {} for e in self.h}
        for i, o in enumerate(ops):
            h = self.h[o.eng]
            wl = {}
            early_k = set()
            for d in o.deps:
                sem, val = ops[d].sig
                k = id(sem)
                if d in o.early or not o.late:
                    early_k.add(k)
                if waited[o.eng].get(k, 0) < val and wl.get(k, (None, 0))[1] < val:
                    wl[k] = (sem, val)
            attach = None
            for k, (sem, val) in wl.items():
                if k not in early_k and attach is None:
                    attach = (sem, val)
                    continue
                h.wait_ge(sem, val)
                waited[o.eng][k] = val
            if o.fn is None:
                continue
            ins = o.fn()
            if attach is not None:
                ins._wait_ge(attach[0], attach[1])
                waited[o.eng][id(attach[0])] = attach[1]
            if o.sig is not None:
                ins.then_inc(o.sig[0], 16 if o.dma else 1)


def build_program(debug=False, stages=None):
    nc = bass.Bass("TRN2", target_bir_lowering=False)
    es = contextlib.ExitStack()
    P = Prog(nc, es)
    run = (lambda s: True) if stages is None else (lambda s: s in stages)

    def din(name, shape, dt=F32):
        return nc.dram_tensor(name, list(shape), dt, kind="ExternalInput").ap()

    def dscr(name, shape, dt):
        kind = "ExternalOutput" if (debug is True or (debug and name in debug)) else "Internal"
        return nc.dram_tensor(name, list(shape), dt, kind=kind).ap()

    x_in = din("x", [3, T, D])
    cT_in = din("cT", [2, 128, 8])
    w_ada = din("w_ada", [D, 3 * D])
    b_ada = din("b_ada", [1, 3 * D])
    norm_g = din("norm_g", [1, D])
    w_in = din("w_in", [D, N_IN])
    w_krs = din("w_krs", [D, 64])
    w_dt = din("w_dt", [2, D, 64])
    w_qn = din("w_qn", [8, 384, 128])
    w_qr = din("w_qr", [8, 384, 64])
    w_qs = din("w_qs", [8, 384, 64])
    w_kn = din("w_kn", [8, 256, 128])
    w_v = din("w_v", [8, 256, 128])
    gq_a = din("gq_a", [128, 3])
    gkv_a = din("gkv_a", [128, 2])
    qk_g = din("qk_g", [128, 6])
    conv_wc = din("conv_wc", [2, 128, 24 * 5])
    conv_bc = din("conv_bc", [128, 24])
    dtb = din("dtb", [2, 1, 64])
    alog = din("alog", [2, 1, 64])
    dskip = din("dskip", [2, 1, 64])
    ssm_g = din("ssm_g", [1, 2048])
    w_pa = din("w_pa", [D, D])
    w_pb = din("w_pb", [2048, D])
    w_o = din("w_o", [D, D])
    ropeC = din("ropeC", [2, 64, 8192])
    ropeS = din("ropeS", [2, 64, 8192])
    consts = din("consts", [128, 5 * 128])
    y_out = nc.dram_tensor("y", [2, T, D], F32, kind="ExternalOutput").ap()

    S = []
    for j in range(2):
        tk = T * (1 + j)
        S.append(dict(
            qnT=dscr(f"qnT{j}", [3, 128, T], BF16), kvnT=dscr(f"kvnT{j}", [2, 128, tk], BF16),
            KRg=dscr(f"KRg{j}", [64, tk], BF16),
            sga=dscr(f"sga{j}", [8, 128, T], BF16), sig=dscr(f"sig{j}", [16, 128, T], BF16),
            xs_tm=dscr(f"xstm{j}", [tk, 2048], BF16), B_tm=dscr(f"Btm{j}", [tk, 512], BF16),
            BT=dscr(f"BT{j}", [4, 128, T], BF16), CT=dscr(f"CT{j}", [4, 128, T], BF16),
            z_tm=dscr(f"ztm{j}", [T, 2048], BF16),
            dt=dscr(f"dt{j}", [tk, 64], F32), a=dscr(f"a{j}", [tk, 64], F32),
            AO=dscr(f"AO{j}", [8, 128, T], BF16), yb=dscr(f"yb{j}", [T, 2048], F32),
            ynT=dscr(f"ynT{j}", [16, 128, T], BF16),
        ))

    import itertools
    uid = itertools.count()

    def sb(name, shape, dt=F32):
        return es.enter_context(nc.sbuf_tensor(name, list(shape), dt))

    PS = [es.enter_context(nc.psum_tensor(f"ps{i}", [128, 512], F32)) for i in range(8)]
    rot = {"i": 0}

    def bank(lo=0, hi=8):
        n = hi - lo
        b = lo + rot.setdefault((lo, hi), 0) % n
        rot[(lo, hi)] += 1
        return b

    def psk(b):
        return f"ps{b}"

    def E(eng, meth, reads, writes, *a, **kw):
        h = P.h[eng]
        P.op(eng, lambda: getattr(h, meth)(*a, **kw), reads, writes)

    def DMA(eng, out, in_, reads, writes, slow=False):
        h = P.h[eng]
        if slow:
            P.op(eng, lambda: h.dma_start(out=out, in_=in_, allow_slow_non_contiguous=True), reads, writes, dma=True)
        else:
            P.op(eng, lambda: h.dma_start(out=out, in_=in_), reads, writes, dma=True)

    def MM(out, lhsT, rhs, start, stop, reads, writes, late=()):
        P.op("pe", lambda: nc.tensor.matmul(out, lhsT=lhsT, rhs=rhs, start=start, stop=stop), reads, writes, late=late)

    def TR(out, in_, ident, reads, writes):
        P.op("pe", lambda: nc.tensor.transpose(out, in_, ident), reads, writes)

    def ACT(out, in_, func, reads, writes, **kw):
        P.op("act", lambda: nc.scalar.activation(out=out, in_=in_, func=func, **kw), reads, writes)

    cp_flip = {"i": 0}

    def COPY(out, in_, reads, writes, eng=None):
        if eng is None:
            eng = ("act", "dve")[cp_flip["i"] % 2]
            cp_flip["i"] += 1
        if eng == "act":
            ACT(out, in_, AF.Copy, reads, writes)
        else:
            E(eng, "tensor_copy", reads, writes, out=out, in_=in_)

    def rstd_act(out, in_, scale, reads, writes, tmp, tmpk):
        ACT(tmp, in_, AF.Ln, reads, [tmpk], bias=eps_c[0:in_.shape[0], 0:1], scale=scale)
        ACT(out, tmp, AF.Exp, [tmpk], writes, scale=-0.5)

    cst32 = sb("cst32", [128, 5 * 128])
    cstbf = sb("cstbf", [128, 5 * 128], BF16)
    ones32 = sb("ones32", [128, 128])
    onesbf = sb("onesbf", [128, 128], BF16)
    eps_c = sb("eps_c", [128, 1])
    DMA("sp", cst32[:], consts[:, :], [], ["cst32"])
    E("dve", "tensor_copy", ["cst32"], ["cstbf"], out=cstbf[:], in_=cst32[:])
    E("dve", "memset", [], ["ones32"], ones32[:], 1.0)
    E("dve", "memset", [], ["onesbf"], onesbf[:], 1.0)
    E("dve", "memset", [], ["eps_c"], eps_c[:], EPS)
    IDENT, LE, GE, GT, LT = range(5)

    def c32(i):
        return cst32[:, i * 128:(i + 1) * 128]

    def cbf(i):
        return cstbf[:, i * 128:(i + 1) * 128]

    gmod, shiftb, gateb = {}, {}, {}

    def stage_A(j):
        with contextlib.ExitStack() as st:
            def sbt(name, shape, dt=F32):
                return st.enter_context(nc.sbuf_tensor(f"{name}_u{next(uid)}", list(shape), dt))
            ct = sbt("A_ct", [128, 8])
            sg = sbt("A_sg", [128, 8])
            csb = sbt("A_csb", [128, 8, 128])
            wbuf = [sbt(f"A_w{i}", [128, 8, 512]) for i in range(2)]
            bb = sbt("A_bb", [128, 3 * D])
            ngb = sbt("A_ngb", [128, D])
            DMA("sp", ct[:], cT_in[j], [], ["A_ct"])
            DMA("sp", bb[:], b_ada[0:1, :].partition_broadcast(128), [], ["A_bb"])
            DMA("sp", ngb[:], norm_g[0:1, :].partition_broadcast(128), [], ["A_ngb"])
            ACT(sg[:], ct[:], AF.Exp, ["A_ct"], ["A_sg"], scale=-1.0)
            E("dve", "tensor_scalar_add", ["A_sg"], ["A_sg"], out=sg[:], in0=sg[:], scalar1=1.0)
            E("dve", "reciprocal", ["A_sg"], ["A_sg"], out=sg[:], in_=sg[:])
            E("dve", "tensor_mul", ["A_sg", "A_ct"], ["A_sg"], out=sg[:], in0=sg[:], in1=ct[:])
            E("dve", "tensor_copy", ["A_sg"], ["A_csb"], out=csb[:],
              in_=sg[:].unsqueeze(2).to_broadcast([128, 8, 128]))
            wv = w_ada.rearrange("(kc p) n -> p kc n", p=128)
            for nb in range(6):
                wb = wbuf[nb % 2]
                wk = f"A_w{nb % 2}"
                DMA("sp", wb[:], wv[:, :, nb * 512:(nb + 1) * 512], [], [wk])
                b = bank()
                for k in range(8):
                    MM(PS[b][:], csb[:, k, :], wb[:, k, :], k == 0, k == 7, [wk, "A_csb"], [psk(b)])
                sec, off = nb // 2, (nb % 2) * 512
                dst = (shiftb[j], gmod[j], gateb[j])[sec]
                dk = (f"shiftb{j}", f"gmod{j}", f"gateb{j}")[sec]
                E("dve", "tensor_tensor", [psk(b), "A_bb"], [dk], out=dst[:, off:off + 512], in0=PS[b][:],
                  in1=bb[:, nb * 512:(nb + 1) * 512], op=ALU.add)
            E("dve", "scalar_tensor_tensor", [f"gmod{j}", "A_ngb"], [f"gmod{j}"], out=gmod[j][:], in0=gmod[j][:],
              scalar=1.0, in1=ngb[:], op0=ALU.add, op1=ALU.mult)
        P.barrier()

    def stage_B(j, hT, srcs):
        with contextlib.ExitStack() as st:
            def sbt(name, shape, dt=F32):
                return st.enter_context(nc.sbuf_tensor(f"{name}_u{next(uid)}", list(shape), dt))
            xt = [sbt(f"B_xt{i}", [128, D]) for i in range(2)]
            junk = sbt("B_junk", [128, D])
            htmp = sbt("B_htmp", [128, D])
            hb = [sbt(f"B_hb{i}", [128, D], BF16) for i in range(2)]
            ssq = [sbt(f"B_ssq{i}", [128, 1]) for i in range(2)]
            lnv = [sbt(f"B_ln{i}", [128, 1]) for i in range(2)]
            rs = [sbt(f"B_rs{i}", [128, 1]) for i in range(2)]
            pend = None
            for n, (xi, r0, nr, c0) in enumerate(srcs):
                s = n % 2
                DMA("sp", xt[s][0:nr, :], x_in[xi, r0:r0 + nr, :], [], [f"B_xt{s}"])
                ACT(junk[0:nr, :], xt[s][0:nr, :], AF.Square, [f"B_xt{s}"], ["B_junk", f"B_ssq{s}"],
                    accum_out=ssq[s][0:nr, :])
                ACT(lnv[s][0:nr, :], ssq[s][0:nr, :], AF.Ln, [f"B_ssq{s}"], [f"B_ln{s}"], bias=eps_c[0:nr, :],
                    scale=1.0 / D)
                ACT(rs[s][0:nr, :], lnv[s][0:nr, :], AF.Exp, [f"B_ln{s}"], [f"B_rs{s}"], scale=-0.5)
                E("dve", "scalar_tensor_tensor", [f"B_xt{s}", f"B_rs{s}", f"gmod{j}"], ["B_htmp"],
                  out=htmp[0:nr, :], in0=xt[s][0:nr, :], scalar=rs[s][0:nr, 0:1], in1=gmod[j][0:nr, :],
                  op0=ALU.mult, op1=ALU.mult)
                E("dve", "tensor_tensor", ["B_htmp", f"shiftb{j}"], [f"B_hb{s}"], out=hb[s][0:nr, :],
                  in0=htmp[0:nr, :], in1=shiftb[j][0:nr, :], op=ALU.add)
                b = bank()
                pv = PS[b][:].bitcast(BF16).rearrange("p (k t) -> p k t", k=8)
                for k in range(8):
                    TR(pv[:, k, 0:nr], hb[s][0:nr, k * 128:(k + 1) * 128], cbf(IDENT)[0:nr, 0:nr],
                       [f"B_hb{s}", "cstbf"], [psk(b)])
                if pend is not None:
                    COPY(pend[0], pend[1], [pend[2]], ["@hT"], eng="dve")
                pend = (hT[:, :, c0:c0 + nr], pv[:, :, 0:nr], psk(b))
            COPY(pend[0], pend[1], [pend[2]], ["@hT"], eng="dve")

    def load_w_chunk(st_w32, st_wbf, key32, keybf, src_ap, ncols):
        DMA("sp", st_w32[:, :, 0:ncols], src_ap.rearrange("(kc p) n -> p kc n", p=128), [], [key32])
        E("pool", "tensor_copy", [key32], [keybf], out=st_wbf[:, :, 0:ncols], in_=st_w32[:, :, 0:ncols])

    def stage_C_fm(j, hT, other):
        sc = S[j]
        tok0 = T if other else 0
        with contextlib.ExitStack() as st:
            def sbt(name, shape, dt=F32):
                return st.enter_context(nc.sbuf_tensor(f"{name}_u{next(uid)}", list(shape), dt))
            w32 = [sbt(f"C_w32{i}", [128, 8, 128]) for i in range(2)]
            wbf = [sbt(f"C_wbf{i}", [128, 8, 128], BF16) for i in range(2)]
            gst = [sbt(f"C_gst{i}", [128, 512], BF16) for i in range(3)]
            pbufs = [sbt(f"C_pbuf{i}", [128, T + 4]) for i in range(2)]
            caccs = [sbt(f"C_cacc{i}", [128, T]) for i in range(2)]
            xc = [sbt(f"C_xc{i}", [128, T], BF16) for i in range(2)]
            tmst = [sbt(f"C_tm{i}", [128, 8, 128], BF16) for i in range(2)]
            cw = sbt("C_cw", [128, 24 * 5])
            cb = sbt("C_cb", [128, 24])
            DMA("sp", cw[:], conv_wc[j], [], ["C_cw"])
            DMA("sp", cb[:], conv_bc[:, :], [], ["C_cb"])
            chunks = []
            if not other:
                chunks += [("sig", i, C_GGA + 128 * i) for i in range(16)]
                chunks += [("sga", i, C_GA + 128 * i) for i in range(8)]
            nx = 20 if other else 24
            chunks += [("xbc", i, C_X + 128 * i) for i in range(nx)]
            load_w_chunk(w32[0], wbf[0], "C_w320", "C_wbf0", w_in[:, chunks[0][2]:chunks[0][2] + 128], 128)
            cnts = {"g": 0, "t": 0}

            def X(n):
                kind, ci, col = chunks[n]
                s = n % 2
                pbuf = pbufs[s]
                pk = f"@C_pbuf{s}"
                cacc = caccs[s]
                ck = f"C_cacc{s}"
                if n + 1 < len(chunks):
                    c2 = chunks[n + 1][2]
                    load_w_chunk(w32[1 - s], wbf[1 - s], f"C_w32{1 - s}", f"C_wbf{1 - s}", w_in[:, c2:c2 + 128], 128)
                for i in range(8):
                    b = bank()
                    for k in range(8):
                        MM(PS[b][:], wbf[s][:, k, :], hT[:, k, 2 + 512 * i:2 + 512 * (i + 1)], k == 0, k == 7,
                           [f"C_wbf{s}", "@hT"], [psk(b)])
                    if kind in ("sig", "sga"):
                        g = cnts["g"] % 3
                        cnts["g"] += 1
                        ACT(gst[g][:], PS[b][:], AF.Sigmoid if kind == "sig" else AF.Silu, [psk(b)], [f"C_gst{g}"])
                        dst = sc[kind][ci, :, 512 * i:512 * (i + 1)]
                        DMA("pool", dst, gst[g][:], [f"C_gst{g}"], [f"@{kind}{j}"])
                    else:
                        COPY(pbuf[:, 2 + 512 * i:2 + 512 * (i + 1)], PS[b][:], [psk(b)], [pk], eng="act")
                if kind != "xbc":
                    return
                b = bank()
                for hh, c0 in enumerate((0, T + 2)):
                    for k in range(8):
                        MM(PS[b][:, 2 * hh:2 * hh + 2], wbf[s][:, k, :], hT[:, k, c0:c0 + 2], k == 0, k == 7,
                           [f"C_wbf{s}", "@hT"], [psk(b)])
                E("dve", "tensor_copy", [psk(b)], [pk], out=pbuf[:, 0:2], in_=PS[b][:, 0:2])
                E("dve", "tensor_copy", [psk(b)], [pk], out=pbuf[:, T + 2:T + 4], in_=PS[b][:, 2:4])
                ACT(cacc[:], pbuf[:, 0:T], AF.Identity, [pk, "C_cw", "C_cb"], [ck],
                    scale=cw[:, ci * 5:ci * 5 + 1], bias=cb[:, ci:ci + 1])
                for w in range(1, 5):
                    E("dve", "scalar_tensor_tensor", [pk, "C_cw", ck], [ck], out=cacc[:],
                      in0=pbuf[:, w:T + w], scalar=cw[:, ci * 5 + w:ci * 5 + w + 1], in1=cacc[:],
                      op0=ALU.mult, op1=ALU.add)

            def Y(n):
                kind, ci, col = chunks[n]
                if kind != "xbc":
                    return
                s = n % 2
                cacc = caccs[s]
                ck = f"C_cacc{s}"
                xs_ = n % 2
                ACT(xc[xs_][:], cacc[:], AF.Silu, [ck], [f"C_xc{xs_}"])
                if ci >= 16 and not other:
                    g = (ci - 16) % 4
                    dst = (sc["BT"] if ci < 20 else sc["CT"])[g]
                    DMA("pool", dst, xc[xs_][:], [f"C_xc{xs_}"], [f"@{'BT' if ci < 20 else 'CT'}{j}"])
                if ci < 20:
                    if ci < 16:
                        dv = sc["xs_tm"].rearrange("(t p) c -> p t c", p=128)
                        dkey, ccol = f"@xs_tm{j}", ci * 128
                    else:
                        dv = sc["B_tm"].rearrange("(t p) c -> p t c", p=128)
                        dkey, ccol = f"@B_tm{j}", (ci - 16) * 128
                    for tg in range(4):
                        b = bank()
                        pv = PS[b][:].bitcast(BF16).rearrange("p (k t) -> p k t", k=8)
                        for t8 in range(8):
                            tt = tg * 8 + t8
                            TR(pv[:, t8, :], xc[xs_][:, tt * 128:(tt + 1) * 128], cbf(IDENT),
                               [f"C_xc{xs_}", "cstbf"], [psk(b)])
                        ts_ = cnts["t"] % 2
                        cnts["t"] += 1
                        COPY(tmst[ts_][:], pv, [psk(b)], [f"C_tm{ts_}"], eng="act")
                        t0 = tok0 // 128 + tg * 8
                        DMA("pool", dv[:, t0:t0 + 8, ccol:ccol + 128], tmst[ts_][:], [f"C_tm{ts_}"], [dkey])

            X(0)
            for n in range(len(chunks)):
                if n + 1 < len(chunks):
                    X(n + 1)
                Y(n)

    def stage_C_z(j, hT):
        sc = S[j]
        with contextlib.ExitStack() as st:
            def sbt(name, shape, dt=F32):
                return st.enter_context(nc.sbuf_tensor(f"{name}_u{next(uid)}", list(shape), dt))
            w32 = [sbt(f"Z_w32{i}", [128, 8, 512]) for i in range(2)]
            wbf = [sbt(f"Z_wbf{i}", [128, 8, 512], BF16) for i in range(2)]
            zst = [sbt(f"Z_st{i}", [128, 512], BF16) for i in range(3)]
            load_w_chunk(w32[0], wbf[0], "Z_w320", "Z_wbf0", w_in[:, C_Z:C_Z + 512], 512)
            for nb in range(4):
                s = nb % 2
                if nb + 1 < 4:
                    load_w_chunk(w32[1 - s], wbf[1 - s], f"Z_w32{1 - s}", f"Z_wbf{1 - s}",
                                 w_in[:, C_Z + 512 * (nb + 1):C_Z + 512 * (nb + 2)], 512)
                for t in range(NCH):
                    b = bank()
                    for k in range(8):
                        MM(PS[b][:], hT[:, k, 2 + 128 * t:2 + 128 * (t + 1)], wbf[s][:, k, :], k == 0, k == 7,
                           [f"Z_wbf{s}", "@hT"], [psk(b)])
                    g = (nb * NCH + t) % 3
                    ACT(zst[g][:], PS[b][:], AF.Silu, [psk(b)], [f"Z_st{g}"])
                    DMA("pool", sc["z_tm"][128 * t:128 * (t + 1), 512 * nb:512 * (nb + 1)], zst[g][:], [f"Z_st{g}"],
                        [f"@z_tm{j}"])

    def stage_C_dt(j, hT, other):
        sc = S[j]
        tok0 = T if other else 0
        with contextlib.ExitStack() as st:
            def sbt(name, shape, dt=F32):
                return st.enter_context(nc.sbuf_tensor(f"{name}_u{next(uid)}", list(shape), dt))
            w32 = sbt("T_w32", [128, 8, 64])
            wbf = sbt("T_wbf", [128, 8, 64], BF16)
            bias = sbt("T_bias", [128, 64])
            alg = sbt("T_alg", [128, 64])
            v = [sbt(f"T_v{i}", [128, 8, 64]) for i in range(2)]
            av = [sbt(f"T_av{i}", [128, 8, 64]) for i in range(2)]
            dtt = [sbt(f"T_dt{i}", [128, 8, 64]) for i in range(2)]
            aa = [sbt(f"T_a{i}", [128, 8, 64]) for i in range(2)]
            load_w_chunk(w32, wbf, "T_w32", "T_wbf", w_dt[j], 64)
            DMA("sp", bias[:], dtb[j, 0:1, :].partition_broadcast(128), [], ["T_bias"])
            DMA("sp", alg[:], alog[j, 0:1, :].partition_broadcast(128), [], ["T_alg"])
            ACT(alg[:], alg[:], AF.Exp, ["T_alg"], ["T_alg"])
            for tb in range(4):
                s = tb % 2
                b = bank()
                pv = PS[b][:].rearrange("p (t c) -> p t c", t=8)
                for t8 in range(8):
                    t = tb * 8 + t8
                    for k in range(8):
                        MM(pv[:, t8, :], hT[:, k, 2 + 128 * t:2 + 128 * (t + 1)], wbf[:, k, :], k == 0, k == 7,
                           ["T_wbf", "@hT"], [psk(b)])
                E("dve", "tensor_tensor", [psk(b), "T_bias"], [f"T_v{s}"], out=v[s][:], in0=pv,
                  in1=bias[:].unsqueeze(1).to_broadcast([128, 8, 64]), op=ALU.add)
                E("dve", "scalar_tensor_tensor", [f"T_v{s}"], [f"T_av{s}"], out=av[s][:], in0=v[s][:], scalar=-1.0,
                  in1=v[s][:], op0=ALU.mult, op1=ALU.max)
                ACT(av[s][:], av[s][:], AF.Exp, [f"T_av{s}"], [f"T_av{s}"], scale=-1.0)
                ACT(av[s][:], av[s][:], AF.Ln, [f"T_av{s}"], [f"T_av{s}"], bias=1.0, scale=1.0)
                E("dve", "scalar_tensor_tensor", [f"T_v{s}", f"T_av{s}"], [f"T_dt{s}"], out=dtt[s][:], in0=v[s][:],
                  scalar=0.0, in1=av[s][:], op0=ALU.max, op1=ALU.add)
                E("dve", "scalar_tensor_tensor", [f"T_dt{s}", "T_alg"], [f"T_a{s}"], out=aa[s][:], in0=dtt[s][:],
                  scalar=-1.0, in1=alg[:].unsqueeze(1).to_broadcast([128, 8, 64]), op0=ALU.mult, op1=ALU.mult)
                r0 = tok0 + tb * 1024
                DMA("pool", sc["dt"][r0:r0 + 1024, :].rearrange("(t p) c -> p t c", p=128), dtt[s][:], [f"T_dt{s}"],
                    [f"@dt{j}"])
                DMA("pool", sc["a"][r0:r0 + 1024, :].rearrange("(t p) c -> p t c", p=128), aa[s][:], [f"T_a{s}"],
                    [f"@a{j}"])

    def stage_C_lat(j, hT, other, ssqr):
        sc = S[j]
        tok0 = T if other else 0
        with contextlib.ExitStack() as st:
            def sbt(name, shape, dt=F32):
                return st.enter_context(nc.sbuf_tensor(f"{name}_u{next(uid)}", list(shape), dt))
            w32 = sbt("L_w32", [128, 8, 768])
            wbf = sbt("L_wbf", [128, 8, 768], BF16)
            ga = sbt("L_ga", [128, 5])
            qkg = sbt("L_qkg", [128, 6])
            sq = [sbt(f"L_sq{i}", [128, 512], BF16) for i in range(3)]
            lnb = sbt("L_ln", [128, 512])
            rsb = sbt("L_rs", [128, 512])
            ost = [sbt(f"L_ost{i}", [128, 3, 512], BF16) for i in range(2)]
            ct = [sbt(f"L_ct{i}", [64, 512]) for i in range(2)]
            stt_ = [sbt(f"L_stt{i}", [64, 512]) for i in range(2)]
            t1 = sbt("L_t1", [64, 512])
            t2 = sbt("L_t2", [64, 512])
            krst = [sbt(f"L_krst{i}", [64, 512], BF16) for i in range(2)]
            DMA("sp", w32[:, :, 0:704], w_in[:, 0:704].rearrange("(kc p) n -> p kc n", p=128), [], ["L_w32"])
            DMA("sp", w32[:, :, 704:768], w_krs[:, :].rearrange("(kc p) n -> p kc n", p=128), [], ["L_w32"])
            E("pool", "tensor_copy", ["L_w32"], ["L_wbf"], out=wbf[:], in_=w32[:])
            DMA("sp", ga[:, 0:3], gq_a[:, :], [], ["L_ga"])
            DMA("sp", ga[:, 3:5], gkv_a[:, :], [], ["L_ga"])
            DMA("sp", qkg[:], qk_g[:, :], [], ["L_qkg"])
            groups = ([] if other else [("qnT", 0, 3, 384.0, 0)]) + [("kvnT", 3, 2, 256.0, 3)]
            for i in range(8):
                cols = slice(2 + 512 * i, 2 + 512 * (i + 1))
                s = i % 2
                for (name, c0, ncn, nf, g0) in groups:
                    bs = []
                    for c in range(ncn):
                        b = bank()
                        bs.append(b)
                        for k in range(8):
                            MM(PS[b][:], wbf[:, k, (c0 + c) * 128:(c0 + c + 1) * 128], hT[:, k, cols], k == 0, k == 7,
                               ["L_wbf", "@hT"], [psk(b)])
                        ACT(sq[c][:], PS[b][:], AF.Square, [psk(b)], [f"L_sq{c}"])
                    bq = bank()
                    for c in range(ncn):
                        MM(PS[bq][:], onesbf[:], sq[c][:], c == 0, c == ncn - 1, ["onesbf", f"L_sq{c}"], [psk(bq)])
                    rstd_act(rsb[:], PS[bq][:], 1.0 / nf, [psk(bq)], ["L_rs"], lnb[:], "L_ln")
                    for c in range(ncn):
                        E("dve", "scalar_tensor_tensor", [psk(bs[c]), "L_ga", "L_rs"], [f"L_ost{s}"],
                          out=ost[s][:, c, :], in0=PS[bs[c]][:], scalar=ga[:, g0 + c:g0 + c + 1], in1=rsb[:],
                          op0=ALU.mult, op1=ALU.mult)
                    tcols = slice(tok0 + 512 * i, tok0 + 512 * (i + 1))
                    DMA("pool", sc[name][:, :, tcols].rearrange("c p t -> p c t"), ost[s][:, 0:ncn, :],
                        [f"L_ost{s}"], [f"@{name}{j}"])
                bk, bks = bank(), bank()
                for k in range(8):
                    MM(PS[bk][0:64, :], wbf[:, k, 640:704], hT[:, k, cols], k == 0, k == 7, ["L_wbf", "@hT"], [psk(bk)])
                for k in range(8):
                    MM(PS[bks][0:64, :], wbf[:, k, 704:768], hT[:, k, cols], k == 0, k == 7, ["L_wbf", "@hT"],
                       [psk(bks)])
                ACT(sq[0][0:64, :], PS[bk][0:64, :], AF.Square, [psk(bk)], ["L_sq0"])
                bq = bank()
                for t4 in range(4):
                    MM(PS[bq][:, t4:t4 + 1], sq[0][0:64, t4 * 128:(t4 + 1) * 128], onesbf[0:64, 0:1], True, True,
                       ["L_sq0", "onesbf"], [psk(bq)])
                kt0 = (tok0 + 512 * i) // 128
                E("dve", "tensor_copy", [psk(bq)], ["ssqr"], out=ssqr[:, kt0:kt0 + 4], in_=PS[bq][:, 0:4])
                pos = slice(tok0 + 512 * i, tok0 + 512 * (i + 1))
                DMA("sp", ct[s][:], ropeC[j, :, pos], [], [f"L_ct{s}"])
                DMA("sp", stt_[s][:], ropeS[j, :, pos], [], [f"L_stt{s}"])
                E("dve", "scalar_tensor_tensor", [psk(bk), "L_qkg", f"L_ct{s}"], ["L_t1"], out=t1[:],
                  in0=PS[bk][0:64, :], scalar=qkg[0:64, 4:5], in1=ct[s][:], op0=ALU.mult, op1=ALU.mult)
                E("dve", "scalar_tensor_tensor", [psk(bks), "L_qkg", f"L_stt{s}"], ["L_t2"], out=t2[:],
                  in0=PS[bks][0:64, :], scalar=qkg[0:64, 5:6], in1=stt_[s][:], op0=ALU.mult, op1=ALU.mult)
                E("dve", "tensor_tensor", ["L_t1", "L_t2"], [f"L_krst{s}"], out=krst[s][:], in0=t1[:], in1=t2[:],
                  op=ALU.add)
                DMA("pool", sc["KRg"][:, pos], krst[s][:], [f"L_krst{s}"], [f"@KRg{j}"])

    def stage_D(j, ssqr):
        from collections import deque
        sc = S[j]
        tk = T * (1 + j)
        nkt = tk // 128
        LOOK = 2
        SB = (0, 6)
        with contextlib.ExitStack() as st:
            def sbt(name, shape, dt=F32):
                return st.enter_context(nc.sbuf_tensor(f"{name}_u{next(uid)}", list(shape), dt))
            qn = sbt("D_qn", [128, 3, T], BF16)
            kvn = sbt("D_kvn", [128, 2, tk], BF16)
            krg = sbt("D_krg", [128, tk], BF16)
            kn = [sbt(f"D_kn{i}", [128, tk], BF16) for i in range(2)]
            vv = [sbt(f"D_v{i}", [128, nkt, 128], BF16) for i in range(2)]
            qkg = sbt("D_qkg", [128, 6])
            w32 = sbt("D_w32", [128, 3 * 256 + 2 * 256])
            wq = [sbt(f"D_wq{i}", [128, 3, 256], BF16) for i in range(2)]
            wk = [sbt(f"D_wk{i}", [128, 2, 256], BF16) for i in range(2)]
            sqk = sbt("D_sqk0", [128, 512], BF16)
            ssqk = sbt("D_ssqk", [128, 64])
            rks = [sbt(f"D_rks{i}", [128, 64]) for i in range(2)]
            qrn = sbt("D_qrn", [128, 512])
            qrr = sbt("D_qrr", [64, 512])
            qrs = sbt("D_qrs", [64, 512])
            sqn = sbt("D_sqn", [128, 512], BF16)
            sqr = sbt("D_sqr", [64, 512], BF16)
            lnb = sbt("D_ln", [128, 512])
            rsb = sbt("D_rs", [128, 512])
            QN = [sbt(f"D_QN{i}", [128, 512], BF16) for i in range(2)]
            QR = [sbt(f"D_QR{i}", [128, 512], BF16) for i in range(2)]
            ct = sbt("D_ct0", [64, 512])
            stt_ = sbt("D_stt0", [64, 512])
            NPT = 4
            PT = [sbt(f"D_PT{i}", [128, 512], BF16) for i in range(NPT)]
            pacc = [sbt(f"D_pacc{i}", [128, 512]) for i in range(2)]
            lnr = sbt("D_lnr", [128, 512])
            rinv = sbt("D_rinv", [128, 512])
            aost = [sbt(f"D_ao{i}", [128, 512], BF16) for i in range(2)]
            for c in range(3):
                DMA("sp", qn[:, c, :], sc["qnT"][c], [f"@qnT{j}"], ["D_qn"])
            for c in range(2):
                DMA("sp", kvn[:, c, :], sc["kvnT"][c], [f"@kvnT{j}"], ["D_kvn"])
            E("dve", "memset", [], ["D_krg"], krg[64:128, :], 0.0)
            DMA("sp", krg[0:64, :], sc["KRg"][:, :], [f"@KRg{j}"], ["D_krg"])
            DMA("sp", qkg[:], qk_g[:, :], [], ["D_qkg"])
            for i in range(2):
                E("dve", "memset", [], [f"D_QR{i}"], QR[i][64:128, :], 0.0)
            ptc = {"i": 0}
            hi, lo = deque(), deque()

            sqk2 = [sqk, sbt("D_sqk1", [128, 512], BF16)]

            def kv_tasks(h):
                s = h % 2
                tl = []

                def t_w():
                    wv32 = w32[:, 0:768].rearrange("p (c n) -> p c n", c=3)
                    DMA("sp", wv32[:, :, 0:128], w_qn[h].rearrange("(c p) n -> p c n", p=128), [], ["D_w32"])
                    DMA("sp", wv32[:, :, 128:192], w_qr[h].rearrange("(c p) n -> p c n", p=128), [], ["D_w32"])
                    DMA("sp", wv32[:, :, 192:256], w_qs[h].rearrange("(c p) n -> p c n", p=128), [], ["D_w32"])
                    wk32 = w32[:, 768:1280].rearrange("p (c n) -> p c n", c=2)
                    DMA("sp", wk32[:, :, 0:128], w_kn[h].rearrange("(c p) n -> p c n", p=128), [], ["D_w32"])
                    DMA("sp", wk32[:, :, 128:256], w_v[h].rearrange("(c p) n -> p c n", p=128), [], ["D_w32"])
                    E("pool", "tensor_copy", ["D_w32"], [f"D_wq{s}"], out=wq[s][:], in_=wv32)
                    E("pool", "tensor_copy", ["D_w32"], [f"D_wk{s}"], out=wk[s][:], in_=wk32)
                tl += [t_w] + [None] * 8
                ng = tk // 512

                def mk_k(i):
                    def t_k():
                        b = bank(*SB)
                        cols = slice(512 * i, 512 * (i + 1))
                        for c in range(2):
                            MM(PS[b][:], wk[s][:, c, 0:128], kvn[:, c, cols], c == 0, c == 1, [f"D_wk{s}", "D_kvn"],
                               [psk(b)])

                        def fin():
                            ACT(kn[s][:, cols], PS[b][:], AF.Identity, [psk(b), "D_qkg"], [f"D_kn{s}"],
                                scale=qkg[:, 3:4])
                            ACT(sqk2[i % 2][:], PS[b][:], AF.Square, [psk(b)], [f"D_sqk{i % 2}"])
                        return fin
                    return t_k

                def mk_k2(i):
                    def t_k2():
                        b = bank(*SB)
                        for t4 in range(4):
                            MM(PS[b][:, t4:t4 + 1], sqk2[i % 2][:, t4 * 128:(t4 + 1) * 128], onesbf[:, 0:1], True, True,
                               [f"D_sqk{i % 2}", "onesbf"], [psk(b)])

                        def fin():
                            E("dve", "tensor_tensor", [psk(b), "ssqr"], ["D_ssqk"], out=ssqk[:, 4 * i:4 * i + 4],
                              in0=PS[b][:, 0:4], in1=ssqr[:, 4 * i:4 * i + 4], op=ALU.add)
                        return fin
                    return t_k2
                seq = []
                for i in range(ng):
                    seq.append(mk_k(i))
                    if i >= 1:
                        seq.append(mk_k2(i - 1))
                seq += [None, mk_k2(ng - 1), None]
                tl += seq

                def t_r1():
                    ACT(ssqk[:, 0:nkt], ssqk[:, 0:nkt], AF.Ln, ["D_ssqk"], ["D_ssqk"], bias=eps_c[:, 0:1],
                        scale=1.0 / 192)

                def t_r2():
                    ACT(ssqk[:, 0:nkt], ssqk[:, 0:nkt], AF.Exp, ["D_ssqk"], ["D_ssqk"], scale=-0.5)

                def t_r3():
                    E("dve", "tensor_scalar_mul", ["D_ssqk"], [f"D_rks{s}"], out=rks[s][:, 0:nkt], in0=ssqk[:, 0:nkt],
                      scalar1=SM_SCALE)
                tl += [t_r1, t_r2, t_r3]
                for i in range(ng):
                    def t_v(i=i):
                        b = bank(*SB)
                        pv = PS[b][:].rearrange("p (t d) -> p t d", t=4)
                        for t4 in range(4):
                            kt = 4 * i + t4
                            for c in range(2):
                                MM(pv[:, t4, :], kvn[:, c, kt * 128:(kt + 1) * 128], wk[s][:, c, 128:256], c == 0,
                                   c == 1, [f"D_wk{s}", "D_kvn"], [psk(b)])

                        def fin():
                            COPY(vv[s][:, 4 * i:4 * i + 4, :], pv, [psk(b)], [f"@D_v{s}"])
                        return fin
                    tl.append(t_v)
                return tl

            def q_tasks(h, qb, qs):
                s = h % 2
                qcols = slice(512 * qb, 512 * (qb + 1))

                def proj(dst, dkey, lo_, hi_, rows, sq, sqkey):
                    def t():
                        b = bank(*SB)
                        for c in range(3):
                            MM(PS[b][0:rows, :], wq[s][:, c, lo_:hi_], qn[:, c, qcols], c == 0, c == 2,
                               [f"D_wq{s}", "D_qn"], [psk(b)])

                        def fin():
                            ACT(dst[:], PS[b][0:rows, :], AF.Identity, [psk(b)], [dkey])
                            if sq is not None:
                                ACT(sq[:], PS[b][0:rows, :], AF.Square, [psk(b)], [sqkey])
                        return fin
                    return t

                def t_tab():
                    DMA("sp", ct[:], ropeC[j, :, qcols], [], ["D_ct0"])
                    DMA("sp", stt_[:], ropeS[j, :, qcols], [], ["D_stt0"])

                def t_ss():
                    b = bank(*SB)
                    MM(PS[b][:], onesbf[:], sqn[:], True, False, ["onesbf", "D_sqn"], [psk(b)])
                    MM(PS[b][:], onesbf[0:64, :], sqr[:], False, True, ["onesbf", "D_sqr"], [psk(b)])

                    def fin():
                        ACT(lnb[:], PS[b][:], AF.Ln, [psk(b)], ["D_ln"], bias=eps_c[:, 0:1], scale=1.0 / 192)
                    return fin

                def t_rs():
                    ACT(rsb[:], lnb[:], AF.Exp, ["D_ln"], ["D_rs"], scale=-0.5)

                def t_d1():
                    E("dve", "scalar_tensor_tensor", ["D_qrn", "D_qkg", "D_rs"], [f"D_QN{qs}"], out=QN[qs][:],
                      in0=qrn[:], scalar=qkg[:, 0:1], in1=rsb[:], op0=ALU.mult, op1=ALU.mult)

                def t_d2():
                    E("dve", "scalar_tensor_tensor", ["D_qrr", "D_qkg", "D_rs"], ["D_qrr"], out=qrr[:],
                      in0=qrr[:], scalar=qkg[0:64, 1:2], in1=rsb[0:64, :], op0=ALU.mult, op1=ALU.mult)

                def t_d3():
                    E("dve", "tensor_mul", ["D_qrr", "D_ct0"], ["D_qrr"], out=qrr[:], in0=qrr[:], in1=ct[:])

                def t_d4():
                    E("dve", "scalar_tensor_tensor", ["D_qrs", "D_qkg", "D_rs"], ["D_qrs"], out=qrs[:],
                      in0=qrs[:], scalar=qkg[0:64, 2:3], in1=rsb[0:64, :], op0=ALU.mult, op1=ALU.mult)

                def t_d5():
                    E("dve", "tensor_mul", ["D_qrs", "D_stt0"], ["D_qrs"], out=qrs[:], in0=qrs[:], in1=stt_[:])

                def t_d6():
                    E("dve", "tensor_tensor", ["D_qrr", "D_qrs"], [f"D_QR{qs}"], out=QR[qs][0:64, :], in0=qrr[:],
                      in1=qrs[:], op=ALU.add)
                return [t_tab, proj(qrn, "D_qrn", 0, 128, 128, sqn, "D_sqn"), proj(qrr, "D_qrr", 128, 192, 64, sqr, "D_sqr"),
                        proj(qrs, "D_qrs", 192, 256, 64, None, None), t_ss, t_rs, t_d1, t_d2, t_d3, t_d4, t_d5, t_d6]

            def epi_tasks(h, qb, qs):
                qcols = slice(512 * qb, 512 * (qb + 1))
                bo = 6 + qs
                pa, pak = pacc[qs], f"D_pacc{qs}"

                def t_e1():
                    b = bank(*SB)
                    MM(PS[b][:], ones32[:], pa[:], True, True, ["ones32", pak], [psk(b)])

                    def fin():
                        ACT(lnr[:], PS[b][:], AF.Ln, [psk(b)], ["D_lnr"])
                    return fin

                def t_e2():
                    ACT(rinv[:], lnr[:], AF.Exp, ["D_lnr"], ["D_rinv"], scale=-1.0)

                def t_e3():
                    E("dve", "tensor_tensor", [psk(bo), "D_rinv"], [f"D_ao{qs}"], out=aost[qs][:], in0=PS[bo][:],
                      in1=rinv[:], op=ALU.mult)
                    DMA("pool", sc["AO"][h, :, qcols], aost[qs][:], [f"D_ao{qs}"], [f"@AO{j}"])
                return [None, t_e1, t_e2, t_e3]

            def pop_bg():
                t = hi.popleft() if hi else (lo.popleft() if lo else None)
                return t() if t is not None else None

            def flash(h, qb, qs):
                s = h % 2
                bo = 6 + qs
                pa, pak = pacc[qs], f"D_pacc{qs}"

                def s_mm(jt):
                    b = bank(*SB)
                    kc = slice(128 * jt, 128 * (jt + 1))
                    MM(PS[b][:], kn[s][:, kc], QN[qs][:], True, False, [f"D_kn{s}", f"D_QN{qs}"], [psk(b)])
                    MM(PS[b][:], krg[:, kc], QR[qs][:], False, True, ["D_krg", f"D_QR{qs}"], [psk(b)])
                    return b

                def acc_pt(jt, p):
                    if jt == 0:
                        E("dve", "tensor_copy", [f"D_PT{p}"], [pak], out=pa[:], in_=PT[p][:])
                    else:
                        E("dve", "tensor_tensor", [f"D_PT{p}", pak], [pak], out=pa[:], in0=pa[:], in1=PT[p][:],
                          op=ALU.add)
                prev_p = None
                pend = [s_mm(t) for t in range(min(LOOK, nkt))]
                for jt in range(nkt):
                    fin = pop_bg()
                    if jt + LOOK < nkt:
                        pend.append(s_mm(jt + LOOK))
                    bcur = pend.pop(0)
                    p = ptc["i"] % NPT
                    ptc["i"] += 1
                    ACT(PT[p][:], PS[bcur][:], AF.Exp, [psk(bcur), f"D_rks{s}"], [f"D_PT{p}"],
                        scale=rks[s][:, jt:jt + 1])
                    MM(PS[bo][:], vv[s][:, jt, :], PT[p][:], jt == 0, jt == nkt - 1, [f"@D_v{s}", f"D_PT{p}"],
                       [psk(bo)], late=(f"D_PT{p}",))
                    if jt >= 1:
                        acc_pt(jt - 1, prev_p)
                    prev_p = p
                    if fin is not None:
                        fin()
                acc_pt(nkt - 1, prev_p)

            def flush(q):
                while q:
                    t = q.popleft()
                    if t is not None:
                        f = t()
                        if f is not None:
                            f()

            def run_all(tl):
                flush(deque(tl))

            blocks = [(h, qb) for h in range(8) for qb in range(8)]
            run_all(kv_tasks(0))
            run_all(q_tasks(0, 0, 0))
            for n, (h, qb) in enumerate(blocks):
                flush(hi)
                if qb == 0:
                    flush(lo)
                    if h + 1 < 8:
                        lo.extend(kv_tasks(h + 1))
                if n >= 1:
                    hp, qp = blocks[n - 1]
                    hi.extend(epi_tasks(hp, qp, (n - 1) % 2))
                if n + 1 < len(blocks):
                    h2, qb2 = blocks[n + 1]
                    if h2 != h:
                        hi.extend(lo)
                        lo.clear()
                    hi.extend(q_tasks(h2, qb2, (n + 1) % 2))
                flash(h, qb, n % 2)
            flush(hi)
            flush(lo)
            run_all(epi_tasks(*blocks[-1], (len(blocks) - 1) % 2))

    def stage_E(j):
        sc = S[j]
        with contextlib.ExitStack() as st:
            def sbt(name, shape, dt=F32):
                return st.enter_context(nc.sbuf_tensor(f"{name}_u{next(uid)}", list(shape), dt))
            xs = [sbt(f"E_xs{i}", [128, 2048], BF16) for i in range(3)]
            btm = [sbt(f"E_btm{i}", [128, 512], BF16) for i in range(3)]
            bt = [sbt(f"E_bt{i}", [128, 4, 128], BF16) for i in range(3)]
            ctt = [sbt(f"E_ct{i}", [128, 4, 128], BF16) for i in range(3)]
            dtl = [sbt(f"E_dt{i}", [128, 64]) for i in range(3)]
            al = [sbt(f"E_a{i}", [128, 64]) for i in range(3)]
            ybl = [sbt(f"E_yb{i}", [128, 2048]) for i in range(3)]
            zl = [sbt(f"E_z{i}", [128, 2048], BF16) for i in range(3)]
            acs = [sbt(f"E_acs{i}", [128, 32]) for i in range(2)]
            dsub = [sbt(f"E_dsub{i}", [128, 32]) for i in range(2)]
            ea = [sbt(f"E_ea{i}", [128, 32]) for i in range(2)]
            eds = [sbt(f"E_eds{i}", [128, 32]) for i in range(2)]
            etot = [sbt(f"E_etot{i}", [128, 32]) for i in range(2)]
            dtw = [sbt(f"E_dtw{i}", [128, 32]) for i in range(2)]
            abf = [sbt(f"E_abf{i}", [128, 32], BF16) for i in range(2)]
            xdt = [sbt(f"E_xdt{i}", [128, 2048], BF16) for i in range(2)]
            xdtw = [sbt(f"E_xdtw{i}", [128, 2048], BF16) for i in range(2)]
            R = [sbt(f"E_R{i}", [128, 32, 128], BF16) for i in range(1)]
            Em = [sbt(f"E_E{i}", [128, 32, 128], BF16) for i in range(1)]
            Mm = [sbt(f"E_M{i}", [128, 32, 128], BF16) for i in range(1)]
            cbm = [sbt(f"E_cbm{i}", [128, 4, 128], BF16) for i in range(1)]
            H = sbt("E_H", [128, 2048])
            Hbf = sbt("E_Hbf", [128, 2048], BF16)
            ytmp = [sbt(f"E_ytmp{i}", [128, 512]) for i in range(2)]
            Y = [sbt(f"E_Y{i}", [128, 2048]) for i in range(2)]
            gain = sbt("E_gain", [128, 2048])
            dsum = sbt("E_dsum", [128, 64])
            gss = sbt("E_gss", [128, 4])
            gln = sbt("E_gln", [128, 4])
            grs = sbt("E_grs", [128, 4])
            junk = sbt("E_junk", [128, 512])
            yn = sbt("E_yn", [128, 2048], BF16)
            ynst = [sbt(f"E_ynst{i}", [128, 16, 128], BF16) for i in range(2)]
            DMA("sp", gain[:], ssm_g[0:1, :].partition_broadcast(128), [], ["E_gain"])
            DMA("sp", dsum[:], dskip[j, 0:1, :].partition_broadcast(128), [], ["E_dsum"])
            E("dve", "tensor_tensor", ["E_dsum"], ["E_dsum"], out=dsum[:, 0:32], in0=dsum[:, 0:32], in1=dsum[:, 32:64],
              op=ALU.add)
            Dmat = sbt("E_Dmat", [128, 32, 128], BF16)
            E("dve", "tensor_tensor", ["E_dsum", "cstbf"], ["E_Dmat"], out=Dmat[:],
              in0=cbf(IDENT).unsqueeze(1).to_broadcast([128, 32, 128]),
              in1=dsum[:, 0:32].unsqueeze(2).to_broadcast([128, 32, 128]), op=ALU.mult)
            cnt = 0
            for sweep in ("b", "f"):
                di = 1 if sweep == "b" else 0
                tri32 = c32(GE if sweep == "b" else LE)
                tribf = cbf(GE if sweep == "b" else LE)
                Ubf = cbf(LT if sweep == "b" else GT)
                order = []
                if sweep == "b":
                    if j == 1:
                        order += [(T + 128 * c, True, c) for c in reversed(range(NCH))]
                    order += [(128 * c, False, c) for c in reversed(range(NCH))]
                else:
                    order += [(128 * c, False, c) for c in range(NCH)]
                E("dve", "memset", [], ["E_H"], H[:], 0.0)
                E("pool", "memset", [], ["E_Hbf"], Hbf[:], 0.0)
                def loads(k):
                    r0, state_only, c = order[k]
                    l = k % 3
                    rows = slice(r0, r0 + 128)
                    DMA("sp", xs[l][:], sc["xs_tm"][rows, :], [f"@xs_tm{j}"], [f"E_xs{l}"])
                    DMA("sp", btm[l][:], sc["B_tm"][rows, :], [f"@B_tm{j}"], [f"E_btm{l}"])
                    DMA("sp", dtl[l][:], sc["dt"][rows, :], [f"@dt{j}"], [f"E_dt{l}"])
                    DMA("sp", al[l][:], sc["a"][rows, :], [f"@a{j}"], [f"E_a{l}"])
                    if not state_only:
                        DMA("sp", bt[l][:], sc["BT"][:, :, rows].rearrange("g p t -> p g t"), [f"@BT{j}"], [f"E_bt{l}"])
                        DMA("sp", ctt[l][:], sc["CT"][:, :, rows].rearrange("g p t -> p g t"), [f"@CT{j}"],
                            [f"E_ct{l}"])
                        if sweep == "f":
                            DMA("sp", ybl[l][:], sc["yb"][rows, :], [f"@yb{j}"], [f"E_yb{l}"])
                            DMA("sp", zl[l][:], sc["z_tm"][rows, :], [f"@z_tm{j}"], [f"E_z{l}"])

                def p1a(k):
                    r0, state_only, c = order[k]
                    s = k % 2
                    l = k % 3
                    a32 = al[l][:, 32 * di:32 * di + 32]
                    dt32 = dtl[l][:, 32 * di:32 * di + 32]
                    xs3 = xs[l][:].rearrange("p (h d) -> p h d", h=32)
                    bsm = bank(4, 8)
                    MM(PS[bsm][:, 0:32], tri32, a32, True, True, ["cst32", f"E_a{l}"], [psk(bsm)])
                    MM(PS[bsm][:, 32:64], ones32[:], a32, True, True, ["ones32", f"E_a{l}"], [psk(bsm)])
                    if not state_only:
                        bcb = bank(4, 8)
                        pcb = PS[bcb][:].rearrange("p (g t) -> p g t", g=4)
                        for g in range(4):
                            MM(pcb[:, g, :], bt[l][:, g, :], ctt[l][:, g, :], True, True, [f"E_bt{l}", f"E_ct{l}"],
                               [psk(bcb)])
                        E("dve", "tensor_tensor", [f"E_xs{l}", f"E_dt{l}"], [f"E_xdt{s}"],
                          out=xdt[s][:].rearrange("p (h d) -> p h d", h=32), in0=xs3,
                          in1=dt32.unsqueeze(2).to_broadcast([128, 32, 64]), op=ALU.mult)
                        for h in range(8):
                            ACT(R[0][:, h, :], tribf, AF.Identity, ["cstbf", f"E_a{l}"], ["E_R0"], scale=a32[:, h:h + 1])
                    E("dve", "tensor_copy", [psk(bsm)], [f"E_acs{s}"], out=acs[s][:], in_=PS[bsm][:, 0:32])
                    E("dve", "tensor_tensor", [psk(bsm), f"E_acs{s}"], [f"E_dsub{s}"], out=dsub[s][:], in0=PS[bsm][:, 32:64],
                      in1=acs[s][:], op=ALU.subtract)
                    if not state_only:
                        E("dve", "tensor_tensor", [psk(bcb), "cstbf"], ["E_cbm0"], out=cbm[0][:], in0=pcb,
                          in1=tribf.unsqueeze(1).to_broadcast([128, 4, 128]), op=ALU.mult)
                    ACT(ea[s][:], acs[s][:], AF.Exp, [f"E_acs{s}"], [f"E_ea{s}"])
                    ACT(eds[s][:], dsub[s][:], AF.Exp, [f"E_dsub{s}"], [f"E_eds{s}"])
                    ACT(etot[s][:], PS[bsm][:, 32:64], AF.Exp, [psk(bsm)], [f"E_etot{s}"])
                    if not state_only:
                        for h in range(8, 32):
                            ACT(R[0][:, h, :], tribf, AF.Identity, ["cstbf", f"E_a{l}"], ["E_R0"], scale=a32[:, h:h + 1])
                    E("dve", "tensor_tensor", [f"E_dt{l}", f"E_eds{s}"], [f"E_dtw{s}"], out=dtw[s][:], in0=dt32, in1=eds[s][:],
                      op=ALU.mult)
                    E("dve", "tensor_tensor", [f"E_xs{l}", f"E_dtw{s}"], [f"E_xdtw{s}"],
                      out=xdtw[s][:].rearrange("p (h d) -> p h d", h=32), in0=xs3,
                      in1=dtw[s][:].unsqueeze(2).to_broadcast([128, 32, 64]), op=ALU.mult)

                def p1b(k):
                    r0, state_only, c = order[k]
                    if state_only:
                        return
                    s = k % 2
                    l = k % 3
                    def d_mm(q):
                        bd = bank(0, 4)
                        MM(PS[bd][:], Ubf, R[0][:, 4 * q:4 * q + 4, :].rearrange("p h t -> p (h t)"), True, True,
                           ["cstbf", "E_R0"], [psk(bd)])
                        return bd
                    pend_d = [d_mm(q) for q in range(3)]
                    for q in range(8):
                        if q + 3 < 8:
                            pend_d.append(d_mm(q + 3))
                        bd = pend_d.pop(0)
                        ACT(Em[0][:, 4 * q:4 * q + 4, :].rearrange("p h t -> p (h t)"), PS[bd][:], AF.Exp, [psk(bd)],
                            ["E_E0"])
                    for g in range(4):
                        E("dve", "tensor_tensor", ["E_E0", "E_cbm0"], ["E_M0"], out=Mm[0][:, 8 * g:8 * g + 8, :],
                          in0=Em[0][:, 8 * g:8 * g + 8, :],
                          in1=cbm[0][:, g, :].unsqueeze(1).to_broadcast([128, 8, 128]), op=ALU.mult)

                def p2(k):
                    r0, state_only, c = order[k]
                    s = k % 2
                    l = k % 3
                    ys = s
                    rows = slice(r0, r0 + 128)
                    xs3 = xs[l][:].rearrange("p (h d) -> p h d", h=32)
                    if not state_only:
                        for g in range(4):
                            byd, byo = bank(0, 4), bank(0, 4)
                            for hh in range(8):
                                h = 8 * g + hh
                                MM(PS[byd][:, 64 * hh:64 * hh + 64], Mm[0][:, h, :], xdt[s][:, 64 * h:64 * h + 64], True,
                                   sweep == "b", ["E_M0", f"E_xdt{s}"], [psk(byd)])
                                if sweep == "f":
                                    MM(PS[byd][:, 64 * hh:64 * hh + 64], Dmat[:, h, :], xs[l][:, 64 * h:64 * h + 64], False,
                                       True, ["E_Dmat", f"E_xs{l}"], [psk(byd)])
                            MM(PS[byo][:], ctt[l][:, g, :], Hbf[:, 512 * g:512 * g + 512], True, True,
                               [f"E_ct{l}", "E_Hbf"], [psk(byo)])
                            E("dve", "tensor_tensor", [psk(byo), f"E_ea{s}"], [f"E_ytmp{s}"],
                              out=ytmp[s][:].rearrange("p (h d) -> p h d", h=8),
                              in0=PS[byo][:].rearrange("p (h d) -> p h d", h=8),
                              in1=ea[s][:, 8 * g:8 * g + 8].unsqueeze(2).to_broadcast([128, 8, 64]), op=ALU.mult)
                            E("dve", "tensor_tensor", [psk(byd), f"E_ytmp{s}"], [f"E_Y{ys}"],
                              out=Y[ys][:, 512 * g:512 * g + 512], in0=PS[byd][:], in1=ytmp[s][:], op=ALU.add)
                    for g in range(4):
                        bst = bank(4, 8)
                        MM(PS[bst][:], btm[l][:, 128 * g:128 * g + 128], xdtw[s][:, 512 * g:512 * g + 512], True, True,
                           [f"E_btm{l}", f"E_xdtw{s}"], [psk(bst)])
                        hv = H[:, 512 * g:512 * g + 512]
                        E("dve", "tensor_tensor", ["E_H", f"E_etot{s}"], ["E_H"],
                          out=hv.rearrange("p (h d) -> p h d", h=8), in0=hv.rearrange("p (h d) -> p h d", h=8),
                          in1=etot[s][:, 8 * g:8 * g + 8].unsqueeze(2).to_broadcast([128, 8, 64]), op=ALU.mult)
                        E("dve", "tensor_tensor", ["E_H", psk(bst)], ["E_H"], out=hv, in0=hv, in1=PS[bst][:],
                          op=ALU.add)
                    ACT(Hbf[:], H[:], AF.Copy, ["E_H"], ["E_Hbf"])
                    if state_only:
                        return
                    if sweep == "b":
                        DMA("pool", sc["yb"][rows, :], Y[ys][:], [f"E_Y{ys}"], [f"@yb{j}"])
                        return
                    yv = Y[ys]
                    yk = f"E_Y{ys}"
                    E("dve", "tensor_tensor", [yk, f"E_yb{l}"], [yk], out=yv[:], in0=yv[:], in1=ybl[l][:], op=ALU.add)
                    E("dve", "tensor_tensor", [yk, f"E_z{l}"], [yk], out=yv[:], in0=yv[:], in1=zl[l][:], op=ALU.mult)

                def p2b(k):
                    r0, state_only, c = order[k]
                    if state_only or sweep == "b":
                        return
                    s = k % 2
                    ys = s
                    rows = slice(r0, r0 + 128)
                    yv = Y[ys]
                    yk = f"E_Y{ys}"
                    for g in range(4):
                        ACT(junk[:], yv[:, 512 * g:512 * g + 512], AF.Square, [yk], ["E_junk", "E_gss"],
                            accum_out=gss[:, g:g + 1])
                    ACT(gln[:], gss[:], AF.Ln, ["E_gss"], ["E_gln"], bias=eps_c[:, 0:1], scale=1.0 / 512)
                    ACT(grs[:], gln[:], AF.Exp, ["E_gln"], ["E_grs"], scale=-0.5)
                    for g in range(4):
                        E("dve", "scalar_tensor_tensor", [yk, "E_grs", "E_gain"], ["E_yn"],
                          out=yn[:, 512 * g:512 * g + 512], in0=yv[:, 512 * g:512 * g + 512], scalar=grs[:, g:g + 1],
                          in1=gain[:, 512 * g:512 * g + 512], op0=ALU.mult, op1=ALU.mult)
                    ns = c % 2
                    for half in range(2):
                        b = bank(4, 8)
                        pv = PS[b][:].bitcast(BF16).rearrange("p (k t) -> p k t", k=8)
                        for k8 in range(8):
                            cc = 8 * half + k8
                            TR(pv[:, k8, :], yn[:, 128 * cc:128 * cc + 128], cbf(IDENT), ["E_yn", "cstbf"], [psk(b)])
                        COPY(ynst[ns][:, 8 * half:8 * half + 8, :], pv, [psk(b)], [f"E_ynst{ns}"], eng="act")
                    DMA("pool", sc["ynT"][:, :, rows].rearrange("c p t -> p c t"), ynst[ns][:], [f"E_ynst{ns}"],
                        [f"@ynT{j}"])

                loads(0)
                if len(order) > 1:
                    loads(1)
                p1a(0)
                p1b(0)
                for k in range(len(order)):
                    if k + 1 < len(order):
                        p1a(k + 1)
                    if k + 2 < len(order):
                        loads(k + 2)
                    p2(k)
                    if k + 1 < len(order):
                        p1b(k + 1)
                    p2b(k)

    def stage_F(j):
        sc = S[j]
        with contextlib.ExitStack() as st:
            def sbt(name, shape, dt=F32):
                return st.enter_context(nc.sbuf_tensor(f"{name}_u{next(uid)}", list(shape), dt))
            w32s = [sbt(f"F_w32{i}", [128, 8, 512]) for i in range(2)]
            fcnt = {"i": 0}
            wpa = sbt("F_wpa", [128, 8, D], BF16)
            wpb = sbt("F_wpb", [128, 16, D], BF16)
            wo = sbt("F_wo", [128, 8, D], BF16)
            ao = sbt("F_ao", [128, 8, 512], BF16)
            sga = sbt("F_sga", [128, 8, 512], BF16)
            ynt = sbt("F_ynt", [128, 16, 512], BF16)
            sgg = sbt("F_sgg", [128, 16, 512], BF16)
            t1 = sbt("F_t1", [128, 512])
            t2 = sbt("F_t2", [128, 512])
            mg = sbt("F_mg", [128, 8, 512], BF16)
            xt = [sbt(f"F_xt{i}", [128, D]) for i in range(2)]
            yo = [sbt(f"F_yo{i}", [128, D]) for i in range(2)]
            for (src, dst, nk, key) in ((w_pa, wpa, 8, "F_wpa"), (w_pb, wpb, 16, "F_wpb"), (w_o, wo, 8, "F_wo")):
                sv = src.rearrange("(kc p) n -> p kc n", p=128)
                for k0 in range(0, nk, 8):
                    for nb in range(2):
                        wsl = fcnt["i"] % 2
                        fcnt["i"] += 1
                        DMA("sp", w32s[wsl][:], sv[:, k0:k0 + 8, 512 * nb:512 * nb + 512], [], [f"F_w32{wsl}"])
                        COPY(dst[:, k0:k0 + 8, 512 * nb:512 * nb + 512], w32s[wsl][:], [f"F_w32{wsl}"], ["@" + key])
            xcnt = 0
            for i in range(8):
                cols = slice(512 * i, 512 * i + 512)
                DMA("sp", ao[:], sc["AO"][:, :, cols].rearrange("h p t -> p h t"), [f"@AO{j}"], ["F_ao"])
                DMA("sp", sga[:], sc["sga"][:, :, cols].rearrange("h p t -> p h t"), [f"@sga{j}"], ["F_sga"])
                DMA("sp", ynt[:], sc["ynT"][:, :, cols].rearrange("h p t -> p h t"), [f"@ynT{j}"], ["F_ynt"])
                DMA("sp", sgg[:], sc["sig"][:, :, cols].rearrange("h p t -> p h t"), [f"@sig{j}"], ["F_sgg"])
                E("dve", "tensor_tensor", ["F_ao", "F_sga"], ["F_ao"], out=ao[:], in0=ao[:], in1=sga[:], op=ALU.mult)
                for dc in range(8):
                    ba, bb_ = bank(), bank()
                    for k in range(8):
                        MM(PS[ba][:], wpa[:, k, 128 * dc:128 * dc + 128], ao[:, k, :], k == 0, k == 7, ["@F_wpa", "F_ao"],
                           [psk(ba)])
                    for k in range(16):
                        MM(PS[bb_][:], wpb[:, k, 128 * dc:128 * dc + 128], ynt[:, k, :], k == 0, k == 15,
                           ["@F_wpb", "F_ynt"], [psk(bb_)])
                    E("dve", "tensor_tensor", [psk(ba), "F_sgg"], ["F_t1"], out=t1[:], in0=PS[ba][:], in1=sgg[:, dc, :],
                      op=ALU.mult)
                    E("dve", "tensor_tensor", [psk(bb_), "F_sgg"], ["F_t2"], out=t2[:], in0=PS[bb_][:],
                      in1=sgg[:, 8 + dc, :], op=ALU.mult)
                    E("dve", "tensor_tensor", ["F_t1", "F_t2"], ["F_mg"], out=mg[:, dc, :], in0=t1[:], in1=t2[:],
                      op=ALU.add)
                for tt in range(4):
                    xsl = xcnt % 2
                    xcnt += 1
                    r0 = 512 * i + 128 * tt
                    DMA("sp", xt[xsl][:], x_in[j, r0:r0 + 128, :], [], [f"F_xt{xsl}"])
                    for nb in range(2):
                        b = bank()
                        for k in range(8):
                            MM(PS[b][:], mg[:, k, 128 * tt:128 * tt + 128], wo[:, k, 512 * nb:512 * nb + 512], k == 0,
                               k == 7, ["F_mg", "@F_wo"], [psk(b)])
                        ysl = yo[xsl][:, 512 * nb:512 * nb + 512]
                        E("dve", "tensor_tensor", [psk(b), f"gateb{j}"], [f"F_yo{xsl}"], out=ysl, in0=PS[b][:],
                          in1=gateb[j][:, 512 * nb:512 * nb + 512], op=ALU.mult)
                        E("dve", "tensor_tensor", [f"F_yo{xsl}", f"F_xt{xsl}"], [f"F_yo{xsl}"], out=ysl, in0=ysl,
                          in1=xt[xsl][:, 512 * nb:512 * nb + 512], op=ALU.add)
                    DMA("pool", y_out[j, r0:r0 + 128, :], yo[xsl][:], [f"F_yo{xsl}"], ["@y_out"])

    with es:
        for j in range(2):
            with contextlib.ExitStack() as jst:
                ssqr = jst.enter_context(nc.sbuf_tensor(f"ssqr{j}", [128, 64], F32))
                gmod[j] = jst.enter_context(nc.sbuf_tensor(f"gmod{j}", [128, D], F32))
                shiftb[j] = jst.enter_context(nc.sbuf_tensor(f"shiftb{j}", [128, D], F32))
                gateb[j] = jst.enter_context(nc.sbuf_tensor(f"gateb{j}", [128, D], F32))
                if run("A"):
                    stage_A(j)
                passes = ([True] if j == 1 else []) + [False]
                for other in passes:
                    with contextlib.ExitStack() as pst:
                        hT = pst.enter_context(nc.sbuf_tensor(f"hT{j}{int(other)}", [128, 8, T + 4], BF16))
                        if run("B"):
                            xi = 2 if other else j
                            srcs = [(xi, 128 * t, 128, 2 + 128 * t) for t in range(NCH)]
                            if other:
                                srcs.append((1, T - 2, 2, 0))
                                E("dve", "memset", [], ["@hT"], hT[:, :, T + 2:T + 4], 0.0)
                            else:
                                E("dve", "memset", [], ["@hT"], hT[:, :, 0:2], 0.0)
                                if j == 1:
                                    srcs.append((2, 0, 2, T + 2))
                                else:
                                    E("dve", "memset", [], ["@hT"], hT[:, :, T + 2:T + 4], 0.0)
                            stage_B(j, hT, srcs)
                            P.barrier()
                        if run("C"):
                            stage_C_lat(j, hT, other, ssqr)
                            P.barrier()
                            stage_C_dt(j, hT, other)
                            P.barrier()
                            if not other:
                                stage_C_z(j, hT)
                                P.barrier()
                            stage_C_fm(j, hT, other)
                    P.barrier()
                if run("D"):
                    stage_D(j, ssqr)
                    P.barrier()
                if run("E"):
                    stage_E(j)
                    P.barrier()
                if run("F"):
                    stage_F(j)
                    P.barrier()
        P.barrier()
        P.op("sp", None, ["@y_out"], [])
        P.emit()
    return nc


def _rope_tables(pos):
    half = 32
    inv_freq = np.exp((-np.log(np.float32(10000.0)) * np.arange(half, dtype=np.float32) / np.float32(half)).astype(np.float32)).astype(np.float32)
    ang = (pos.astype(np.float32)[:, None] * inv_freq[None, :]).astype(np.float32)
    c, s = np.cos(ang).astype(np.float32), np.sin(ang).astype(np.float32)
    C = np.concatenate([c, c], axis=1).T
    Ssg = np.concatenate([-s, s], axis=1).T
    return np.ascontiguousarray(C), np.ascontiguousarray(Ssg)


def make_in_maps(x_prompt, x_sample, c_prompt, c_sample, norm_g, w_ada, b_ada, w_in, q_a_norm, w_q_up,
                 kv_a_norm, w_kv_up, q_norm, k_norm, w_proj_a, conv_w, conv_b, dt_bias_f, dt_bias_b,
                 a_log_f, a_log_b, d_f, d_b, ssm_norm, w_proj_b, w_out):
    f = lambda a: np.ascontiguousarray(np.asarray(a, dtype=np.float32))
    w_in0 = f(w_in[0])
    sw = np.concatenate([np.arange(32, 64), np.arange(0, 32)])
    wq = f(w_q_up[0]).reshape(384, 8, 192)
    wkv = f(w_kv_up[0]).reshape(256, 8, 256)
    qn_, kn_ = f(q_norm[0]), f(k_norm[0])
    qk_g = np.zeros((128, 6), np.float32)
    qk_g[:, 0] = qn_[:128]
    qk_g[:64, 1] = qn_[128:]
    qk_g[:64, 2] = qn_[128:][sw]
    qk_g[:, 3] = kn_[:128]
    qk_g[:64, 4] = kn_[128:]
    qk_g[:64, 5] = kn_[128:][sw]
    pidx = np.arange(128)[:, None]
    fidx = np.arange(128)[None, :]
    consts = np.concatenate([(pidx == fidx), (pidx <= fidx), (pidx >= fidx), (pidx > fidx), (pidx < fidx)],
                            axis=1).astype(np.float32)
    cw = f(conv_w[0])
    common = dict(
        w_ada=f(w_ada[0]), b_ada=f(b_ada), norm_g=f(norm_g), w_in=w_in0,
        w_krs=f(w_in0[:, C_KR:C_KR + 64][:, sw]),
        w_qn=f(wq[:, :, :128].transpose(1, 0, 2)), w_qr=f(wq[:, :, 128:].transpose(1, 0, 2)),
        w_qs=f(wq[:, :, 128:][:, :, sw].transpose(1, 0, 2)),
        w_kn=f(wkv[:, :, :128].transpose(1, 0, 2)), w_v=f(wkv[:, :, 128:].transpose(1, 0, 2)),
        gq_a=f(f(q_a_norm[0]).reshape(3, 128).T), gkv_a=f(f(kv_a_norm[0]).reshape(2, 128).T), qk_g=qk_g,
        conv_bc=f(f(conv_b[0]).reshape(24, 128).T), ssm_g=f(ssm_norm), w_pa=f(w_proj_a[0]), w_pb=f(w_proj_b[0]),
        w_o=f(w_out[0]), consts=consts,
    )
    wdt_f, wdt_b = w_in0[:, C_DTF:C_DTF + 32], w_in0[:, C_DTB:C_DTB + 32]
    fwd = dict(w_dt=np.concatenate([wdt_f, wdt_b], 1), dtb=np.concatenate([f(dt_bias_f[0]), f(dt_bias_b[0])]),
               alog=np.concatenate([f(a_log_f[0]), f(a_log_b[0])]), dskip=np.concatenate([f(d_f[0]), f(d_b[0])]),
               cw=cw)
    rev = dict(w_dt=np.concatenate([wdt_b, wdt_f], 1), dtb=np.concatenate([f(dt_bias_b[0]), f(dt_bias_f[0])]),
               alog=np.concatenate([f(a_log_b[0]), f(a_log_f[0])]), dskip=np.concatenate([f(d_b[0]), f(d_f[0])]),
               cw=cw[::-1])
    xp, xs_, cp, cs = f(x_prompt), f(x_sample), f(c_prompt), f(c_sample)
    Cp, Sp = _rope_tables(np.arange(8192))
    Cr, Sr = _rope_tables(np.arange(8191, -1, -1))
    in_maps = []
    for i in range(8):
        b, half = i // 2, i % 2
        if half == 0:
            own, oth, o1 = xs_[b, :T], xs_[b, T:], fwd
            C1, S1 = Cp, Sp
        else:
            own, oth, o1 = xs_[b, :T - 1:-1], xs_[b, T - 1::-1], rev
            C1, S1 = Cr, Sr
        jobs = (fwd, o1)
        m = dict(common)
        m["x"] = np.ascontiguousarray(np.stack([xp[i], own, oth]))
        m["cT"] = np.ascontiguousarray(np.stack([cp[i].reshape(8, 128).T, cs[b].reshape(8, 128).T]))
        m["w_dt"] = np.ascontiguousarray(np.stack([jb["w_dt"] for jb in jobs]))
        m["dtb"] = np.ascontiguousarray(np.stack([jb["dtb"][None] for jb in jobs]))
        m["alog"] = np.ascontiguousarray(np.stack([jb["alog"][None] for jb in jobs]))
        m["dskip"] = np.ascontiguousarray(np.stack([jb["dskip"][None] for jb in jobs]))
        m["conv_wc"] = np.ascontiguousarray(np.stack(
            [jb["cw"].reshape(5, 24, 128).transpose(2, 1, 0).reshape(128, 120) for jb in jobs]))
        m["ropeC"] = np.ascontiguousarray(np.stack([Cp, C1]))
        m["ropeS"] = np.ascontiguousarray(np.stack([Sp, S1]))
        in_maps.append(m)
    return in_maps


_NC_CACHE = {}


def kernel(**inputs):
    in_maps = make_in_maps(**inputs)
    if "nc" not in _NC_CACHE:
        _NC_CACHE["nc"] = build_program()
    res = run_bass_kernel_spmd(_NC_CACHE["nc"], in_maps, core_ids=list(range(8)))
    y_prompt = np.empty((8, T, D), np.float32)
    y_sample = np.empty((4, 2 * T, D), np.float32)
    for i in range(8):
        y = res.results[i]["y"]
        y_prompt[i] = y[0]
        b, half = i // 2, i % 2
        if half == 0:
            y_sample[b, :T] = y[1]
        else:
            y_sample[b, T:] = y[1][::-1]
    return (y_prompt, y_sample)
```
